# Optimizing a Trainium2 kernel written in Bass

```python
import math
import jax
import jax.numpy as jnp
from jax import lax
import numpy as np

D_MODEL = 1024
BATCH = 32
SEQ = 256
DEPTH = 4
DEC_BATCH = 4
DEC_SEQ = 2048
PAST_LEN = 256

GRID_W = 64
D_MIX = D_MODEL
W_GROUP = D_MIX // 4
H_RET = 4
DK_RET = W_GROUP // H_RET
W_HY = W_GROUP
HY_EMB = 33
HY_HIDDEN = 64
HY_DECAY_TARGET = 1e-2
HY_FAST_PCT = 0.3
HY_SLOW_PCT = 1.5
H_GDN = 4
DK_GDN = W_GROUP // H_GDN
DV_GDN = W_GROUP // H_GDN
W_GDN = H_GDN * DV_GDN
H_SSD = 4
P_SSD = W_GROUP // H_SSD
W_SSD = H_SSD * P_SSD
G_SSD = 2
N_SSD = 128
SHORT_CONV = 3
CHUNK = 64
D_FF = 4 * D_MODEL
ROPE_BASE = 10000.0
EPS = 1e-6
RET_COLS = 4 * W_GROUP
HY_COLS = 3 * W_HY
GDN_COLS = 4 * W_GDN + 4 * H_GDN
SSD_COLS = 2 * W_SSD + 2 * G_SSD * N_SSD + 2 * H_SSD
IN_COLS = RET_COLS + HY_COLS + GDN_COLS + SSD_COLS
SPLITS = (RET_COLS, RET_COLS + HY_COLS, RET_COLS + HY_COLS + GDN_COLS)

kernel_name = "hybrid_parallel_groups_diffusion_step"


def rms_norm(x, w):
    xf = x.astype(jnp.float32)
    y = xf * lax.rsqrt(jnp.mean(xf * xf, axis=-1, keepdims=True) + EPS)
    return (y * w.astype(jnp.float32)).astype(x.dtype)


def head_rms(o, w):
    h, d = o.shape[-2:]
    return rms_norm(o, w.reshape(h, d)).reshape(*o.shape[:-2], h * d)


def l2norm(x):
    xf = x.astype(jnp.float32)
    return (xf * lax.rsqrt(jnp.sum(xf * xf, axis=-1, keepdims=True) + EPS)).astype(x.dtype)


def _rev(t):
    return jnp.flip(t, axis=1)


def centred_conv(x, w, b=None):
    k_w = w.shape[0]
    pad = k_w // 2
    length = x.shape[1]
    xp = jnp.pad(x, ((0, 0), (pad, pad), (0, 0)))
    y = xp[:, 0:length] * w[0]
    for i in range(1, k_w):
        y = y + xp[:, i:i + length] * w[i]
    return y if b is None else y + b


def axial_rope(n_tok):
    rows = n_tok // GRID_W
    r = jnp.repeat(jnp.arange(rows), GRID_W).astype(jnp.float32)
    col = jnp.tile(jnp.arange(GRID_W), rows).astype(jnp.float32)
    quarter = DK_RET // 4
    inv = ROPE_BASE ** (-jnp.arange(quarter, dtype=jnp.float32) / quarter)
    ang = jnp.concatenate([r[:, None] * inv, col[:, None] * inv], axis=-1)
    return jnp.cos(ang), jnp.sin(ang)


def apply_rope(x, cos, sin):
    x1, x2 = jnp.split(x, 2, axis=-1)
    c = cos[None, :, None, :]
    s = sin[None, :, None, :]
    return jnp.concatenate([x1 * c - x2 * s, x1 * s + x2 * c], axis=-1).astype(x.dtype)


def _to_chunks(t):
    b, length, h = t.shape[:3]
    t = t.astype(jnp.float32).reshape(b, length // CHUNK, CHUNK, h, *t.shape[3:])
    return jnp.moveaxis(jnp.moveaxis(t, 1, 0), 3, 2)


def _from_chunks(o):
    n, b, h, c, v = o.shape
    return jnp.swapaxes(jnp.moveaxis(o, 0, 1), 2, 3).reshape(b, n * c, h, v)


def decay_scan(q, k, v, log_a, s0):
    incl = jnp.tril(jnp.ones((CHUNK, CHUNK), dtype=bool))

    def step(S, inp):
        qc, kc, vc, gc = inp
        cum = jnp.cumsum(gc, axis=-1)
        dec = jnp.exp(jnp.where(incl, cum[..., :, None] - cum[..., None, :], -jnp.inf))
        att = jnp.einsum('bhik,bhjk->bhij', qc, kc) * dec
        o = (jnp.einsum('bhij,bhjv->bhiv', att, vc)
             + jnp.einsum('bhck,bhkv->bhcv', qc * jnp.exp(cum)[..., None], S))
        last = cum[..., -1:]
        S = (jnp.exp(last)[..., None] * S
             + jnp.einsum('bhck,bhcv->bhkv', kc * jnp.exp(last - cum)[..., None], vc))
        return S, o

    S, o = lax.scan(step, s0.astype(jnp.float32),
                    (_to_chunks(q), _to_chunks(k), _to_chunks(v), _to_chunks(log_a)))
    return _from_chunks(o).astype(v.dtype), S.astype(v.dtype)


def gated_delta_scan(q, k, v, log_a, beta, s0):
    dk = q.shape[-1]
    incl = jnp.tril(jnp.ones((CHUNK, CHUNK), dtype=bool))
    strict = jnp.tril(jnp.ones((CHUNK, CHUNK), dtype=bool), -1)
    eye = jnp.eye(CHUNK, dtype=jnp.float32)

    def step(S, inp):
        qc, kc, vc, gc, bc = inp
        cum = jnp.cumsum(gc, axis=-1)
        dec = jnp.exp(jnp.where(incl, cum[..., :, None] - cum[..., None, :], -jnp.inf))
        a_mat = jnp.where(strict, dec * jnp.einsum('bhik,bhjk->bhij', kc, kc), 0.0) * bc[..., :, None]
        rhs = jnp.concatenate([kc * (bc * jnp.exp(cum))[..., None], vc * bc[..., None]], axis=-1)
        sol = lax.linalg.triangular_solve(a_mat + eye, rhs, left_side=True, lower=True,
                                          unit_diagonal=True)
        w_c, u_c = sol[..., :dk], sol[..., dk:]
        v_new = u_c - jnp.einsum('bhck,bhkv->bhcv', w_c, S)
        att = jnp.einsum('bhik,bhjk->bhij', qc, kc) * dec
        o = (jnp.einsum('bhck,bhkv->bhcv', qc * jnp.exp(cum)[..., None], S)
             + jnp.einsum('bhij,bhjv->bhiv', att, v_new))
        last = cum[..., -1:]
        S = (jnp.exp(last)[..., None] * S
             + jnp.einsum('bhck,bhcv->bhkv', kc * jnp.exp(last - cum)[..., None], v_new))
        return S, o

    S, o = lax.scan(step, s0.astype(jnp.float32),
                    (_to_chunks(q), _to_chunks(k), _to_chunks(v), _to_chunks(log_a), _to_chunks(beta)))
    return _from_chunks(o).astype(v.dtype), S.astype(v.dtype)


def retention_mixer(p, log_decay, norm_w, s0, rope):
    b, length, _ = p.shape
    q, k, v, g = jnp.split(p, 4, axis=-1)
    q = q.reshape(b, length, H_RET, DK_RET)
    k = k.reshape(b, length, H_RET, DK_RET) * (DK_RET ** -0.5)
    v = v.reshape(b, length, H_RET, DK_RET)
    if rope is not None:
        q = apply_rope(q, rope[0], rope[1])
        k = apply_rope(k, rope[0], rope[1])
    la = jnp.broadcast_to(log_decay, (b, length, 2, H_RET))
    o_f, s_f = decay_scan(q, k, v, la[:, :, 0], s0[:, 0])
    o_b, s_b = decay_scan(_rev(q), _rev(k), _rev(v), la[:, :, 1], s0[:, 1])
    o = head_rms(o_f + _rev(o_b), norm_w) * jax.nn.silu(g)
    return o, jnp.stack([s_f, s_b], axis=1)


def hyena_filter(length, freq, w1, b1, w2, b2, w3):
    t = jnp.linspace(0.0, 1.0, length, dtype=jnp.float32)[:, None]
    bands = (HY_EMB - 1) // 2
    w = 2.0 * math.pi * jnp.arange(length, dtype=jnp.float32)[:, None] / length
    f = jnp.linspace(1e-4, bands - 1, bands, dtype=jnp.float32)[None, :]
    z = jnp.concatenate([t, jnp.cos(f * w), -jnp.sin(f * w)], axis=-1)
    hdn = jnp.sin(freq * (z @ w1 + b1))
    hdn = jnp.sin(freq * (hdn @ w2 + b2))
    taps = (hdn @ w3).astype(jnp.float32).reshape(length, 2, W_HY)
    deltas = jnp.abs(jnp.linspace(math.log(HY_DECAY_TARGET) / HY_SLOW_PCT,
                                  math.log(HY_DECAY_TARGET) / HY_FAST_PCT, W_HY, dtype=jnp.float32))
    return taps * jnp.exp(-t * deltas)[:, None, :]


def bidir_long_conv(u, taps, bias):
    length = u.shape[1]
    circ_taps = jnp.concatenate([taps[:, 0], jnp.zeros_like(taps[:1, 0]), taps[:0:-1, 1]], axis=0)
    u_f = jnp.fft.rfft(u.astype(jnp.float32), n=2 * length, axis=1)
    t_f = jnp.fft.rfft(circ_taps, n=2 * length, axis=0)
    y = jnp.fft.irfft(u_f * t_f[None], n=2 * length, axis=1)[:, :length]
    return (y + u * bias).astype(u.dtype)


def hyena_mixer(p, conv_w, conv_b, taps, bias, norm_w):
    x0, x1, v = jnp.split(centred_conv(p, conv_w, conv_b), 3, axis=-1)
    y = x0 * bidir_long_conv(v * x1, taps, bias)
    return rms_norm(y, norm_w)


def gdn_mixer(p, conv_w, a_log, dt_bias, norm_w, s0):
    b, length, _ = p.shape
    qkv, z, a, bb = jnp.split(p, (3 * W_GDN, 4 * W_GDN, 4 * W_GDN + 2 * H_GDN), axis=-1)
    q, k, v = jnp.split(jax.nn.silu(centred_conv(qkv, conv_w)), 3, axis=-1)
    q = l2norm(q.reshape(b, length, H_GDN, DK_GDN)) * (DK_GDN ** -0.5)
    k = l2norm(k.reshape(b, length, H_GDN, DK_GDN))
    v = v.reshape(b, length, H_GDN, DV_GDN)
    log_a = -jnp.exp(a_log) * jax.nn.softplus(a.reshape(b, length, 2, H_GDN) + dt_bias)
    beta = jax.nn.sigmoid(bb.reshape(b, length, 2, H_GDN))
    o_f, s_f = gated_delta_scan(q, k, v, log_a[:, :, 0], beta[:, :, 0], s0[:, 0])
    o_b, s_b = gated_delta_scan(_rev(q), _rev(k), _rev(v), _rev(log_a[:, :, 1]),
                                _rev(beta[:, :, 1]), s0[:, 1])
    o = head_rms(o_f + _rev(o_b), norm_w) * jax.nn.silu(z)
    return o, jnp.stack([s_f, s_b], axis=1)


def ssd_mixer(p, conv_w, conv_b, a_log, dt_bias, d_skip, norm_w, s0):
    b, length, _ = p.shape
    gn = G_SSD * N_SSD
    xbc, z, dt = jnp.split(p, (W_SSD + 2 * gn, 2 * W_SSD + 2 * gn), axis=-1)
    xs, bm, cm = jnp.split(jax.nn.silu(centred_conv(xbc, conv_w, conv_b)), (W_SSD, W_SSD + gn), axis=-1)
    xs = xs.reshape(b, length, H_SSD, P_SSD)
    bm = jnp.repeat(bm.reshape(b, length, G_SSD, N_SSD), H_SSD // G_SSD, axis=2)
    cm = jnp.repeat(cm.reshape(b, length, G_SSD, N_SSD), H_SSD // G_SSD, axis=2)
    dt = jax.nn.softplus(dt.reshape(b, length, 2, H_SSD) + dt_bias)
    log_a = dt * -jnp.exp(a_log)
    xdt = xs[:, :, None] * dt[..., None]
    y_f, s_f = decay_scan(cm, bm, xdt[:, :, 0], log_a[:, :, 0], s0[:, 0])
    y_b, s_b = decay_scan(_rev(cm), _rev(bm), _rev(xdt[:, :, 1]), _rev(log_a[:, :, 1]), s0[:, 1])
    y = (y_f + _rev(y_b) + xs * d_skip[:, None]).reshape(b, length, W_SSD) * jax.nn.silu(z)
    y = rms_norm(y.reshape(b, length, G_SSD, W_SSD // G_SSD),
                 norm_w.reshape(G_SSD, W_SSD // G_SSD)).reshape(b, length, W_SSD)
    return y, jnp.stack([s_f, s_b], axis=1)


def layer(x, mod, s_ret, s_gdn, s_ssd, rope, lw):
    length = x.shape[1]
    sh1, sc1, g1, sh2, sc2, g2 = jnp.split(mod[:, None, :], 6, axis=-1)
    h = rms_norm(x, lw['norm1']) * (1 + sc1) + sh1
    p_ret, p_hy, p_gdn, p_ssd = jnp.split(h @ lw['w_in'], SPLITS, axis=-1)
    o_ret, f_ret = retention_mixer(p_ret, lw['ret_log_decay'], lw['ret_norm'], s_ret, rope)
    taps = hyena_filter(length, lw['hy_freq'], lw['hy_w1'], lw['hy_b1'], lw['hy_w2'], lw['hy_b2'], lw['hy_w3'])
    o_hy = hyena_mixer(p_hy, lw['hy_conv_w'], lw['hy_conv_b'], taps, lw['hy_bias'], lw['hy_norm'])
    o_gdn, f_gdn = gdn_mixer(p_gdn, lw['gdn_conv_w'], lw['gdn_A_log'], lw['gdn_dt_bias'], lw['gdn_norm'], s_gdn)
    o_ssd, f_ssd = ssd_mixer(p_ssd, lw['ssd_conv_w'], lw['ssd_conv_b'], lw['ssd_A_log'], lw['ssd_dt_bias'],
                             lw['ssd_D'], lw['ssd_norm'], s_ssd)
    mixed = jnp.concatenate([o_ret, o_hy, o_gdn, o_ssd], axis=-1)
    x = x + g1 * (mixed @ lw['w_out'])
    h2 = rms_norm(x, lw['norm2']) * (1 + sc2) + sh2
    x = x + g2 * (jnp.square(jax.nn.relu(h2 @ lw['mlp_w1'])) @ lw['mlp_w2'])
    return x, f_ret, f_gdn, f_ssd


def trunk(x, cond, st_ret, st_gdn, st_ssd, rope, layers, final_norm_w):
    finals = []
    for l in range(DEPTH):
        lw = layers[l]
        mod = jax.nn.silu(cond) @ lw['w_mod'] + lw['b_mod']
        x, f_ret, f_gdn, f_ssd = layer(x, mod, st_ret[:, l], st_gdn[:, l], st_ssd[:, l], rope, lw)
        finals.append((f_ret, f_gdn, f_ssd))
    return rms_norm(x, final_norm_w), finals


def setup_inputs(seed: int = 0) -> dict:
    key = jax.random.key(seed)
    keys = jax.random.split(key, 64)
    counter = [0]

    def nk():
        counter[0] += 1
        return keys[counter[0] - 1]

    def nrm(shape, scale=1.0):
        return scale * jax.random.normal(nk(), shape, jnp.float32)

    def gain(shape):
        return 1.0 + 0.02 * jax.random.normal(nk(), shape, jnp.float32)

    def loguni(shape, lo, hi):
        return jnp.exp(jax.random.uniform(nk(), shape, jnp.float32, math.log(lo), math.log(hi)))

    def inv_softplus_dt(shape):
        dt = loguni(shape, 1e-3, 1e-1)
        return dt + jnp.log(-jnp.expm1(-dt))

    ret_base = jnp.log(1.0 - 2.0 ** (-5.0 - jnp.arange(H_RET, dtype=jnp.float32)))
    return {
        'x_prompt': nrm((BATCH, SEQ, D_MODEL)),
        'x_sample': nrm((DEC_BATCH, DEC_SEQ, D_MODEL)),
        'state_ret': nrm((DEC_BATCH, DEPTH, 2, H_RET, DK_RET, DK_RET), 0.1),
        'state_gdn': nrm((DEC_BATCH, DEPTH, 2, H_GDN, DK_GDN, DV_GDN), 0.1),
        'state_ssd': nrm((DEC_BATCH, DEPTH, 2, H_SSD, N_SSD, P_SSD), 0.1),
        'c': nrm((DEC_BATCH, D_MODEL)),
        'c_ctx': nrm((D_MODEL,)),
        'norm1_w': gain((DEPTH, D_MODEL)),
        'norm2_w': gain((DEPTH, D_MODEL)),
        'w_mod': nrm((DEPTH, D_MODEL, 6 * D_MODEL), 0.5 * D_MODEL ** -0.5),
        'b_mod': nrm((DEPTH, 6 * D_MODEL), 0.01),
        'w_in': nrm((DEPTH, D_MODEL, IN_COLS), D_MODEL ** -0.5),
        'w_out': nrm((DEPTH, D_MIX, D_MODEL), D_MIX ** -0.5),
        'ret_log_decay': ret_base * (1.0 + 0.05 * jax.random.normal(nk(), (DEPTH, 2, H_RET), jnp.float32)),
        'ret_norm_w': gain((DEPTH, W_GROUP)),
        'hy_conv_w': nrm((DEPTH, SHORT_CONV, HY_COLS), SHORT_CONV ** -0.5),
        'hy_conv_b': nrm((DEPTH, HY_COLS), 0.01),
        'hy_freq': gain((DEPTH, HY_HIDDEN)),
        'hy_w1': nrm((DEPTH, HY_EMB, HY_HIDDEN), HY_EMB ** -0.5),
        'hy_b1': nrm((DEPTH, HY_HIDDEN), 0.1),
        'hy_w2': nrm((DEPTH, HY_HIDDEN, HY_HIDDEN), HY_HIDDEN ** -0.5),
        'hy_b2': nrm((DEPTH, HY_HIDDEN), 0.1),
        'hy_w3': nrm((DEPTH, HY_HIDDEN, 2 * W_HY), HY_HIDDEN ** -0.5),
        'hy_bias': nrm((DEPTH, W_HY)),
        'hy_norm_w': gain((DEPTH, W_HY)),
        'gdn_conv_w': nrm((DEPTH, SHORT_CONV, 3 * W_GDN), SHORT_CONV ** -0.5),
        'gdn_A_log': jnp.log(jax.random.uniform(nk(), (DEPTH, 2, H_GDN), jnp.float32, 1.0, 16.0)),
        'gdn_dt_bias': inv_softplus_dt((DEPTH, 2, H_GDN)),
        'gdn_norm_w': gain((DEPTH, W_GDN)),
        'ssd_conv_w': nrm((DEPTH, SHORT_CONV, W_SSD + 2 * G_SSD * N_SSD), SHORT_CONV ** -0.5),
        'ssd_conv_b': nrm((DEPTH, W_SSD + 2 * G_SSD * N_SSD), 0.01),
        'ssd_A_log': jnp.log(jax.random.uniform(nk(), (DEPTH, 2, H_SSD), jnp.float32, 1.0, 16.0)),
        'ssd_dt_bias': inv_softplus_dt((DEPTH, 2, H_SSD)),
        'ssd_D': gain((DEPTH, H_SSD)),
        'ssd_norm_w': gain((DEPTH, W_SSD)),
        'mlp_w1': nrm((DEPTH, D_MODEL, D_FF), D_MODEL ** -0.5),
        'mlp_w2': nrm((DEPTH, D_FF, D_MODEL), D_FF ** -0.5),
        'final_norm_w': gain((D_MODEL,)),
    }


def reference(x_prompt, x_sample, state_ret, state_gdn, state_ssd, c, c_ctx,
              norm1_w, norm2_w, w_mod, b_mod, w_in, w_out,
              ret_log_decay, ret_norm_w,
              hy_conv_w, hy_conv_b, hy_freq, hy_w1, hy_b1, hy_w2, hy_b2, hy_w3, hy_bias, hy_norm_w,
              gdn_conv_w, gdn_A_log, gdn_dt_bias, gdn_norm_w,
              ssd_conv_w, ssd_conv_b, ssd_A_log, ssd_dt_bias, ssd_D, ssd_norm_w,
              mlp_w1, mlp_w2, final_norm_w):
    layers = [dict(
        norm1=norm1_w[l], norm2=norm2_w[l], w_mod=w_mod[l], b_mod=b_mod[l],
        w_in=w_in[l], w_out=w_out[l],
        ret_log_decay=ret_log_decay[l], ret_norm=ret_norm_w[l],
        hy_conv_w=hy_conv_w[l], hy_conv_b=hy_conv_b[l], hy_freq=hy_freq[l],
        hy_w1=hy_w1[l], hy_b1=hy_b1[l], hy_w2=hy_w2[l], hy_b2=hy_b2[l], hy_w3=hy_w3[l],
        hy_bias=hy_bias[l], hy_norm=hy_norm_w[l],
        gdn_conv_w=gdn_conv_w[l], gdn_A_log=gdn_A_log[l], gdn_dt_bias=gdn_dt_bias[l], gdn_norm=gdn_norm_w[l],
        ssd_conv_w=ssd_conv_w[l], ssd_conv_b=ssd_conv_b[l], ssd_A_log=ssd_A_log[l],
        ssd_dt_bias=ssd_dt_bias[l], ssd_D=ssd_D[l], ssd_norm=ssd_norm_w[l],
        mlp_w1=mlp_w1[l], mlp_w2=mlp_w2[l]) for l in range(DEPTH)]

    bsz = x_prompt.shape[0]
    dt = x_prompt.dtype
    z_ret = jnp.zeros((bsz, DEPTH, 2, H_RET, DK_RET, DK_RET), dt)
    z_gdn = jnp.zeros((bsz, DEPTH, 2, H_GDN, DK_GDN, DV_GDN), dt)
    z_ssd = jnp.zeros((bsz, DEPTH, 2, H_SSD, N_SSD, P_SSD), dt)
    y_prompt, ctx_finals = trunk(x_prompt, c_ctx[None, :], z_ret, z_gdn, z_ssd, None, layers, final_norm_w)
    new_state_ret = jnp.stack([f[0] for f in ctx_finals], axis=1)
    new_state_gdn = jnp.stack([f[1] for f in ctx_finals], axis=1)
    new_state_ssd = jnp.stack([f[2] for f in ctx_finals], axis=1)

    rope = axial_rope(x_sample.shape[1])
    y_sample, _ = trunk(x_sample, c, state_ret, state_gdn, state_ssd, rope, layers, final_norm_w)

    return (y_prompt, y_sample, new_state_ret, new_state_gdn, new_state_ssd)
```

```python
import math
import numpy as np
import ml_dtypes
from contextlib import ExitStack
import concourse.bass as bass
import concourse.mybir as mybir
from concourse.bass_utils import run_bass_kernel_spmd

F32 = mybir.dt.float32
BF16 = mybir.dt.bfloat16
AF = mybir.ActivationFunctionType
ALU = mybir.AluOpType
AX = mybir.AxisListType

D = 1024
T = 2048
NT = 16
NB = 4
DEPTH = 4
EPS = 1e-6
IN_COLS = 3864
NEG = -30000.0
GDN_FP32 = True


class MK:
    def __init__(self):
        self.nc = bass.Bass("TRN2", target_bir_lowering=False)
        self.es = ExitStack()
        nc = self.nc
        self.eng = {"pe": nc.tensor, "act": nc.scalar, "dve": nc.vector, "pool": nc.gpsimd, "sp": nc.sync}
        self.sem = {}
        self.cnt = {}
        for e in self.eng:
            self.sem[e] = self.es.enter_context(nc.semaphore("s_" + e))
            self.cnt[e] = 0
        self.waited = {e: {} for e in self.eng}
        self.last_w = {}
        self.readers = {}
        self.dsem = {}
        self.dcnt = {}
        self.n_inst = 0
        self.bank_i = 0

    def sb(self, name, shape, dt=F32, stack=None):
        self.uid = getattr(self, "uid", 0) + 1
        return (stack or self.es).enter_context(self.nc.sbuf_tensor(f"sb{self.uid}_{name}", list(shape), dt))

    def ps(self, name, shape, dt=F32):
        return self.es.enter_context(self.nc.psum_tensor("ps_" + name, list(shape), dt))

    def dram(self, name, shape, dt=F32, kind="ExternalInput"):
        return self.nc.dram_tensor(name, list(shape), dt, kind=kind).ap()

    def _deps(self, e, reads, writes, attach=False):
        evs = []
        for key in list(reads) + list(writes):
            ev = self.last_w.get(key)
            if ev is not None:
                evs.append(ev)
        for key in writes:
            evs.extend(self.readers.get(key, ()))
        need = {}
        srcs = {}
        for (s, v, src) in evs:
            if src == "pe" and e == "pe":
                continue
            if need.get(id(s), (0,))[0] < v:
                need[id(s)] = (v, s)
                srcs[id(s)] = src
        w = self.waited[e]
        todo = [(sid, v, s) for sid, (v, s) in need.items() if w.get(sid, 0) < v]
        self._attach = None
        if attach and todo and e in ("act", "dve", "pool"):
            def age(t):
                src = srcs[t[0]]
                return (self.cnt[src] - t[1]) if src in self.cnt else -1
            todo.sort(key=age, reverse=True)
            sid, v, s = todo.pop()
            self._attach = (s, v)
            w[sid] = v
        for sid, v, s in todo:
            self.eng[e].wait_ge(s, v)
            w[sid] = v
            self.n_inst += 1

    def _record(self, ev, reads, writes):
        for key in reads:
            lst = self.readers.setdefault(key, [])
            lst[:] = [x for x in lst if not (x[0] is ev[0])]
            lst.append(ev)
        for key in writes:
            self.last_w[key] = ev
            self.readers[key] = []

    def op(self, e, fn, reads=(), writes=()):
        self._deps(e, reads, writes, attach=True)
        inst = fn(self.eng[e])
        if self._attach is not None:
            inst._wait_ge(self._attach[0], self._attach[1])
            self._attach = None
        self.cnt[e] += 1
        inst.then_inc(self.sem[e], 1)
        self.n_inst += 1
        ev = (self.sem[e], self.cnt[e], e)
        self._record(ev, reads, writes)
        return ev

    def dma(self, q, out, in_, reads=(), writes=(), chan=None, **kw):
        assert chan is not None
        if chan == "c0":
            chan = "k_" + "".join(ch if ch.isalnum() else "_" for ch in str(writes[0]))
        if chan not in self.dsem:
            self.dsem[chan] = self.es.enter_context(self.nc.semaphore("d_" + chan))
            self.dcnt[chan] = 0
        self._deps(q, reads, writes)
        inst = self.eng[q].dma_start(out=out, in_=in_, **kw)
        self.dcnt[chan] += 16
        inst.then_inc(self.dsem[chan], 16)
        self.n_inst += 1
        ev = (self.dsem[chan], self.dcnt[chan], "dma")
        self._record(ev, reads, writes)
        return ev

    def barrier(self):
        for e in self.eng:
            w = self.waited[e]
            for e2 in self.eng:
                if e2 != e and self.cnt[e2] > w.get(id(self.sem[e2]), 0):
                    self.eng[e].wait_ge(self.sem[e2], self.cnt[e2])
                    w[id(self.sem[e2])] = self.cnt[e2]
                    self.n_inst += 1
            for c, s in self.dsem.items():
                if self.dcnt[c] > w.get(id(s), 0):
                    self.eng[e].wait_ge(s, self.dcnt[c])
                    w[id(s)] = self.dcnt[c]
                    self.n_inst += 1

    def scope(self):
        return _Scope(self)

    def mm(self, out, lhsT, rhs, start, stop, reads, writes, **kw):
        return self.op("pe", lambda e: e.matmul(out, lhsT=lhsT, rhs=rhs, start=start, stop=stop, **kw), reads, writes)

    def tr(self, out, in_, ident, reads, writes):
        return self.op("pe", lambda e: e.transpose(out, in_=in_, identity=ident), reads, writes)

    def act(self, out, in_, func, reads, writes, bias=None, scale=None, eng="act"):
        kw = {}
        if bias is not None:
            kw["bias"] = bias
        if scale is not None:
            kw["scale"] = scale
        return self.op(eng, lambda e: e.activation(out=out, in_=in_, func=func, **kw), reads, writes)

    def tt(self, out, in0, in1, op, reads, writes, eng="dve"):
        return self.op(eng, lambda e: e.tensor_tensor(out=out, in0=in0, in1=in1, op=op), reads, writes)

    def ts(self, out, in0, s1, op0, reads, writes, s2=None, op1=None, eng="dve"):
        if op1 is None:
            return self.op(eng, lambda e: e.tensor_scalar(out=out, in0=in0, scalar1=s1, scalar2=None, op0=op0), reads, writes)
        return self.op(eng, lambda e: e.tensor_scalar(out=out, in0=in0, scalar1=s1, scalar2=s2, op0=op0, op1=op1), reads, writes)

    def stt(self, out, in0, scalar, in1, op0, op1, reads, writes):
        return self.op("dve", lambda e: e.scalar_tensor_tensor(out=out, in0=in0, scalar=scalar, in1=in1, op0=op0, op1=op1), reads, writes)

    def cp(self, out, in_, reads, writes, eng="act"):
        if eng == "act":
            return self.op("act", lambda e: e.copy(out=out, in_=in_), reads, writes)
        return self.op(eng, lambda e: e.tensor_copy(out=out, in_=in_), reads, writes)


class _Scope:
    def __init__(self, k):
        self.k = k
        self.st = ExitStack()

    def __enter__(self):
        return self.st

    def __exit__(self, *a):
        if a[0] is None:
            self.k.barrier()
        self.st.close()
        return False


C32 = {}


def _consts():
    names = ["ident", "ones", "triU", "triL", "blk64", "idxm_f", "idxm_b", "idx1", "idx2",
             "neg_f", "neg_b"]
    i = np.arange(128)
    J, I = np.meshgrid(i, i, indexing="ij")
    tab = {}
    tab["ident"] = (J == I).astype(np.float32)
    tab["ones"] = np.ones((128, 128), np.float32)
    tab["triU"] = (J <= I).astype(np.float32)
    tab["triL"] = (J >= I).astype(np.float32)
    tab["blk64"] = ((J // 64) == (I // 64)).astype(np.float32)
    tab["idxm_f"] = np.where(I >= J, I - J, 1e6).astype(np.float32)
    tab["idxm_b"] = np.where(I <= J, J - I, 1e6).astype(np.float32)
    tab["idx1"] = (I + 1).astype(np.float32)
    tab["idx2"] = (128 - I).astype(np.float32)
    tab["neg_f"] = np.where(I >= J, 0.0, NEG).astype(np.float32)
    tab["neg_b"] = np.where(I <= J, 0.0, NEG).astype(np.float32)
    arr = np.stack([tab[n] for n in names], axis=1)
    for n_i, n in enumerate(names):
        C32[n] = n_i
    colc = np.stack([127.0 - i, i.astype(np.float64)], axis=1).astype(np.float32)
    return np.ascontiguousarray(arr), colc


def _consts2():
    i = np.arange(128)
    J, I = np.meshgrid(i, i, indexing="ij")
    same = (J // 64) == (I // 64)
    t = [
        (same & (J <= I)),
        (same & (J >= I)),
        (J < 64) & (I >= 0),
        (J >= 64) & (I >= 0),
    ]
    out = [x.astype(np.float32) for x in t]
    out.append(np.where(same & (I >= J), 0.0, NEG).astype(np.float32))
    out.append(np.where(same & (I <= J), 0.0, NEG).astype(np.float32))
    out.append(np.where(same & (I < J), 0.0, NEG).astype(np.float32))
    out.append(np.where(same & (I > J), 0.0, NEG).astype(np.float32))
    return np.ascontiguousarray(np.stack(out, axis=1))


def build(depth=DEPTH, mixers=("ret", "hy", "gdn", "ssd"), dbg=()):
    k = MK()
    nc = k.nc
    cst_np, colc_np = _consts()
    NCST = cst_np.shape[1]

    x_d = k.dram("x", [T, D])
    cond_d = k.dram("cond", [128, 8])
    mflag_d = k.dram("mflag", [1, 2])
    cst_d = k.dram("cst", [128, NCST, 128])
    colc_d = k.dram("colc", [128, 2])
    rope_d = k.dram("rope", [128, 2, T], BF16)
    sret_d = k.dram("sret", [depth, 2, 4, 64, 64])
    W = {}
    for name, shp in [("norm1_w", [depth, D]), ("norm2_w", [depth, D]), ("w_mod", [depth, D, 6 * D]),
                      ("b_mod", [depth, 6 * D]), ("w_in", [depth, D, IN_COLS]), ("w_out", [depth, D, D]),
                      ("ret_log_decay", [depth, 2, 4]), ("ret_norm_w", [depth, 256]),
                      ("hy_conv_w", [depth, 3, 768]), ("hy_conv_b", [depth, 768]), ("hy_freq", [depth, 64]),
                      ("hy_w1", [depth, 33, 64]), ("hy_b1", [depth, 64]), ("hy_w2", [depth, 64, 64]), ("hy_b2", [depth, 64]),
                      ("hy_w3", [depth, 64, 512]), ("hy_bias", [depth, 256]), ("hy_norm_w", [depth, 256]),
                      ("gdn_conv_w", [depth, 3, 768]), ("gdn_A_log", [depth, 2, 4]), ("gdn_dt_bias", [depth, 2, 4]),
                      ("gdn_norm_w", [depth, 256]),
                      ("ssd_conv_w", [depth, 3, 768]), ("ssd_conv_b", [depth, 768]), ("ssd_A_log", [depth, 2, 4]),
                      ("ssd_dt_bias", [depth, 2, 4]), ("ssd_D", [depth, 4]), ("ssd_norm_w", [depth, 256]),
                      ("mlp_w1", [depth, D, 4 * D]), ("mlp_w2", [depth, 4 * D, D]), ("final_norm_w", [D])]:
        W[name] = k.dram(name, shp)
    y_d = k.dram("y", [T, D], kind="ExternalOutput")
    nsret_d = k.dram("nsret", [8, depth, 2, 4, 64, 64], kind="ExternalOutput")
    sssd_d = k.dram("sssd", [depth, 2, 4, 128, 64])
    sgdn_d = k.dram("sgdn", [depth, 2, 4, 64, 64])
    nsgdn_d = k.dram("nsgdn", [8, depth, 2, 4, 64, 64], kind="ExternalOutput")
    cst2_d = k.dram("cst2", [128, 8, 128])
    hyz_d = k.dram("hyz", [33, T])
    hyd_d = k.dram("hyd", [1, 256])
    hyt_d = k.dram("hyt", [128, 2, NT])
    hys_d = k.dram("hys", [128, 2, 16])
    hyg_d = k.dram("hyg", [128, NT], BF16)
    ffwd_d = k.dram("ffwd", [32, 128, 16, 128], BF16)
    finv_d = k.dram("finv", [4, 128, 32, 512], BF16)
    nsssd_d = k.dram("nsssd", [8, depth, 2, 4, 128, 64], kind="ExternalOutput")
    dbg_d = {}
    out_keys = []

    cst = k.sb("cst", [128, NCST, 128])
    cstb = k.sb("cstb", [128, 3, 128], BF16)
    colc = k.sb("colc", [128, 2])
    epsc = k.sb("epsc", [128, 4])
    k.op("dve", lambda e: e.memset(epsc[:, 0:1], EPS), writes=["epsc"])
    k.op("dve", lambda e: e.memset(epsc[:, 1:2], 64.0 * EPS), writes=["epsc"])
    k.op("dve", lambda e: e.memset(epsc[:, 2:3], 128.0 * EPS), writes=["epsc"])
    k.op("dve", lambda e: e.memset(epsc[:, 3:4], 256.0 * EPS), writes=["epsc"])
    mfl = k.sb("mfl", [128, 2])
    modT = k.sb("modT", [128, depth, 48])
    nw = k.sb("nw", [128, depth, 2, 8])
    fnw = k.sb("fnw", [128, 8])
    AB = k.sb("AB", [128, 2, 8])
    banks = [k.ps(f"bank{i}", [128, 512]) for i in range(8)]

    def C(name):
        return cst[:, C32[name], :]

    IDB, ONESB, BLKB = cstb[:, 0, :], cstb[:, 1, :], cstb[:, 2, :]

    def run_pipelined(gen_fn, items, depth):
        items = list(items)
        active = []
        nxt = 0
        while nxt < len(items) or active:
            while nxt < len(items) and len(active) < depth:
                slot = [sl for sl in range(depth) if sl not in [a[0] for a in active]][0]
                active.append((slot, gen_fn(items[nxt], slot)))
                nxt += 1
            for a in list(active):
                try:
                    next(a[1])
                except StopIteration:
                    active.remove(a)

    def bank():
        i = k.bank_i
        k.bank_i = (i + 1) % 8
        return banks[i], f"bank{i}"

    stg_i = [0]
    cast_i = [0]

    def load_w(dst, dkey, src, nk, ncols, coff=0, cast_eng=None):
        if ncols > 1024:
            load_w(dst, dkey, src[:, 0:1024], nk, 1024, coff, cast_eng)
            load_w(dst, dkey, src[:, 1024:ncols], nk, ncols - 1024, coff + 1024, cast_eng)
            return
        per = max(1, 1024 // ncols)
        kc = 0
        while kc < nk:
            n = min(per, nk - kc)
            si = stg_i[0]
            stg_i[0] = (si + 1) % 2
            sview = stg[si][:, 0:n * ncols].rearrange("p (a c) -> p a c", c=ncols)
            k.dma("sp", sview, src[kc * 128:(kc + n) * 128, :].rearrange("(a p) c -> p a c", p=128),
                  writes=[f"stg{si}"], chan=f"stg{si}")
            cast_i[0] += 1
            k.cp(dst[:, kc:kc + n, coff:coff + ncols], sview, reads=[f"stg{si}"], writes=[dkey],
                 eng=(cast_eng or ("pool" if cast_i[0] % 3 == 0 else "act")))
            kc += n

    def load_w_gen(dst, dkey, src, nk, ncols, cast_eng="act"):
        assert ncols <= 1024
        per = max(1, 1024 // ncols)
        pend = None
        kc = 0
        while kc < nk or pend is not None:
            cur = None
            if kc < nk:
                n = min(per, nk - kc)
                si = stg_i[0]
                stg_i[0] = (si + 1) % 2
                sview = stg[si][:, 0:n * ncols].rearrange("p (a c) -> p a c", c=ncols)
                k.dma("sp", sview, src[kc * 128:(kc + n) * 128, :].rearrange("(a p) c -> p a c", p=128),
                      writes=[f"stg{si}"], chan=f"stg{si}")
                cur = (kc, n, si, sview)
                kc += n
            if pend is not None:
                pkc, pn, psi, pview = pend
                k.cp(dst[:, pkc:pkc + pn, 0:ncols], pview, reads=[f"stg{psi}"], writes=[dkey], eng=cast_eng)
            pend = cur
            yield

    k.dma("sp", cst[:], cst_d, writes=["cst"], chan="c0")
    k.dma("sp", colc[:], colc_d, writes=["colc"], chan="c0")
    k.dma("sp", mfl[:], mflag_d.partition_broadcast(128), writes=["mfl"], chan="c0")
    for j, nm in enumerate(["ident", "ones", "blk64"]):
        k.cp(cstb[:, j, :], C(nm), reads=["cst"], writes=["cstb"], eng="dve")
    with nc.allow_non_contiguous_dma(reason="tiny feature-major param loads"):
        for l in range(depth):
            k.dma("sp", nw[:, l, 0, :], W["norm1_w"][l].rearrange("(c p) -> p c", p=128), writes=["nw"], chan="c0")
            k.dma("sp", nw[:, l, 1, :], W["norm2_w"][l].rearrange("(c p) -> p c", p=128), writes=["nw"], chan="c0")
        k.dma("sp", fnw[:], W["final_norm_w"].rearrange("(c p) -> p c", p=128), writes=["fnw"], chan="c0")

    with k.scope() as st:
        scond = k.sb("scond", [128, 8], stack=st)
        condr = k.sb("condr", [128, 8], stack=st)
        mrow = k.sb("mrow", [1, 6 * D], stack=st)
        brow = k.sb("brow", [1, 6 * D], stack=st)
        one1 = k.sb("one1", [1, 1], stack=st)
        NWST = 8
        wst = [k.sb(f"wst{i}", [128, 2048], stack=st) for i in range(NWST)]
        k.dma("sp", condr[:], cond_d, writes=["condr"], chan="c0")
        k.act(scond[:], condr[:], AF.Silu, reads=["condr"], writes=["scond"])
        k.op("dve", lambda e: e.memset(one1[:], 1.0), writes=["one1"])
        for l in range(depth):
            k.dma("sp", brow[:], W["b_mod"][l:l + 1, :], writes=["brow"], chan="c0")
            wi = 0
            for cb in range(3):
                pbs = [bank() for _ in range(4)]
                for kc in range(8):
                    ws, wk = wst[wi % NWST], f"wst{wi % NWST}"
                    wi += 1
                    k.dma("sp", ws[:], W["w_mod"][l, kc * 128:(kc + 1) * 128, cb * 2048:(cb + 1) * 2048],
                          writes=[wk], chan=wk)
                    for q in range(4):
                        k.mm(pbs[q][0][0:1, :], scond[:, kc:kc + 1], ws[:, q * 512:(q + 1) * 512], kc == 0, kc == 7,
                             reads=[wk, "scond"], writes=[pbs[q][1]])
                for q in range(4):
                    c0 = cb * 2048 + q * 512
                    k.tt(mrow[:, c0:c0 + 512], pbs[q][0][0:1, :], brow[:, c0:c0 + 512], ALU.add,
                         reads=[pbs[q][1], "brow"], writes=["mrow"])
            b, bk = bank()
            for j in range(48):
                k.mm(b[:, j:j + 1], mrow[0:1, j * 128:(j + 1) * 128], one1[0:1, 0:1], True, True,
                     reads=["mrow", "one1"], writes=[bk])
            k.cp(modT[:, l, :], b[:, 0:48], reads=[bk], writes=["modT"], eng="dve")

    xT = k.sb("xT", [128, 8, T])
    hT = k.sb("hT", [128, 8, T], BF16)
    stg = []

    def alloc_stg(stack):
        stg[:] = [k.sb(f"stg{i}", [128, 1024], stack=stack) for i in range(2)]
    _xs = k.scope()
    alloc_stg(_xs.__enter__())
    for t in range(NT):
        si = stg_i[0]
        stg_i[0] = (si + 1) % 2
        k.dma("sp", stg[si][:], x_d[t * 128:(t + 1) * 128, :], writes=[f"stg{si}"], chan=f"stg{si}")
        for half in range(2):
            b, bk = bank()
            for c4 in range(4):
                c = half * 4 + c4
                k.tr(b[:, c4 * 128:(c4 + 1) * 128], stg[si][:, c * 128:(c + 1) * 128], C("ident"),
                     reads=[f"stg{si}", "cst"], writes=[bk])
            k.cp(xT[:, half * 4:half * 4 + 4, t * 128:(t + 1) * 128],
                 b[:].rearrange("p (a c) -> p a c", c=128), reads=[bk], writes=[("xT", t // 4)])
    _xs.__exit__(None, None, None)

    def rmsnorm_mod(l, which, st):
        sqs = [k.sb(f"nsq{i}", [128, 512], BF16, stack=st) for i in range(4)]
        rstds = [k.sb(f"nrstd{i}", [128, 512], stack=st) for i in range(2)]
        tmps = [k.sb(f"ntmp{i}", [128, 512], stack=st) for i in range(4)]
        o = 0 if which == 0 else 24
        k.stt(AB[:, 0, :], modT[:, l, o + 8:o + 16], 1.0, nw[:, l, which, :], ALU.add, ALU.mult,
              reads=["modT", "nw"], writes=["AB"])
        k.cp(AB[:, 1, :], modT[:, l, o:o + 8], reads=["modT"], writes=["AB"], eng="dve")
        for tb in range(NB):
            sl = slice(tb * 512, (tb + 1) * 512)
            b, bk = bank()
            rstd, rk_ = rstds[tb % 2], f"nrstd{tb % 2}"
            for c in range(8):
                sq, sk_ = sqs[c % 4], f"nsq{c % 4}"
                if c % 2 == 0:
                    k.act(sq[:], xT[:, c, sl], AF.Square, reads=[("xT", tb)], writes=[sk_])
                else:
                    k.tt(sq[:], xT[:, c, sl], xT[:, c, sl], ALU.mult, reads=[("xT", tb)], writes=[sk_])
                k.mm(b[:], ONESB, sq[:], c == 0, c == 7, reads=[sk_, "cstb"], writes=[bk])
            k.act(rstd[:], b[:], AF.Ln, reads=[bk, "epsc"], writes=[rk_], bias=epsc[:, 0:1], scale=1.0 / D)
            k.act(rstd[:], rstd[:], AF.Exp, reads=[rk_], writes=[rk_], scale=-0.5)
            for c in range(8):
                tmp, tk_ = tmps[c % 4], f"ntmp{c % 4}"
                k.tt(tmp[:], xT[:, c, sl], rstd[:], ALU.mult, reads=[("xT", tb), rk_], writes=[tk_])
                k.act(hT[:, c, sl], tmp[:], AF.Identity, reads=[tk_, "AB"], writes=[("hT", tb)],
                      bias=AB[:, 1, c:c + 1], scale=AB[:, 0, c:c + 1])

    def proj_fm(wt, wkey, col0, ncol, evac):
        for tb in range(NB):
            b, bk = bank()
            for kc in range(8):
                k.mm(b[0:ncol, :], wt[:, kc, col0:col0 + ncol], hT[:, kc, tb * 512:(tb + 1) * 512], kc == 0, kc == 7,
                     reads=[wkey, ("hT", tb)], writes=[bk])
            evac(tb, b, bk)

    def out_proj(l, src, skey, nkc, wrow0, gate_off):
        with k.scope() as st2:
            alloc_stg(st2)
            wt, wkey = k.sb("wo_w", [128, nkc, D], BF16, stack=st2), "wo_w"
            load_w(wt, wkey, W["w_out"][l, wrow0:wrow0 + nkc * 128, :], nkc, D)
            for dc in range(8):
                for tb in range(NB):
                    sl = slice(tb * 512, (tb + 1) * 512)
                    b, bk = bank()
                    for kc in range(nkc):
                        k.mm(b[:], wt[:, kc, dc * 128:(dc + 1) * 128], src[:, kc, sl], kc == 0, kc == nkc - 1,
                             reads=[wkey, skey], writes=[bk])
                    k.stt(xT[:, dc, sl], b[:], modT[:, l, gate_off + dc:gate_off + dc + 1], xT[:, dc, sl], ALU.mult, ALU.add,
                          reads=[bk, "modT", ("xT", tb)], writes=[("xT", tb)])

    def ret_proj(l, qT, kT, gT, vtok, t1):
        with k.scope() as st2:
            alloc_stg(st2)
            wa, wak = k.sb("r_wa", [128, 8, 1024], BF16, stack=st2), "r_wa"
            wsw, wswk = k.sb("r_wsw", [128, 8, 512], BF16, stack=st2), "r_wsw"
            ropet = k.sb("r_rope", [128, 2, T], BF16, stack=st2)
            t2 = k.sb("r_t2", [128, 512], stack=st2)
            k.dma("sp", ropet[:], rope_d, writes=["r_rope"], chan="c0")
            load_w(wa, wak, W["w_in"][l, :, 0:1024], 8, 1024)
            with nc.allow_non_contiguous_dma(reason="rope half-swapped q/k weight columns"):
                for kc in range(8):
                    si = stg_i[0]
                    stg_i[0] = (si + 1) % 2
                    sv = stg[si][:, 0:512].rearrange("p (h two f) -> p h two f", two=2, f=32)
                    src = W["w_in"][l, kc * 128:(kc + 1) * 128, 0:512].rearrange("p (h two f) -> p h two f", two=2, f=32)
                    k.dma("sp", sv[:, :, 0, :], src[:, :, 1, :], writes=[f"stg{si}"], chan=f"stg{si}")
                    k.dma("sp", sv[:, :, 1, :], src[:, :, 0, :], writes=[f"stg{si}"], chan=f"stg{si}")
                    k.cp(wsw[:, kc, 0:512], stg[si][:, 0:512], reads=[f"stg{si}"], writes=[wswk], eng="pool")

            for which, dst, dkey, scl in ((0, qT, "r_qT", 1.0), (1, kT, "r_kT", 0.125)):
                for pr in range(2):
                    col = which * 256 + pr * 128
                    for tb in range(NB):
                        sl = slice(tb * 512, (tb + 1) * 512)
                        b1, bk1 = bank()
                        b2, bk2 = bank()
                        for kc in range(8):
                            k.mm(b1[:], wa[:, kc, col:col + 128], hT[:, kc, sl], kc == 0, kc == 7, reads=[wak, ("hT", tb)], writes=[bk1])
                        for kc in range(8):
                            k.mm(b2[:], wsw[:, kc, col:col + 128], hT[:, kc, sl], kc == 0, kc == 7, reads=[wswk, ("hT", tb)], writes=[bk2])
                        k.stt(t1[:], b1[:], scl, ropet[:, 0, sl], ALU.mult, ALU.mult, reads=[bk1, "r_rope"], writes=["r_t1"])
                        k.stt(t2[:], b2[:], scl, ropet[:, 1, sl], ALU.mult, ALU.mult, reads=[bk2, "r_rope"], writes=["r_t2"])
                        k.tt(dst[:, pr, sl], t1[:], t2[:], ALU.add, reads=["r_t1", "r_t2"], writes=[dkey], eng="pool")
            for pr in range(2):
                proj_fm(wa, wak, 768 + pr * 128, 128,
                        lambda tb, b, bk, pr=pr: k.act(gT[:, pr, tb * 512:(tb + 1) * 512], b[:], AF.Silu, reads=[bk], writes=["r_gT"]))
            for t in range(NT):
                b, bk = bank()
                for kc in range(8):
                    k.mm(b[:, 0:256], hT[:, kc, t * 128:(t + 1) * 128], wa[:, kc, 512:768], kc == 0, kc == 7,
                         reads=[wak, ("hT", t // 4)], writes=[bk])
                k.cp(vtok[:, t, :], b[:, 0:256], reads=[bk], writes=["r_vtok"])

    def ret_mixer(l, st):
        omix = k.sb("r_omix", [128, 2, T], BF16, stack=st)
        qT = k.sb("r_qT", [128, 2, T], BF16, stack=st)
        kT = k.sb("r_kT", [128, 2, T], BF16, stack=st)
        gT = k.sb("r_gT", [128, 2, T], BF16, stack=st)
        vtok = k.sb("r_vtok", [128, NT, 256], BF16, stack=st)
        t1 = k.sb("r_t1", [128, 512], stack=st)
        w8 = k.sb("r_w8", [128, 2], stack=st)
        ld = W["ret_log_decay"]
        ret_proj(l, qT, kT, gT, vtok, t1)
        if "stop:proj" in dbg:
            return
        with k.scope() as st3:
            ret_scan(l, st3, omix, qT, kT, gT, vtok, t1, w8, ld)
        if "stop:scan" in dbg or "stop:tables" in dbg or "stop:state" in dbg:
            return
        out_proj(l, omix, "r_omix", 2, 0, 16)

    def ret_scan(l, st, omix, qT, kT, gT, vtok, t1, w8, ld):
        Sf = k.sb("r_Sf", [128, NT, 128], BF16, stack=st)
        Sb = k.sb("r_Sb", [128, NT, 128], BF16, stack=st)
        S32 = k.sb("r_S32", [128, 2, 128], stack=st)
        Sout = k.sb("r_Sout", [128, 2, 8, 128], stack=st)
        gcol = k.sb("r_gcol", [128, 2, 2], stack=st)
        acol = k.sb("r_acol", [128, 2, 2], stack=st)
        gall = k.sb("r_gall", [128, 8], stack=st)
        dk = k.sb("r_dk", [128, 8], stack=st)
        dec2 = k.sb("r_dec2", [128, 4, 128], BF16, stack=st)
        dtmp = k.sb("r_dtmp", [128, 4, 128], stack=st)
        EE = k.sb("r_EE", [128, 2, 2, 128], stack=st)
        kd = k.sb("r_kd", [128, 256], BF16, stack=st)
        att = k.sb("r_att", [128, 4, 128], BF16, stack=st)
        qE = k.sb("r_qE", [128, 2, 2, 128], BF16, stack=st)
        o32 = k.sb("r_o32", [128, 2, 512], stack=st)
        sq = k.sb("r_sq", [128, 512], BF16, stack=st)
        rs = k.sb("r_rs", [128, 512], stack=st)

        with nc.allow_non_contiguous_dma(reason="tiny"):
            for half in range(2):
                k.dma("sp", gcol[half * 64:(half + 1) * 64, :, :],
                      ld[l:l + 1].rearrange("o d (pr hf) -> o hf d pr", hf=2)[:, half].partition_broadcast(64),
                      writes=["r_gcol"], chan="c0")
            k.dma("sp", gall[:], ld[l:l + 1].rearrange("o d h -> o (d h)").partition_broadcast(128), writes=["r_gall"], chan="c0")
            k.dma("sp", w8[:], W["ret_norm_w"][l].rearrange("(c p) -> p c", p=128), writes=["r_w8"], chan="c0")
        k.ts(w8[:], w8[:], 8.0, ALU.mult, reads=["r_w8"], writes=["r_w8"])
        for h in range(4):
            k.act(dtmp[:, h, :], C("idxm_f"), AF.Exp, reads=["cst", "r_gall"], writes=["r_dtmp"], scale=gall[:, h:h + 1])
            k.act(att[:, h, :], C("idxm_b"), AF.Exp, reads=["cst", "r_gall"], writes=["r_att"], scale=gall[:, 4 + h:5 + h])
        k.tt(dec2[:], dtmp[:], att[:], ALU.add, reads=["r_dtmp", "r_att"], writes=["r_dec2"])
        for d in range(2):
            for pr in range(2):
                k.act(EE[:, d, pr, :], C("idx1" if d == 0 else "idx2"), AF.Exp, reads=["cst", "r_gcol"], writes=["r_EE"],
                      scale=gcol[:, d, pr:pr + 1])
            k.act(dk[:, d * 4:d * 4 + 4], gall[:, d * 4:d * 4 + 4], AF.Exp, reads=["r_gall", "colc"], writes=["r_dk"],
                  scale=colc[:, d:d + 1])
        k.act(acol[:], gcol[:], AF.Exp, reads=["r_gcol"], writes=["r_acol"], scale=128.0)

        if "stop:tables" in dbg:
            return
        kds = [kd, k.sb("r_kd1", [128, 256], BF16, stack=st)]

        def ret_pass(d):
            kd = kds[d]
            kdk = f"r_kd{d}"
            order = list(range(NT)) if d == 0 else list(range(NT - 1, -1, -1))
            snap = Sf if d == 0 else Sb
            snk = "r_Sf" if d == 0 else "r_Sb"
            skey = f"r_S32_{d}"
            with nc.allow_non_contiguous_dma(reason="state load"):
                k.dma("sp", S32[:, d, :].rearrange("p (pr v) -> p pr v", v=64),
                      sret_d[l, d].rearrange("(pr hf) kk v -> (hf kk) pr v", hf=2), writes=[skey], chan="c0")
            for idx, n in enumerate(order):
                if idx > 0 and idx % 2 == 0:
                    k.ts(S32[:, d, :], S32[:, d, :], mfl[:, 0:1], ALU.mult, reads=[skey, "mfl"], writes=[skey])
                k.cp(snap[:, n, :], S32[:, d, :], reads=[skey], writes=[snk])
                b, bk = bank()
                bt = b[:].bitcast(BF16)
                for pr in range(2):
                    k.tr(bt[:, pr * 128:(pr + 1) * 128], kT[:, pr, n * 128:(n + 1) * 128], IDB, reads=["r_kT", "cstb"], writes=[bk])
                k.tt(kd[:].rearrange("p (h f) -> p h f", f=64), bt[:, 0:256].rearrange("p (h f) -> p h f", f=64),
                     dk[:, d * 4:d * 4 + 4].unsqueeze(2).to_broadcast([128, 4, 64]), ALU.mult, reads=[bk, "r_dk"], writes=[kdk])
                b2, bk2 = bank()
                for h in range(4):
                    pb = (h % 2) * 64
                    k.mm(b2[pb:pb + 64, (h // 2) * 64:(h // 2) * 64 + 64], kd[:, h * 64:(h + 1) * 64], vtok[:, n, h * 64:(h + 1) * 64],
                         True, True, reads=[kdk, "r_vtok"], writes=[bk2])
                for pr in range(2):
                    k.stt(S32[:, d, pr * 64:(pr + 1) * 64], S32[:, d, pr * 64:(pr + 1) * 64], acol[:, d, pr:pr + 1],
                          b2[:, pr * 64:(pr + 1) * 64], ALU.mult, ALU.add, reads=[skey, "r_acol", bk2], writes=[skey])
                if idx % 2 == 1:
                    seg = n // 2
                    k.cp(Sout[:, d, seg, :], S32[:, d, :], reads=[skey], writes=["r_Sout"])
                yield
        gens = [ret_pass(0), ret_pass(1)]
        while gens:
            for gg in list(gens):
                try:
                    next(gg)
                except StopIteration:
                    gens.remove(gg)
        with nc.allow_non_contiguous_dma(reason="state store"):
            for d in range(2):
                for pr in range(2):
                    k.dma("sp", nsret_d[:, l, d, pr * 2:pr * 2 + 2].rearrange("s hf kk v -> (hf kk) s v"),
                          Sout[:, d, :, pr * 64:(pr + 1) * 64], reads=["r_Sout"], writes=["nsret"], chan="nsret")

        if "stop:state" in dbg:
            return
        PD = 3
        atts = [att] + [k.sb(f"r_att{i}", [128, 4, 128], BF16, stack=st) for i in range(1, PD)]
        qEs = [qE] + [k.sb(f"r_qE{i}", [128, 2, 2, 128], BF16, stack=st) for i in range(1, PD)]

        def ret_tile(item, slot):
            tb, t4 = item
            n = tb * 4 + t4
            tsl = slice(n * 128, (n + 1) * 128)
            att, qE = atts[slot], qEs[slot]
            sx = f"_{slot}"
            bb2 = [bank(), bank()]
            for h in (0, 2, 1, 3):
                pb = (h % 2) * 64
                b, bk = bb2[h % 2]
                k.mm(b[:, (h // 2) * 128:(h // 2 + 1) * 128], kT[pb:pb + 64, h // 2, tsl], qT[pb:pb + 64, h // 2, tsl], True, True,
                     reads=["r_kT", "r_qT"], writes=[bk])
            for d in range(2):
                k.tt(qE[:, d], qT[:, :, tsl], EE[:, d], ALU.mult, reads=["r_qT", "r_EE"], writes=["r_qE" + sx])
            yield
            for hf in range(2):
                b, bk = bb2[hf]
                k.tt(att[:, hf::2, :], b[:, 0:256].rearrange("p (h i) -> p h i", i=128), dec2[:, hf::2, :], ALU.mult,
                     reads=[bk, "r_dec2"], writes=["r_att" + sx])
            yield
            ob2 = [bank(), bank()]
            for h in (0, 2, 1, 3):
                pb = (h % 2) * 64
                pr = h // 2
                b2, bk2 = ob2[h % 2]
                o_ap = b2[pb:pb + 64, pr * 128:(pr + 1) * 128]
                k.mm(o_ap, vtok[:, n, h * 64:(h + 1) * 64], att[:, h, :], True, False, reads=["r_vtok", "r_att" + sx], writes=[bk2])
                k.mm(o_ap, Sf[pb:pb + 64, n, pr * 64:(pr + 1) * 64], qE[pb:pb + 64, 0, pr, :], False, False,
                     reads=["r_Sf", "r_qE" + sx], writes=[bk2])
                k.mm(o_ap, Sb[pb:pb + 64, n, pr * 64:(pr + 1) * 64], qE[pb:pb + 64, 1, pr, :], False, True,
                     reads=["r_Sb", "r_qE" + sx], writes=[bk2])
            yield
            for hf in range(2):
                b2, bk2 = ob2[hf]
                pb = hf * 64
                k.cp(o32[pb:pb + 64, :, t4 * 128:(t4 + 1) * 128], b2[pb:pb + 64, 0:256].rearrange("p (pr i) -> p pr i", i=128),
                     reads=[bk2], writes=[("r_o32", t4)])

        for tb in range(NB):
            run_pipelined(ret_tile, [(tb, t4) for t4 in range(4)], PD)
            okeys = [("r_o32", j) for j in range(4)]
            sl = slice(tb * 512, (tb + 1) * 512)
            if "ost:0" in dbg or "ost:1" in dbg or "ost:2" in dbg or "ost:3" in dbg:
                continue
            for pr in range(2):
                k.act(sq[:], o32[:, pr, :], AF.Square, reads=okeys, writes=["r_sq"])
                b, bk = bank()
                k.mm(b[:], BLKB, sq[:], True, True, reads=["r_sq", "cstb"], writes=[bk])
                k.act(rs[:], b[:], AF.Ln, reads=[bk, "epsc"], writes=["r_rs"], bias=epsc[:, 1:2], scale=1.0)
                k.act(rs[:], rs[:], AF.Exp, reads=["r_rs"], writes=["r_rs"], scale=-0.5)
                k.tt(t1[:], o32[:, pr, :], rs[:], ALU.mult, reads=okeys + ["r_rs"], writes=["r_t1"])
                k.stt(omix[:, pr, sl], t1[:], w8[:, pr:pr + 1], gT[:, pr, sl], ALU.mult, ALU.mult,
                      reads=["r_t1", "r_w8", "r_gT"], writes=["r_omix"])
        if "ret_o" in dbg:
            dbg_d["ret_o"] = k.dram("dbg_ret_o", [128, 2, T], BF16, kind="ExternalOutput")
            k.dma("sp", dbg_d["ret_o"], omix[:], reads=["r_omix"], writes=["dbg_ret_o"], chan="dbg")
            out_keys.append("dbg_ret_o")

    def load_conv_params(cw, ckey, wname, bname, l, nch, st):
        with nc.allow_non_contiguous_dma(reason="tiny conv params"):
            for kk in range(3):
                k.dma("sp", cw[:, :, kk], W[wname][l, kk].rearrange("(c p) -> p c", p=128), writes=[ckey], chan="c0")
            if bname is not None:
                k.dma("sp", cw[:, :, 3], W[bname][l].rearrange("(c p) -> p c", p=128), writes=[ckey], chan="c0")
            else:
                k.op("dve", lambda e: e.memset(cw[:, :, 3:4], 0.0), writes=[ckey])
        k.ts(cw[:, :, 4:5], cw[:, :, 0:1], mfl[:, 1:2], ALU.mult, reads=[ckey, "mfl"], writes=[ckey], s2=-1.0, op1=ALU.mult)
        k.ts(cw[:, :, 5:6], cw[:, :, 2:3], mfl[:, 1:2], ALU.mult, reads=[ckey, "mfl"], writes=[ckey], s2=-1.0, op1=ALU.mult)

    def conv_chunk(wt, wkey, col0, cw, ckey, ci, pre, acc, dst, dkey, func):
        for tb in range(NB):
            b, bk = bank()
            for kc in range(8):
                k.mm(b[:], wt[:, kc, col0:col0 + 128], hT[:, kc, tb * 512:(tb + 1) * 512], kc == 0, kc == 7,
                     reads=[wkey, ("hT", tb)], writes=[bk])
            k.cp(pre[:, 1 + tb * 512:1 + (tb + 1) * 512], b[:], reads=[bk], writes=["cv_pre"])
        k.act(acc[:], pre[:, 1:T + 1], AF.Identity, reads=["cv_pre", ckey], writes=["cv_acc"],
              bias=cw[:, ci, 3:4], scale=cw[:, ci, 1:2])
        k.stt(acc[:], pre[:, 0:T], cw[:, ci, 0:1], acc[:], ALU.mult, ALU.add, reads=["cv_pre", ckey, "cv_acc"], writes=["cv_acc"])
        k.stt(acc[:], pre[:, 2:T + 2], cw[:, ci, 2:3], acc[:], ALU.mult, ALU.add, reads=["cv_pre", ckey, "cv_acc"], writes=["cv_acc"])
        av = acc[:, 256:T].rearrange("p (s q) -> p s q", q=256)[:, :, 0]
        pv = pre[:, 256:T].rearrange("p (s q) -> p s q", q=256)[:, :, 0]
        k.stt(av, pv, cw[:, ci, 4:5], av, ALU.mult, ALU.add, reads=["cv_pre", ckey, "cv_acc"], writes=["cv_acc"])
        av2 = acc[:, 255:T - 1].rearrange("p (s q) -> p s q", q=256)[:, :, 0]
        pv2 = pre[:, 257:T + 1].rearrange("p (s q) -> p s q", q=256)[:, :, 0]
        k.stt(av2, pv2, cw[:, ci, 5:6], av2, ALU.mult, ALU.add, reads=["cv_pre", ckey, "cv_acc"], writes=["cv_acc"])
        if func is None:
            k.cp(dst, acc[:], reads=["cv_acc"], writes=[dkey], eng="dve")
        else:
            k.act(dst, acc[:], func, reads=["cv_acc"], writes=[dkey])

    def to_tok(src, skey, nchunk, dst, dkey):
        for t in range(NT):
            b, bk = bank()
            bt = b[:].bitcast(BF16)
            for c in range(nchunk):
                k.tr(bt[:, c * 128:(c + 1) * 128], src[:, c, t * 128:(t + 1) * 128], IDB, reads=[skey, "cstb"], writes=[bk])
            k.cp(dst[:, t, :], bt[:, 0:nchunk * 128], reads=[bk], writes=[dkey])

    def head_cols(dst, dkey, src_row):
        with nc.allow_non_contiguous_dma(reason="tiny"):
            for half in range(2):
                k.dma("sp", dst[half * 64:(half + 1) * 64, :],
                      src_row.rearrange("o (pr hf) -> o hf pr", hf=2)[:, half].partition_broadcast(64), writes=[dkey], chan="c0")

    SSD0 = 2832

    def ssd_mixer(l, st):
        omix = k.sb("s_omix", [128, 2, T], BF16, stack=st)
        bmT = k.sb("s_bmT", [128, 2, T], BF16, stack=st)
        cmT = k.sb("s_cmT", [128, 2, T], BF16, stack=st)
        zT = k.sb("s_zT", [128, 2, T], BF16, stack=st)
        xtok = k.sb("s_xtok", [128, NT, 256], BF16, stack=st)
        btok = k.sb("s_btok", [128, NT, 256], BF16, stack=st)
        dt = k.sb("s_dt", [128, NT, 8], stack=st)
        g = k.sb("s_g", [128, NT, 8], stack=st)
        lndt = k.sb("s_lndt", [128, NT, 8], stack=st)
        one_c = k.sb("s_one", [128, 1], stack=st)
        k.op("dve", lambda e: e.memset(one_c[:], 1.0), writes=["s_one"])
        with k.scope() as st2:
            alloc_stg(st2)
            wt, wkey = k.sb("s_w", [128, 8, 1032], BF16, stack=st2), "s_w"
            xsT = k.sb("s_xsT", [128, 2, T], BF16, stack=st2)
            pre = k.sb("cv_pre", [128, T + 2], stack=st2)
            acc = k.sb("cv_acc", [128, T], stack=st2)
            cw = k.sb("s_cw", [128, 6, 6], stack=st2)
            dtb = k.sb("s_dtb", [128, 8], stack=st2)
            nA = k.sb("s_nA", [128, 8], stack=st2)
            load_w(wt, wkey, W["w_in"][l, :, SSD0:SSD0 + 1032], 8, 1032)
            load_conv_params(cw, "s_cw", "ssd_conv_w", "ssd_conv_b", l, 6, st2)
            k.op("dve", lambda e: e.memset(pre[:, 0:1], 0.0), writes=["cv_pre"])
            k.op("dve", lambda e: e.memset(pre[:, T + 1:T + 2], 0.0), writes=["cv_pre"])
            k.dma("sp", dtb[:], W["ssd_dt_bias"][l:l + 1].rearrange("o d h -> o (d h)").partition_broadcast(128), writes=["s_dtb"], chan="c0")
            k.dma("sp", nA[:], W["ssd_A_log"][l:l + 1].rearrange("o d h -> o (d h)").partition_broadcast(128), writes=["s_nA"], chan="c0")
            k.act(nA[:], nA[:], AF.Exp, reads=["s_nA"], writes=["s_nA"])
            k.ts(nA[:], nA[:], -1.0, ALU.mult, reads=["s_nA"], writes=["s_nA"])
            for ci, (dst, dkey) in enumerate([(xsT[:, 0, :], "s_xsT"), (xsT[:, 1, :], "s_xsT"), (bmT[:, 0, :], "s_bmT"),
                                               (bmT[:, 1, :], "s_bmT"), (cmT[:, 0, :], "s_cmT"), (cmT[:, 1, :], "s_cmT")]):
                conv_chunk(wt, wkey, ci * 128, cw, "s_cw", ci, pre, acc, dst, dkey, AF.Silu)
            for pr in range(2):
                proj_fm(wt, wkey, 768 + pr * 128, 128,
                        lambda tb, b, bk, pr=pr: k.act(zT[:, pr, tb * 512:(tb + 1) * 512], b[:], AF.Silu, reads=[bk], writes=["s_zT"]))
            to_tok(xsT, "s_xsT", 2, xtok, "s_xtok")
            to_tok(bmT, "s_bmT", 2, btok, "s_btok")
            b, bk = bank()
            for t in range(NT):
                for kc in range(8):
                    k.mm(b[:, t * 8:(t + 1) * 8], hT[:, kc, t * 128:(t + 1) * 128], wt[:, kc, 1024:1032], kc == 0, kc == 7,
                         reads=[wkey, ("hT", t // 4)], writes=[bk])
            k.tt(dt[:], b[:, 0:NT * 8].rearrange("p (t e) -> p t e", e=8), dtb[:].unsqueeze(1).to_broadcast([128, NT, 8]), ALU.add,
                 reads=[bk, "s_dtb"], writes=["s_dt"])
            k.act(dt[:], dt[:], AF.Exp, reads=["s_dt"], writes=["s_dt"])
            k.act(dt[:], dt[:], AF.Ln, reads=["s_dt", "s_one"], writes=["s_dt"], bias=one_c[:, 0:1])
            k.ts(dt[:], dt[:], 1e-30, ALU.max, reads=["s_dt"], writes=["s_dt"])
            k.act(lndt[:], dt[:], AF.Ln, reads=["s_dt"], writes=["s_lndt"])
            k.tt(g[:], dt[:], nA[:].unsqueeze(1).to_broadcast([128, NT, 8]), ALU.mult, reads=["s_dt", "s_nA"], writes=["s_g"])
        if "stop:proj" in dbg:
            return
        with k.scope() as st3:
            ssd_scan(l, st3, omix, bmT, cmT, zT, xtok, btok, dt, g, lndt)
        if "stop:state" in dbg:
            return
        out_proj(l, omix, "s_omix", 2, 768, 16)

    def decay_tile(n, g, gkey, pfx, cumt, tot_sb):
        b, bk = bank()
        k.mm(b[:, 0:4], C("triU"), g[:, n, 0:4], True, True, reads=["cst", gkey], writes=[bk])
        k.mm(b[:, 4:8], C("triL"), g[:, n, 4:8], True, True, reads=["cst", gkey], writes=[bk])
        k.mm(b[:, 8:16], C("ones"), g[:, n, :], True, True, reads=["cst", gkey], writes=[bk])
        k.cp(cumt[:, n, :], b[:, 0:8], reads=[bk], writes=[pfx + "cumt"], eng="dve")
        k.cp(tot_sb[:, n, :], b[:, 8:16], reads=[bk], writes=[pfx + "tot"], eng="dve")

    def cumB_tile(n, g, gkey, pfx, GU):
        cb = []
        for d in range(2):
            k.tt(GU[:, 0], g[:, n, d * 4:(d + 1) * 4].unsqueeze(2).to_broadcast([128, 4, 128]),
                 C("triU" if d == 0 else "triL").unsqueeze(1).to_broadcast([128, 4, 128]), ALU.mult,
                 reads=[gkey, "cst"], writes=[pfx + "GU"])
            b, bk = bank()
            k.mm(b[:], C("ones"), GU[:, 0].rearrange("p h i -> p (h i)"), True, True, reads=["cst", pfx + "GU"], writes=[bk])
            cb.append((b, bk))
        return cb

    def ssd_scan(l, st, omix, bmT, cmT, zT, xtok, btok, dt, g, lndt):
        Sf = k.sb("s_Sf", [128, NT, 256], BF16, stack=st)
        Sb = k.sb("s_Sb", [128, NT, 256], BF16, stack=st)
        S32 = k.sb("s_S32", [128, 2, 256], stack=st)
        sst = [k.sb(f"s_sst{i}", [128, 256], stack=st) for i in range(2)]
        Dall = k.sb("s_Dall", [128, 4], stack=st)
        idD = k.sb("s_idD", [128, 4, 128], BF16, stack=st)
        cumt = k.sb("s_cumt", [128, NT, 8], stack=st)
        tot = k.sb("s_tot", [128, NT, 8], stack=st)
        wgt = k.sb("s_wgt", [128, NT, 8], stack=st)
        aex = k.sb("s_aex", [128, NT, 8], stack=st)
        bias1 = k.sb("s_bias1", [128, NT, 8], stack=st)
        GU = k.sb("s_GU", [128, 1, 4, 128], stack=st)
        tmpD = k.sb("s_tmpD", [128, 4, 128], stack=st)
        decm = k.sb("s_decm", [128, 2, 4, 128], BF16, stack=st)
        dec2 = k.sb("s_dec2", [128, 4, 128], BF16, stack=st)
        att = k.sb("s_att", [128, 4, 128], BF16, stack=st)
        EE = k.sb("s_EE", [128, 4, 128], BF16, stack=st)
        cE = k.sb("s_cE", [128, 2, 4, 128], BF16, stack=st)
        xsd = k.sb("s_xsd", [128, 4, 64], BF16, stack=st)
        o32 = k.sb("s_o32", [128, 2, 512], stack=st)
        y1 = k.sb("s_y1", [128, 512], stack=st)
        sq = k.sb("s_sq", [128, 512], BF16, stack=st)
        rs = k.sb("s_rs", [128, 512], stack=st)
        nwc = k.sb("s_nwc", [128, 2], stack=st)
        k.dma("sp", Dall[:], W["ssd_D"][l:l + 1].partition_broadcast(128), writes=["s_Dall"], chan="c0")
        for h in range(4):
            k.ts(idD[:, h, :], C("ident"), Dall[:, h:h + 1], ALU.mult, reads=["cst", "s_Dall"], writes=["s_idD"])
        with nc.allow_non_contiguous_dma(reason="tiny"):
            k.dma("sp", nwc[:], W["ssd_norm_w"][l].rearrange("(c p) -> p c", p=128), writes=["s_nwc"], chan="c0")
        for n in range(NT):
            decay_tile(n, g, "s_g", "s_", cumt, tot)
        k.tt(wgt[:], tot[:], cumt[:], ALU.subtract, reads=["s_tot", "s_cumt"], writes=["s_wgt"])
        k.act(wgt[:], wgt[:], AF.Exp, reads=["s_wgt"], writes=["s_wgt"])
        k.tt(wgt[:], wgt[:], dt[:], ALU.mult, reads=["s_wgt", "s_dt"], writes=["s_wgt"])
        k.act(aex[:], tot[:], AF.Exp, reads=["s_tot"], writes=["s_aex"])
        k.tt(bias1[:], lndt[:], cumt[:], ALU.subtract, reads=["s_lndt", "s_cumt"], writes=["s_bias1"])
        xsds = [xsd, k.sb("s_xsd1", [128, 4, 64], BF16, stack=st)]

        def ssd_pass(d):
            xsd = xsds[d]
            xk = f"s_xsd{d}"
            order = list(range(NT)) if d == 0 else list(range(NT - 1, -1, -1))
            snap = Sf if d == 0 else Sb
            snk = "s_Sf" if d == 0 else "s_Sb"
            skey = f"s_S32_{d}"
            k.dma("sp", S32[:, d, :].rearrange("p (h v) -> p h v", v=64), sssd_d[l, d].rearrange("h n v -> n h v"), writes=[skey], chan="c0")
            for idx, n in enumerate(order):
                if idx > 0 and idx % 2 == 0:
                    k.ts(S32[:, d, :], S32[:, d, :], mfl[:, 0:1], ALU.mult, reads=[skey, "mfl"], writes=[skey])
                k.cp(snap[:, n, :], S32[:, d, :], reads=[skey], writes=[snk])
                k.tt(xsd[:], xtok[:, n, :].rearrange("p (h v) -> p h v", v=64),
                     wgt[:, n, d * 4:(d + 1) * 4].unsqueeze(2).to_broadcast([128, 4, 64]), ALU.mult,
                     reads=["s_xtok", "s_wgt"], writes=[xk])
                b2, bk2 = bank()
                for h in range(4):
                    gi = h // 2
                    k.mm(b2[:, h * 64:(h + 1) * 64], btok[:, n, gi * 128:(gi + 1) * 128], xsd[:, h, :], True, True,
                         reads=["s_btok", xk], writes=[bk2])
                k.tt(S32[:, d, :].rearrange("p (h v) -> p h v", v=64), S32[:, d, :].rearrange("p (h v) -> p h v", v=64),
                     aex[:, n, d * 4:(d + 1) * 4].unsqueeze(2).to_broadcast([128, 4, 64]), ALU.mult, reads=[skey, "s_aex"], writes=[skey])
                k.tt(S32[:, d, :], S32[:, d, :], b2[:, 0:256], ALU.add, reads=[skey, bk2], writes=[skey])
                if idx % 2 == 1:
                    si = (n // 2) % 2
                    k.cp(sst[si][:], S32[:, d, :], reads=[skey], writes=[f"s_sst{si}"])
                    with nc.allow_non_contiguous_dma(reason="state store"):
                        k.dma("sp", nsssd_d[n // 2, l, d].rearrange("h n v -> n h v"), sst[si][:].rearrange("p (h v) -> p h v", v=64),
                              reads=[f"s_sst{si}"], writes=["nsssd"], chan=f"s_sst{si}")
                yield
        gens = [ssd_pass(0), ssd_pass(1)]
        while gens:
            for gg in list(gens):
                try:
                    next(gg)
                except StopIteration:
                    gens.remove(gg)
        if "stop:state" in dbg:
            return
        PD = 2
        decms = [decm] + [k.sb(f"s_decm{i}", [128, 2, 4, 128], BF16, stack=st) for i in range(1, PD)]
        dec2s = [dec2] + [k.sb(f"s_dec2{i}", [128, 4, 128], BF16, stack=st) for i in range(1, PD)]
        atts = [att] + [k.sb(f"s_att{i}", [128, 4, 128], BF16, stack=st) for i in range(1, PD)]
        EEs = [EE] + [k.sb(f"s_EE{i}", [128, 4, 128], BF16, stack=st) for i in range(1, PD)]
        cEs = [cE] + [k.sb(f"s_cE{i}", [128, 2, 4, 128], BF16, stack=st) for i in range(1, PD)]
        tmpDs = [tmpD] + [k.sb(f"s_tmpD{i}", [128, 4, 128], stack=st) for i in range(1, PD)]

        def ssd_tile(item, slot):
            tb, t4 = item
            n = tb * 4 + t4
            tsl = slice(n * 128, (n + 1) * 128)
            decm, dec2, att, EE, cE, tmpD = decms[slot], dec2s[slot], atts[slot], EEs[slot], cEs[slot], tmpDs[slot]
            sx = f"_{slot}"
            for d in range(2):
                k.tt(GU[:, 0], g[:, n, d * 4:(d + 1) * 4].unsqueeze(2).to_broadcast([128, 4, 128]),
                     C("triU" if d == 0 else "triL").unsqueeze(1).to_broadcast([128, 4, 128]), ALU.mult,
                     reads=["s_g", "cst"], writes=["s_GU"])
                b, bk = bank()
                k.mm(b[:], C("ones"), GU[:, 0].rearrange("p h i -> p (h i)"), True, True, reads=["cst", "s_GU"], writes=[bk])
                yield
                k.tt(tmpD[:], b[:].rearrange("p (h i) -> p h i", i=128),
                     C("neg_f" if d == 0 else "neg_b").unsqueeze(1).to_broadcast([128, 4, 128]), ALU.add,
                     reads=[bk, "cst"], writes=["s_tmpD" + sx])
                k.act(EE[:], b[:].rearrange("p (h i) -> p h i", i=128), AF.Exp, reads=[bk, "s_tmpD" + sx], writes=["s_EE" + sx])
                yield
                for h in range(4):
                    k.act(decm[:, d, h, :], tmpD[:, h, :], AF.Exp, reads=["s_tmpD" + sx, "s_bias1"], writes=["s_decm" + sx],
                          bias=bias1[:, n, d * 4 + h:d * 4 + h + 1])
                k.tt(cE[:, d].rearrange("p (g h) i -> p g h i", h=2), cmT[:, :, tsl].unsqueeze(2).to_broadcast([128, 2, 2, 128]),
                     EE[:].rearrange("p (g h) i -> p g h i", h=2), ALU.mult, reads=["s_cmT", "s_EE" + sx], writes=["s_cE" + sx])
                yield
            k.tt(dec2[:], decm[:, 0], decm[:, 1], ALU.add, reads=["s_decm" + sx], writes=["s_dec2" + sx], eng="pool")
            b, bk = bank()
            for gi in range(2):
                k.mm(b[:, gi * 128:(gi + 1) * 128], bmT[:, gi, tsl], cmT[:, gi, tsl], True, True, reads=["s_bmT", "s_cmT"], writes=[bk])
            yield
            k.tt(att[:].rearrange("p (g h) i -> p g h i", h=2),
                 b[:, 0:256].rearrange("p (g i) -> p g i", i=128).unsqueeze(2).to_broadcast([128, 2, 2, 128]),
                 dec2[:].rearrange("p (g h) i -> p g h i", h=2), ALU.mult, reads=[bk, "s_dec2" + sx], writes=["s_att" + sx])
            k.tt(att[:], att[:], idD[:], ALU.add, reads=["s_att" + sx, "s_idD"], writes=["s_att" + sx])
            yield
            b2, bk2 = bank()
            for h in range(4):
                pb = (h % 2) * 64
                pr = h // 2
                o_ap = b2[pb:pb + 64, pr * 128:(pr + 1) * 128]
                k.mm(o_ap, xtok[:, n, h * 64:(h + 1) * 64], att[:, h, :], True, False, reads=["s_xtok", "s_att" + sx], writes=[bk2])
                k.mm(o_ap, Sf[:, n, h * 64:(h + 1) * 64], cE[:, 0, h, :], False, False, reads=["s_Sf", "s_cE" + sx], writes=[bk2])
                k.mm(o_ap, Sb[:, n, h * 64:(h + 1) * 64], cE[:, 1, h, :], False, True, reads=["s_Sb", "s_cE" + sx], writes=[bk2])
            yield
            k.cp(o32[:, :, t4 * 128:(t4 + 1) * 128], b2[:, 0:256].rearrange("p (pr i) -> p pr i", i=128), reads=[bk2], writes=[("s_o32", t4)])

        for tb in range(NB):
            run_pipelined(ssd_tile, [(tb, t4) for t4 in range(4)], PD)
            okeys = [("s_o32", j) for j in range(4)]
            sl = slice(tb * 512, (tb + 1) * 512)
            for pr in range(2):
                k.tt(y1[:], o32[:, pr, :], zT[:, pr, sl], ALU.mult, reads=okeys + ["s_zT"], writes=["s_y1"])
                k.act(sq[:], y1[:], AF.Square, reads=["s_y1"], writes=["s_sq"])
                b, bk = bank()
                k.mm(b[:], ONESB, sq[:], True, True, reads=["s_sq", "cstb"], writes=[bk])
                k.act(rs[:], b[:], AF.Ln, reads=[bk, "epsc"], writes=["s_rs"], bias=epsc[:, 0:1], scale=1.0 / 128)
                k.act(rs[:], rs[:], AF.Exp, reads=["s_rs"], writes=["s_rs"], scale=-0.5)
                k.stt(omix[:, pr, sl], y1[:], nwc[:, pr:pr + 1], rs[:], ALU.mult, ALU.mult,
                      reads=["s_y1", "s_nwc", "s_rs"], writes=["s_omix"])
        if "ssd_o" in dbg:
            dbg_d["ssd_o"] = k.dram("dbg_ssd_o", [128, 2, T], BF16, kind="ExternalOutput")
            k.dma("sp", dbg_d["ssd_o"], omix[:], reads=["s_omix"], writes=["dbg_ssd_o"], chan="dbg")
            out_keys.append("dbg_ssd_o")

    GDN0 = 1792

    def gdn_mixer(l, st):
        omix = k.sb("g_omix", [128, 2, T], BF16, stack=st)
        qT = k.sb("g_qT", [128, 2, T], BF16, stack=st)
        kT = k.sb("g_kT", [128, 2, T], BF16, stack=st)
        zT = k.sb("g_zT", [128, 2, T], BF16, stack=st)
        ktok = k.sb("g_ktok", [128, NT, 256], BF16, stack=st)
        vtok = k.sb("g_vtok", [128, NT, 256], BF16, stack=st)
        g = k.sb("g_g", [128, NT, 8], stack=st)
        beta = k.sb("g_beta", [128, NT, 8], stack=st)
        one_c = k.sb("g_one", [128, 1], stack=st)
        k.op("dve", lambda e: e.memset(one_c[:], 1.0), writes=["g_one"])
        with k.scope() as st2:
            alloc_stg(st2)
            wt, wkey = k.sb("g_w", [128, 8, 1040], BF16, stack=st2), "g_w"
            vT = k.sb("g_vT", [128, 2, T], BF16, stack=st2)
            pre = k.sb("cv_pre", [128, T + 2], stack=st2)
            acc = k.sb("cv_acc", [128, T], stack=st2)
            xc = k.sb("g_xc", [128, T], BF16, stack=st2)
            cw = k.sb("g_cw", [128, 6, 6], stack=st2)
            sq = k.sb("g_sq", [128, 512], BF16, stack=st2)
            rs = acc[:, 0:512]
            dtb = k.sb("g_dtb", [128, 8], stack=st2)
            nA = k.sb("g_nA", [128, 8], stack=st2)
            load_w(wt, wkey, W["w_in"][l, :, GDN0:GDN0 + 1040], 8, 1040)
            load_conv_params(cw, "g_cw", "gdn_conv_w", None, l, 6, st2)
            k.op("dve", lambda e: e.memset(pre[:, 0:1], 0.0), writes=["cv_pre"])
            k.op("dve", lambda e: e.memset(pre[:, T + 1:T + 2], 0.0), writes=["cv_pre"])
            k.dma("sp", dtb[:], W["gdn_dt_bias"][l:l + 1].rearrange("o d h -> o (d h)").partition_broadcast(128), writes=["g_dtb"], chan="c0")
            k.dma("sp", nA[:], W["gdn_A_log"][l:l + 1].rearrange("o d h -> o (d h)").partition_broadcast(128), writes=["g_nA"], chan="c0")
            k.act(nA[:], nA[:], AF.Exp, reads=["g_nA"], writes=["g_nA"])
            k.ts(nA[:], nA[:], -1.0, ALU.mult, reads=["g_nA"], writes=["g_nA"])
            for ci in range(6):
                if ci < 4:
                    conv_chunk(wt, wkey, ci * 128, cw, "g_cw", ci, pre, acc, xc[:], "g_xc", AF.Silu)
                    dstT, dkey = (qT, "g_qT") if ci < 2 else (kT, "g_kT")
                    scl = 0.125 if ci < 2 else 1.0
                    for tb in range(NB):
                        sl = slice(tb * 512, (tb + 1) * 512)
                        k.act(sq[:], xc[:, sl], AF.Square, reads=["g_xc"], writes=["g_sq"])
                        b, bk = bank()
                        k.mm(b[:], BLKB, sq[:], True, True, reads=["g_sq", "cstb"], writes=[bk])
                        k.act(rs, b[:], AF.Ln, reads=[bk, "epsc"], writes=["cv_acc"], bias=epsc[:, 0:1])
                        k.act(rs, rs, AF.Exp, reads=["cv_acc"], writes=["cv_acc"], scale=-0.5)
                        k.stt(dstT[:, ci % 2, sl], xc[:, sl], scl, rs, ALU.mult, ALU.mult, reads=["g_xc", "cv_acc"], writes=[dkey])
                else:
                    conv_chunk(wt, wkey, ci * 128, cw, "g_cw", ci, pre, acc, vT[:, ci - 4, :], "g_vT", AF.Silu)
            for pr in range(2):
                proj_fm(wt, wkey, 768 + pr * 128, 128,
                        lambda tb, b, bk, pr=pr: k.act(zT[:, pr, tb * 512:(tb + 1) * 512], b[:], AF.Silu, reads=[bk], writes=["g_zT"]))
            to_tok(kT, "g_kT", 2, ktok, "g_ktok")
            to_tok(vT, "g_vT", 2, vtok, "g_vtok")
            b, bk = bank()
            for t in range(NT):
                for kc in range(8):
                    k.mm(b[:, t * 16:(t + 1) * 16], hT[:, kc, t * 128:(t + 1) * 128], wt[:, kc, 1024:1040], kc == 0, kc == 7,
                         reads=[wkey, ("hT", t // 4)], writes=[bk])
            bv = b[:, 0:NT * 16].rearrange("p (t e) -> p t e", e=16)
            k.tt(g[:], bv[:, :, 0:8], dtb[:].unsqueeze(1).to_broadcast([128, NT, 8]), ALU.add, reads=[bk, "g_dtb"], writes=["g_g"])
            k.act(g[:], g[:], AF.Exp, reads=["g_g"], writes=["g_g"])
            k.act(g[:], g[:], AF.Ln, reads=["g_g", "g_one"], writes=["g_g"], bias=one_c[:, 0:1])
            k.tt(g[:], g[:], nA[:].unsqueeze(1).to_broadcast([128, NT, 8]), ALU.mult, reads=["g_g", "g_nA"], writes=["g_g"])
            k.act(beta[:], bv[:, :, 8:16], AF.Sigmoid, reads=[bk], writes=["g_beta"])
        if "stop:proj" in dbg:
            return
        with k.scope() as st3:
            gdn_scan(l, st3, omix, qT, kT, zT, ktok, vtok, g, beta)
        if "gdn_o" in dbg:
            dbg_d["gdn_o"] = k.dram("dbg_gdn_o", [128, 2, T], BF16, kind="ExternalOutput")
            k.dma("sp", dbg_d["gdn_o"], omix[:], reads=["g_omix"], writes=["dbg_gdn_o"], chan="dbg2")
            out_keys.append("dbg_gdn_o")
        out_proj(l, omix, "g_omix", 2, 512, 16)

    def gdn_scan(l, st, omix, qT, kT, zT, ktok, vtok, g, beta):
        c2 = k.sb("g_c2", [128, 8, 128], stack=st)
        k.dma("sp", c2[:], cst2_d, writes=["g_c2"], chan="c0")
        C2N = {"triU64": 0, "triL64": 1, "sel0": 2, "sel1": 3, "negT_f": 4, "negT_b": 5, "negS_f": 6, "negS_b": 7}

        def C2(nm):
            return c2[:, C2N[nm], :]
        cumt = k.sb("g_cumt", [128, NT, 8], stack=st)
        ncumt = k.sb("g_ncumt", [128, NT, 8], stack=st)
        town = k.sb("g_town", [128, NT, 8], stack=st)
        aexp = k.sb("g_aexp", [128, NT, 2, 8], stack=st)
        Aall = k.sb("g_Aall", [128, NT, 2, 4], stack=st)
        bec = k.sb("g_bec", [128, NT, 8], stack=st)
        nbeta = k.sb("g_nbeta", [128, NT, 8], stack=st)
        dkw = k.sb("g_dkw", [128, NT, 8], stack=st)
        GU = k.sb("g_GU", [128, 4, 128], stack=st)
        tmpD = k.sb("g_tmpD", [128, 4, 128], stack=st)
        decT = k.sb("g_decT", [128, 4, 128], BF16, stack=st)
        decS = k.sb("g_decS", [128, 4, 128], BF16, stack=st)
        EE = k.sb("g_EE", [128, 4, 128], BF16, stack=st)
        GDT = F32 if GDN_FP32 else BF16
        XY = [[k.sb(f"g_X{i}", [128, 4, 128], GDT, stack=st), k.sb(f"g_Y{i}", [128, 4, 128], GDT, stack=st)] for i in range(2)]
        TT = k.sb("g_TT", [128, 4, 128], GDT, stack=st)
        att = k.sb("g_att", [128, 4, 128], BF16, stack=st)
        qE = k.sb("g_qE", [128, 2, 128], BF16, stack=st)
        rk = k.sb("g_rk", [128, 4, 64], GDT, stack=st)
        rv = k.sb("g_rv", [128, 4, 64], GDT, stack=st)
        kd = k.sb("g_kd", [128, 4, 64], BF16, stack=st)
        WT = k.sb("g_WT", [128, 2, 128], BF16, stack=st)
        U32 = k.sb("g_U32", [128, 4, 64], stack=st)
        vnew = k.sb("g_vnew", [128, 4, 64], BF16, stack=st)
        S32 = k.sb("g_S32", [128, 2, 128], stack=st)
        Sbf = k.sb("g_Sbf", [128, 2, 128], BF16, stack=st)
        hflat = hT[:].rearrange("p c t -> p (c t)")
        hoff = [0]

        def carve(shape, dt):
            nel = int(np.prod(shape[1:]))
            nb16 = nel * (2 if dt == F32 else 1)
            ap = hflat[:, hoff[0]:hoff[0] + nb16]
            hoff[0] += nb16
            if dt == F32:
                ap = ap.bitcast(F32)
            if len(shape) == 3:
                return ap.rearrange("p (a b) -> p a b", b=shape[2])
            return ap

        class _V:
            def __init__(self, ap):
                self.ap = ap

            def __getitem__(self, idx):
                return self.ap[idx]
        B1 = [_V(carve([128, 4, 128], F32)), _V(carve([128, 4, 128], F32)),
              _V(carve([128, 4, 128], BF16)), _V(carve([128, 4, 128], BF16)), _V(carve([128, 4, 128], BF16)), _V(carve([128, 4, 128], BF16)),
              _V(carve([128, 2, 128], BF16)),
              [[_V(carve([128, 4, 128], GDT)), _V(carve([128, 4, 128], GDT))] for _ in range(2)],
              _V(carve([128, 4, 128], GDT)), _V(carve([128, 4, 64], GDT)), _V(carve([128, 4, 64], GDT)), _V(carve([128, 4, 64], BF16)),
              _V(carve([128, 2, 128], BF16)), _V(carve([128, 4, 64], F32)), _V(carve([128, 4, 64], BF16)), _V(carve([128, 2, 128], BF16))]
        assert hoff[0] <= 8 * T
        BUFS = [[GU, tmpD, decT, decS, EE, att, qE, XY, TT, rk, rv, kd, WT, U32, vnew, Sbf], B1]
        gst = [k.sb(f"g_sst{i}", [128, 128], stack=st) for i in range(2)]
        oacc = k.sb("g_oacc", [128, 2, T], stack=st)
        sq = k.sb("g_sq2", [128, 512], BF16, stack=st)
        rs = GU[:].rearrange("p h i -> p (h i)")
        t1 = tmpD[:].rearrange("p h i -> p (h i)")
        w8 = k.sb("g_w8", [128, 2], stack=st)
        with nc.allow_non_contiguous_dma(reason="tiny"):
            k.dma("sp", w8[:], W["gdn_norm_w"][l].rearrange("(c p) -> p c", p=128), writes=["g_w8"], chan="c0")
        k.ts(w8[:], w8[:], 8.0, ALU.mult, reads=["g_w8"], writes=["g_w8"])
        for n in range(NT):
            b, bk = bank()
            k.mm(b[:, 0:4], C2("triU64"), g[:, n, 0:4], True, True, reads=["g_c2", "g_g"], writes=[bk])
            k.mm(b[:, 4:8], C2("triL64"), g[:, n, 4:8], True, True, reads=["g_c2", "g_g"], writes=[bk])
            k.mm(b[:, 8:16], C("blk64"), g[:, n, :], True, True, reads=["cst", "g_g"], writes=[bk])
            k.mm(b[:, 16:24], C2("sel0"), g[:, n, :], True, True, reads=["g_c2", "g_g"], writes=[bk])
            k.mm(b[:, 24:32], C2("sel1"), g[:, n, :], True, True, reads=["g_c2", "g_g"], writes=[bk])
            k.cp(cumt[:, n, :], b[:, 0:8], reads=[bk], writes=["g_cumt"], eng="dve")
            k.cp(town[:, n, :], b[:, 8:16], reads=[bk], writes=["g_town"], eng="dve")
            k.act(aexp[:, n, :, :], b[:, 16:32].rearrange("p (c e) -> p c e", e=8), AF.Exp, reads=[bk, "g_town"], writes=["g_aexp"])
        k.ts(ncumt[:], cumt[:], -1.0, ALU.mult, reads=["g_cumt"], writes=["g_ncumt"])
        k.act(bec[:], cumt[:], AF.Exp, reads=["g_cumt"], writes=["g_bec"])
        k.tt(bec[:], bec[:], beta[:], ALU.mult, reads=["g_bec", "g_beta"], writes=["g_bec"])
        k.ts(nbeta[:], beta[:], -1.0, ALU.mult, reads=["g_beta"], writes=["g_nbeta"])
        k.tt(dkw[:], town[:], cumt[:], ALU.subtract, reads=["g_town", "g_cumt"], writes=["g_dkw"])
        k.act(dkw[:], dkw[:], AF.Exp, reads=["g_dkw"], writes=["g_dkw"])
        for c in range(2):
            for d in range(2):
                for hf in range(2):
                    k.cp(Aall[hf * 64:(hf + 1) * 64, :, c, d * 2:d * 2 + 2],
                         aexp[hf * 64:(hf + 1) * 64, :, c, d * 4:d * 4 + 4].rearrange("p n (pr hf) -> p n pr hf", hf=2)[:, :, :, hf],
                         reads=["g_aexp"], writes=["g_Aall"], eng="dve")
        if "gst:0" in dbg:
            return
        def dir_pass(d, B):
            GU, tmpD, decT, decS, EE, att, qE, XY, TT, rk, rv, kd, WT, U32, vnew, Sbf = B
            sfx = f"_{d}"
            order = list(range(NT)) if d == 0 else list(range(NT - 1, -1, -1))
            skey = "g_S32" + sfx
            with nc.allow_non_contiguous_dma(reason="state load"):
                k.dma("sp", S32[:, d, :].rearrange("p (pr v) -> p pr v", v=64),
                      sgdn_d[l, d].rearrange("(pr hf) kk v -> (hf kk) pr v", hf=2), writes=[skey], chan="c0")
            tri = "triU64" if d == 0 else "triL64"
            for idx, n in enumerate(order):
                tsl = slice(n * 128, (n + 1) * 128)
                if idx > 0 and idx % 2 == 0:
                    k.ts(S32[:, d, :], S32[:, d, :], mfl[:, 0:1], ALU.mult, reads=[skey, "mfl"], writes=[skey])
                k.tt(GU[:], g[:, n, d * 4:(d + 1) * 4].unsqueeze(2).to_broadcast([128, 4, 128]),
                     C2(tri).unsqueeze(1).to_broadcast([128, 4, 128]), ALU.mult, reads=["g_g", "g_c2"], writes=["g_GU" + sfx])
                cb, cbk = bank()
                k.mm(cb[:], C("ones"), GU[:].rearrange("p h i -> p (h i)"), True, True, reads=["cst", "g_GU" + sfx], writes=[cbk])
                cbv = cb[:].rearrange("p (h i) -> p h i", i=128)
                k.tt(tmpD[:], cbv, C2("negT_f" if d == 0 else "negT_b").unsqueeze(1).to_broadcast([128, 4, 128]), ALU.add,
                     reads=[cbk, "g_c2"], writes=["g_tmpD" + sfx])
                for h in range(4):
                    k.act(decT[:, h, :], tmpD[:, h, :], AF.Exp, reads=["g_tmpD" + sfx, "g_ncumt"], writes=["g_decT" + sfx],
                          bias=ncumt[:, n, d * 4 + h:d * 4 + h + 1])
                k.tt(tmpD[:], cbv, C2("negS_f" if d == 0 else "negS_b").unsqueeze(1).to_broadcast([128, 4, 128]), ALU.subtract,
                     reads=[cbk, "g_c2", "g_decT" + sfx], writes=["g_tmpD" + sfx])
                for h in range(4):
                    k.act(decS[:, h, :], tmpD[:, h, :], AF.Exp, reads=["g_tmpD" + sfx, "g_cumt"], writes=["g_decS" + sfx],
                          bias=cumt[:, n, d * 4 + h:d * 4 + h + 1], scale=-1.0)
                k.act(EE[:], cbv, AF.Exp, reads=[cbk], writes=["g_EE" + sfx])
                for pr in range(2):
                    for hf in range(2):
                        ps_ = slice(hf * 64, (hf + 1) * 64)
                        k.tt(qE[ps_, pr, :], qT[ps_, pr, tsl], EE[ps_, pr * 2 + hf, :], ALU.mult, reads=["g_qT", "g_EE" + sfx], writes=["g_qE" + sfx])
                if "gst:a" in dbg:
                    continue
                yield
                kkb = [bank(), bank()]
                qkb = [bank(), bank()]
                for h in (0, 2, 1, 3):
                    pb = (h % 2) * 64
                    pr = h // 2
                    k.mm(kkb[h % 2][0][:, pr * 128:(pr + 1) * 128], kT[pb:pb + 64, pr, tsl], kT[pb:pb + 64, pr, tsl], True, True,
                         reads=["g_kT"], writes=[kkb[h % 2][1]])
                    k.mm(qkb[h % 2][0][:, pr * 128:(pr + 1) * 128], kT[pb:pb + 64, pr, tsl], qT[pb:pb + 64, pr, tsl], True, True,
                         reads=["g_kT", "g_qT"], writes=[qkb[h % 2][1]])
                X0, Y0 = XY[0]
                for h in range(4):
                    pr = h // 2
                    k.stt(X0[:, h, :], kkb[h % 2][0][:, pr * 128:(pr + 1) * 128], nbeta[:, n, d * 4 + h:d * 4 + h + 1], decS[:, h, :],
                          ALU.mult, ALU.mult, reads=[kkb[h % 2][1], "g_nbeta", "g_decS" + sfx], writes=["g_X0" + sfx])
                for hf in range(2):
                    k.tt(att[:, hf::2, :], qkb[hf][0][:, 0:256].rearrange("p (h i) -> p h i", i=128), decT[:, hf::2, :], ALU.mult,
                         reads=[qkb[hf][1], "g_decT" + sfx], writes=["g_att" + sfx])
                if "gst:b" in dbg:
                    continue
                yield
                yb, ybk = bank()
                if GDN_FP32:
                    for h in range(4):
                        k.tr(yb[:, h * 128:(h + 1) * 128], X0[:, h, :], C("ident"), reads=["g_X0" + sfx, "cst"], writes=[ybk])
                    ybv = yb[:].rearrange("p (h i) -> p h i", i=128)
                else:
                    ybt = yb[:].bitcast(BF16)
                    for h in range(4):
                        k.tr(ybt[:, h * 128:(h + 1) * 128], X0[:, h, :], IDB, reads=["g_X0" + sfx, "cstb"], writes=[ybk])
                    ybv = ybt[:, 0:512].rearrange("p (h i) -> p h i", i=128)
                if "gst:c1" in dbg:
                    continue
                k.cp(Y0[:], ybv, reads=[ybk], writes=["g_Y0" + sfx])
                if "gst:c2" in dbg:
                    continue
                k.tt(TT[:], Y0[:], C("ident").unsqueeze(1).to_broadcast([128, 4, 128]), ALU.add, reads=["g_Y0" + sfx, "cst"], writes=["g_TT" + sfx])
                if "gst:c" in dbg:
                    continue
                cur = 0
                for lev in range(5):
                    Xp, Yp = XY[cur]
                    Xn, Yn = XY[1 - cur]
                    xk, yk, xnk, ynk = f"g_X{cur}" + sfx, f"g_Y{cur}" + sfx, f"g_X{1 - cur}" + sfx, f"g_Y{1 - cur}" + sfx
                    bx, bxk = bank()
                    by, byk = bank()
                    for h in range(4):
                        k.mm(bx[:, h * 128:(h + 1) * 128], Yp[:, h, :], Xp[:, h, :], True, True, reads=[xk, yk], writes=[bxk])
                    for h in range(4):
                        k.mm(by[:, h * 128:(h + 1) * 128], Xp[:, h, :], Yp[:, h, :], True, True, reads=[xk, yk], writes=[byk])
                    k.cp(Xn[:], bx[:].rearrange("p (h i) -> p h i", i=128), reads=[bxk], writes=[xnk])
                    k.cp(Yn[:], by[:].rearrange("p (h i) -> p h i", i=128), reads=[byk], writes=[ynk], eng="dve")
                    bt_, btk = bank()
                    for h in range(4):
                        k.mm(bt_[:, h * 128:(h + 1) * 128], Xn[:, h, :], TT[:, h, :], True, True, reads=[xnk, "g_TT" + sfx], writes=[btk])
                    k.tt(TT[:], TT[:], bt_[:].rearrange("p (h i) -> p h i", i=128), ALU.add, reads=["g_TT" + sfx, btk], writes=["g_TT" + sfx])
                    cur = 1 - cur
                    yield
                if "gst:d" in dbg:
                    continue
                kv = ktok[:, n, :].rearrange("p (h f) -> p h f", f=64)
                vv = vtok[:, n, :].rearrange("p (h f) -> p h f", f=64)
                bsl = slice(d * 4, d * 4 + 4)
                k.tt(rk[:], kv, bec[:, n, bsl].unsqueeze(2).to_broadcast([128, 4, 64]), ALU.mult, reads=["g_ktok", "g_bec"], writes=["g_rk" + sfx])
                k.tt(rv[:], vv, beta[:, n, bsl].unsqueeze(2).to_broadcast([128, 4, 64]), ALU.mult, reads=["g_vtok", "g_beta"], writes=["g_rv" + sfx])
                k.tt(kd[:], kv, dkw[:, n, bsl].unsqueeze(2).to_broadcast([128, 4, 64]), ALU.mult, reads=["g_ktok", "g_dkw"], writes=["g_kd" + sfx])
                bw, bwk = bank()
                bu, buk = bank()
                for h in range(4):
                    pb = (h % 2) * 64
                    pr = h // 2
                    k.mm(bw[pb:pb + 64, pr * 128:(pr + 1) * 128], rk[:, h, :], TT[:, h, :], True, True, reads=["g_rk" + sfx, "g_TT" + sfx], writes=[bwk])
                    k.mm(bu[:, h * 64:(h + 1) * 64], TT[:, h, :], rv[:, h, :], True, True, reads=["g_rv" + sfx, "g_TT" + sfx], writes=[buk])
                k.cp(WT[:], bw[:, 0:256].rearrange("p (pr i) -> p pr i", i=128), reads=[bwk], writes=["g_WT" + sfx])
                k.cp(U32[:], bu[:, 0:256].rearrange("p (h v) -> p h v", v=64), reads=[buk], writes=["g_U32" + sfx], eng="dve")
                if "gst:e" in dbg:
                    continue
                yield
                for c in ((0, 1) if d == 0 else (1, 0)):
                    cs = slice(c * 64, (c + 1) * 64)
                    k.cp(Sbf[:, c, :], S32[:, d, :], reads=[skey], writes=[f"g_Sbf{c}" + sfx])
                    vb = [bank(), bank()]
                    for h in (0, 2, 1, 3):
                        pb = (h % 2) * 64
                        pr = h // 2
                        k.mm(vb[h % 2][0][cs, pr * 64:(pr + 1) * 64], WT[pb:pb + 64, pr, c * 64:(c + 1) * 64],
                             Sbf[pb:pb + 64, c, pr * 64:(pr + 1) * 64], True, True, reads=["g_WT" + sfx, f"g_Sbf{c}" + sfx], writes=[vb[h % 2][1]])
                    for hf in range(2):
                        k.tt(vnew[cs, hf::2, :], U32[cs, hf::2, :], vb[hf][0][cs, 0:128].rearrange("p (pr v) -> p pr v", v=64), ALU.subtract,
                             reads=["g_U32" + sfx, vb[hf][1]], writes=["g_vnew" + sfx])
                    bs_, bsk = bank()
                    for h in range(4):
                        pb = (h % 2) * 64
                        pr = h // 2
                        k.mm(bs_[pb:pb + 64, pr * 64:(pr + 1) * 64], kd[cs, h, :], vnew[cs, h, :], True, True,
                             reads=["g_kd" + sfx, "g_vnew" + sfx], writes=[bsk])
                    k.tt(S32[:, d, :].rearrange("p (pr v) -> p pr v", v=64), S32[:, d, :].rearrange("p (pr v) -> p pr v", v=64),
                         Aall[:, n, c, d * 2:d * 2 + 2].unsqueeze(2).to_broadcast([128, 2, 64]), ALU.mult, reads=[skey, "g_Aall"], writes=[skey])
                    k.tt(S32[:, d, :], S32[:, d, :], bs_[:, 0:128], ALU.add, reads=[skey, bsk], writes=[skey])
                    yield
                if "gst:f" in dbg:
                    continue
                yield
                ob = [bank(), bank()]
                for h in (0, 2, 1, 3):
                    pb = (h % 2) * 64
                    pr = h // 2
                    o_ap = ob[h % 2][0][pb:pb + 64, pr * 128:(pr + 1) * 128]
                    k.mm(o_ap, vnew[:, h, :], att[:, h, :], True, False, reads=["g_vnew" + sfx, "g_att" + sfx], writes=[ob[h % 2][1]])
                    for c in range(2):
                        k.mm(ob[h % 2][0][pb:pb + 64, pr * 128 + c * 64:pr * 128 + (c + 1) * 64],
                             Sbf[pb:pb + 64, c, pr * 64:(pr + 1) * 64], qE[pb:pb + 64, pr, c * 64:(c + 1) * 64], False, c == 1,
                             reads=[f"g_Sbf{c}" + sfx, "g_qE" + sfx], writes=[ob[h % 2][1]])
                for hf in range(2):
                    ps_ = slice(hf * 64, (hf + 1) * 64)
                    src = ob[hf][0][ps_, 0:256].rearrange("p (pr i) -> p pr i", i=128)
                    k.tt(oacc[ps_, :, tsl], oacc[ps_, :, tsl], src, ALU.add, reads=[ob[hf][1], ("g_oacc", n)], writes=[("g_oacc", n)])
                if idx % 2 == 1:
                    si = (n // 2) % 2
                    k.cp(gst[si][:], S32[:, d, :], reads=[skey], writes=[f"g_sst{si}"])
                    with nc.allow_non_contiguous_dma(reason="state store"):
                        for pr in range(2):
                            k.dma("sp", nsgdn_d[n // 2, l, d, pr * 2:pr * 2 + 2].rearrange("hf kk v -> (hf kk) v"),
                                  gst[si][:, pr * 64:(pr + 1) * 64], reads=[f"g_sst{si}"], writes=["nsgdn"], chan=f"g_sst{si}")

        for n in range(NT):
            k.op("pool", lambda e, n=n: e.memset(oacc[:, :, n * 128:(n + 1) * 128], 0.0), writes=[("g_oacc", n)])
        gens = [dir_pass(0, BUFS[0]), dir_pass(1, BUFS[1])]
        while gens:
            for gg in list(gens):
                try:
                    next(gg)
                except StopIteration:
                    gens.remove(gg)
        for tb in range(NB):
            sl = slice(tb * 512, (tb + 1) * 512)
            for pr in range(2):
                okeys = [("g_oacc", tb * 4 + j) for j in range(4)]
                k.act(sq[:], oacc[:, pr, sl], AF.Square, reads=okeys, writes=["g_sq2"])
                b, bk = bank()
                k.mm(b[:], BLKB, sq[:], True, True, reads=["g_sq2", "cstb"], writes=[bk])
                k.act(rs, b[:], AF.Ln, reads=[bk, "epsc"], writes=["g_GU_0"], bias=epsc[:, 1:2], scale=1.0)
                k.act(rs, rs, AF.Exp, reads=["g_GU_0"], writes=["g_GU_0"], scale=-0.5)
                k.tt(t1, oacc[:, pr, sl], rs, ALU.mult, reads=okeys + ["g_GU_0"], writes=["g_tmpD_0"])
                k.stt(omix[:, pr, sl], t1, w8[:, pr:pr + 1], zT[:, pr, sl], ALU.mult, ALU.mult,
                      reads=["g_tmpD_0", "g_w8", "g_zT"], writes=["g_omix"])

    HY0 = 1024
    I32 = mybir.dt.int32

    def hy_mixer(l, st):
        omix = k.sb("h_omix", [128, 2, T], BF16, stack=st)
        x0T = k.sb("h_x0T", [128, 2, T], BF16, stack=st)
        uT = k.sb("h_uT", [128, 2, T], BF16, stack=st)
        utok = k.sb("h_utok", [128, NT, 256], BF16, stack=st)
        Atok = k.sb("h_Atok", [128, NT, 256], BF16, stack=st)
        Btok = k.sb("h_Btok", [128, NT, 256], BF16, stack=st)
        with k.scope() as st2:
            alloc_stg(st2)
            wt, wkey = k.sb("h_w", [128, 8, 768], BF16, stack=st2), "h_w"
            pre = k.sb("cv_pre", [128, T + 2], stack=st2)
            acc = k.sb("cv_acc", [128, T], stack=st2)
            x1T = k.sb("h_x1T", [128, 2, T], BF16, stack=st2)
            cw = k.sb("h_cw", [128, 6, 6], stack=st2)
            load_w(wt, wkey, W["w_in"][l, :, HY0:HY0 + 768], 8, 768)
            load_conv_params(cw, "h_cw", "hy_conv_w", "hy_conv_b", l, 6, st2)
            k.op("dve", lambda e: e.memset(pre[:, 0:1], 0.0), writes=["cv_pre"])
            k.op("dve", lambda e: e.memset(pre[:, T + 1:T + 2], 0.0), writes=["cv_pre"])
            for ci in range(6):
                dst, dkey = [(x0T, "h_x0T"), (x1T, "h_x1T"), (uT, "h_uT")][ci // 2]
                conv_chunk(wt, wkey, ci * 128, cw, "h_cw", ci, pre, acc, dst[:, ci % 2, :], dkey, None)
            k.tt(uT[:], uT[:], x1T[:], ALU.mult, reads=["h_uT", "h_x1T"], writes=["h_uT"])
            to_tok(uT, "h_uT", 2, utok, "h_utok")
        with k.scope() as st2:
            w1 = k.sb("h_w1", [33, 64], stack=st2)
            w2 = k.sb("h_w2", [64, 64], stack=st2)
            w3 = k.sb("h_w3", [64, 512], stack=st2)
            fcol = k.sb("h_fcol", [64, 5], stack=st2)
            zb = k.sb("h_zb", [33, 512], stack=st2)
            arg = k.sb("h_arg", [64, 512], stack=st2)
            ki = k.sb("h_ki", [64, 512], I32, stack=st2)
            h1 = k.sb("h_h1", [64, 512], stack=st2)
            h2 = k.sb("h_h2", [64, 512], stack=st2)
            dl = k.sb("h_dl", [128, 256], stack=st2)
            dec = k.sb("h_dec", [128, 256], stack=st2)
            tf = k.sb("h_tf", [128, 256], stack=st2)
            tb_ = k.sb("h_tb", [128, 256], stack=st2)
            tcol = k.sb("h_tcol", [128, 2, NT], stack=st2)
            k.dma("sp", w1[:], W["hy_w1"][l], writes=["h_w1"], chan="c0")
            k.dma("sp", w2[:], W["hy_w2"][l], writes=["h_w2"], chan="c0")
            k.dma("sp", w3[:], W["hy_w3"][l], writes=["h_w3"], chan="c0")
            with nc.allow_non_contiguous_dma(reason="tiny"):
                for j, nm in enumerate(["hy_freq", "hy_b1", "hy_b2"]):
                    k.dma("sp", fcol[:, j:j + 1], W[nm][l:l + 1].rearrange("o f -> f o"), writes=["h_fcol"], chan="c0")
            k.tt(fcol[:, 3:4], fcol[:, 1:2], fcol[:, 0:1], ALU.mult, reads=["h_fcol"], writes=["h_fcol"])
            k.tt(fcol[:, 4:5], fcol[:, 2:3], fcol[:, 0:1], ALU.mult, reads=["h_fcol"], writes=["h_fcol"])
            k.dma("sp", dl[:], hyd_d.partition_broadcast(128), writes=["h_dl"], chan="c0")
            k.dma("sp", tcol[:], hyt_d, writes=["h_tcol"], chan="c0")

            def sin_layer(dst, dkey, wmat, wk, src, skey, nk, bcol):
                b, bk = bank()
                k.mm(b[0:64, :], wmat[0:nk, :], src[0:nk, :], True, True, reads=[wk, skey], writes=[bk])
                k.act(arg[:], b[0:64, :], AF.Identity, reads=[bk, "h_fcol"], writes=["h_arg"], bias=fcol[:, bcol:bcol + 1], scale=fcol[:, 0:1])
                k.ts(ki[:], arg[:], 1.0 / (2 * math.pi), ALU.mult, reads=["h_arg"], writes=["h_ki"])
                k.stt(arg[:], ki[:], -2 * math.pi, arg[:], ALU.mult, ALU.add, reads=["h_ki", "h_arg"], writes=["h_arg"])
                k.act(dst[:], arg[:], AF.Sin, reads=["h_arg"], writes=[dkey])

            for tb in range(NB):
                k.dma("sp", zb[:], hyz_d[:, tb * 512:(tb + 1) * 512], writes=["h_zb"], chan="c0")
                sin_layer(h1, "h_h1", w1, "h_w1", zb, "h_zb", 33, 3)
                sin_layer(h2, "h_h2", w2, "h_w2", h1, "h_h1", 64, 4)
                for t4 in range(4):
                    n = tb * 4 + t4
                    b, bk = bank()
                    k.mm(b[:], h2[:, t4 * 128:(t4 + 1) * 128], w3[:], True, True, reads=["h_h2", "h_w3"], writes=[bk])
                    k.act(dec[:], dl[:], AF.Exp, reads=["h_dl", "h_tcol"], writes=["h_dec"], scale=tcol[:, 0, n:n + 1])
                    k.tt(tf[:], b[:, 0:256], dec[:], ALU.mult, reads=[bk, "h_dec"], writes=["h_tf"])
                    k.stt(tb_[:], b[:, 256:512], tcol[:, 1, n:n + 1], dec[:], ALU.mult, ALU.mult, reads=[bk, "h_dec", "h_tcol"], writes=["h_tb"])
                    k.tt(Atok[:, n, :], tf[:], tb_[:], ALU.add, reads=["h_tf", "h_tb"], writes=["h_Atok"])
                    k.tt(Btok[:, n, :], tf[:], tb_[:], ALU.subtract, reads=["h_tf", "h_tb"], writes=["h_Btok"], eng="pool")
        if "stop:proj" in dbg:
            return
        with k.scope() as st3:
            Ysb = k.sb("h_Ysb", [128, 32, 256], BF16, stack=st3)
            with k.scope() as st2:
                fw = [k.sb(f"h_fw{i}", [128, 2, 16, 128], BF16, stack=st2) for i in range(2)]
                sm = k.sb("h_sm", [128, 2, 16], stack=st2)
                sgn = k.sb("h_sgn", [128, NT], BF16, stack=st2)
                tn = k.sb("h_tn", [1, 256], stack=st2)
                Tre = k.sb("h_Tre", [128, 256], stack=st2)
                Tim = k.sb("h_Tim", [128, 256], stack=st2)
                Dd = k.sb("h_Dd", [128, 256], stack=st2)
                t1 = k.sb("h_t1", [128, 256], stack=st2)
                t2 = k.sb("h_t2", [128, 256], stack=st2)
                t3 = k.sb("h_t3", [128, 256], stack=st2)
                t4_ = k.sb("h_t4", [128, 256], stack=st2)
                k.dma("sp", sm[:], hys_d, writes=["h_sm"], chan="c0")
                k.dma("sp", sgn[:], hyg_d, writes=["h_sgn"], chan="c0")
                b, bk = bank()
                for tc in range(NT):
                    k.mm(b[0:1, 0:256], sgn[:, tc:tc + 1], Atok[:, tc, :], tc == 0, tc == NT - 1, reads=["h_sgn", "h_Atok"], writes=[bk])
                k.cp(tn[:], b[0:1, 0:256], reads=[bk], writes=["h_tn"], eng="dve")
                for fc in range(16):
                    f_, fk = fw[fc % 2], f"h_fw{fc % 2}"
                    k.dma("sp", f_[:, 0], ffwd_d[fc], writes=[fk], chan=fk)
                    k.dma("sp", f_[:, 1], ffwd_d[16 + fc], reads=[], writes=[fk + "b"], chan=fk + "b")
                    bU, bUk = bank()
                    bT, bTk = bank()
                    for part, (src, skey) in enumerate(((utok, "h_utok"), (utok, "h_utok"))):
                        for tc in range(NT):
                            k.mm(bU[:, part * 256:(part + 1) * 256], f_[:, part, tc, :], src[:, tc, :], tc == 0, tc == NT - 1,
                                 reads=[fk, fk + "b", skey], writes=[bUk])
                    for part, (src, skey) in enumerate(((Atok, "h_Atok"), (Btok, "h_Btok"))):
                        for tc in range(NT):
                            k.mm(bT[:, part * 256:(part + 1) * 256], f_[:, part, tc, :], src[:, tc, :], tc == 0, tc == NT - 1,
                                 reads=[fk, fk + "b", skey], writes=[bTk])
                    k.cp(Tre[:], bT[:, 0:256], reads=[bTk], writes=["h_Tre"])
                    k.act(Tim[:], bT[:, 256:512], AF.Copy, reads=[bTk, "h_sm"], writes=["h_Tim"], scale=sm[:, 0, fc:fc + 1])
                    k.cp(Dd[:], Tre[:], reads=["h_Tre"], writes=["h_Dd"], eng="pool")
                    k.ts(Dd[0:1, :], Dd[0:1, :], sm[0:1, 0, fc:fc + 1], ALU.mult, reads=["h_Dd", "h_sm"], writes=["h_Dd"])
                    k.stt(Dd[0:1, :], tn[0:1, :], sm[0:1, 1, fc:fc + 1], Dd[0:1, :], ALU.mult, ALU.add, reads=["h_tn", "h_sm", "h_Dd"], writes=["h_Dd"])
                    k.tt(t1[:], bU[:, 0:256], Tre[:], ALU.mult, reads=[bUk, "h_Tre"], writes=["h_t1"])
                    k.tt(t2[:], bU[:, 256:512], Tim[:], ALU.mult, reads=[bUk, "h_Tim"], writes=["h_t2"])
                    k.tt(Ysb[:, fc, :], t1[:], t2[:], ALU.subtract, reads=["h_t1", "h_t2"], writes=["h_Ysb"], eng="pool")
                    k.tt(t3[:], bU[:, 0:256], Tim[:], ALU.mult, reads=[bUk, "h_Tim"], writes=["h_t3"])
                    k.tt(t4_[:], bU[:, 256:512], Dd[:], ALU.mult, reads=[bUk, "h_Dd"], writes=["h_t4"])
                    k.tt(Ysb[:, 16 + fc, :], t3[:], t4_[:], ALU.add, reads=["h_t3", "h_t4"], writes=["h_Ysb"], eng="pool")
            with k.scope() as st2:
                fi = [k.sb(f"h_fi{i}", [128, 16, 512], BF16, stack=st2) for i in range(2)]
                y32 = k.sb("h_y32", [128, 2, 512], stack=st2)
                sq = k.sb("h_sq", [128, 512], BF16, stack=st2)
                rs = k.sb("h_rs", [128, 512], stack=st2)
                hb = k.sb("h_hb", [128, 2], stack=st2)
                nwc = k.sb("h_nwc", [128, 2], stack=st2)
                with nc.allow_non_contiguous_dma(reason="tiny"):
                    k.dma("sp", hb[:], W["hy_bias"][l].rearrange("(c p) -> p c", p=128), writes=["h_hb"], chan="c0")
                    k.dma("sp", nwc[:], W["hy_norm_w"][l].rearrange("(c p) -> p c", p=128), writes=["h_nwc"], chan="c0")
                for tb in range(NB):
                    sl = slice(tb * 512, (tb + 1) * 512)
                    for hf in range(2):
                        k.dma("sp", fi[hf][:], finv_d[tb, :, hf * 16:(hf + 1) * 16, :], writes=[f"h_fi{hf}"], chan=f"h_fi{hf}")
                    for ch in range(2):
                        b, bk = bank()
                        for fc in range(32):
                            k.mm(b[:], Ysb[:, fc, ch * 128:(ch + 1) * 128], fi[fc // 16][:, fc % 16, :], fc == 0, fc == 31,
                                 reads=["h_Ysb", f"h_fi{fc // 16}"], writes=[bk])
                        k.stt(y32[:, ch, :], uT[:, ch, sl], hb[:, ch:ch + 1], b[:], ALU.mult, ALU.add, reads=["h_uT", "h_hb", bk], writes=["h_y32"])
                        k.tt(y32[:, ch, :], y32[:, ch, :], x0T[:, ch, sl], ALU.mult, reads=["h_y32", "h_x0T"], writes=["h_y32"])
                    b, bk = bank()
                    for ch in range(2):
                        k.act(sq[:], y32[:, ch, :], AF.Square, reads=["h_y32"], writes=["h_sq"])
                        k.mm(b[:], ONESB, sq[:], ch == 0, ch == 1, reads=["h_sq", "cstb"], writes=[bk])
                    k.act(rs[:], b[:], AF.Ln, reads=[bk, "epsc"], writes=["h_rs"], bias=epsc[:, 0:1], scale=1.0 / 256)
                    k.act(rs[:], rs[:], AF.Exp, reads=["h_rs"], writes=["h_rs"], scale=-0.5)
                    for ch in range(2):
                        k.stt(omix[:, ch, sl], y32[:, ch, :], nwc[:, ch:ch + 1], rs[:], ALU.mult, ALU.mult,
                              reads=["h_y32", "h_nwc", "h_rs"], writes=["h_omix"])
        if "hy_o" in dbg:
            dbg_d["hy_o"] = k.dram("dbg_hy_o", [128, 2, T], BF16, kind="ExternalOutput")
            k.dma("sp", dbg_d["hy_o"], omix[:], reads=["h_omix"], writes=["dbg_hy_o"], chan="dbg3")
            out_keys.append("dbg_hy_o")
        out_proj(l, omix, "h_omix", 2, 256, 16)

    def mlp(l, st):
        hid = [k.sb(f"m_hid{i}", [128, 8, 512], BF16, stack=st) for i in range(2)]
        alloc_stg(st)
        wb = [k.sb(f"wb{i}", [128, 8, 1024], BF16, stack=st) for i in range(4)]
        rlu = [k.sb(f"m_rl{i}", [128, 512], BF16, stack=st) for i in range(2)]
        hi = 0
        def ld_gen(q):
            yield from load_w_gen(wb[(2 * q) % 4], f"wb{(2 * q) % 4}", W["mlp_w1"][l, :, q * 1024:(q + 1) * 1024], 8, 1024)
            yield from load_w_gen(wb[(2 * q + 1) % 4], f"wb{(2 * q + 1) % 4}", W["mlp_w2"][l, q * 1024:(q + 1) * 1024, :], 8, 1024)

        def pump(gen):
            if gen is not None:
                try:
                    next(gen)
                except StopIteration:
                    pass

        for _ in ld_gen(0):
            pass
        for q in range(4):
            w1, w1k = wb[(2 * q) % 4], f"wb{(2 * q) % 4}"
            w2, w2k = wb[(2 * q + 1) % 4], f"wb{(2 * q + 1) % 4}"
            nxt = ld_gen(q + 1) if q + 1 < 4 else None
            for tb in range(NB):
                sl = slice(tb * 512, (tb + 1) * 512)
                hd, hk = hid[hi % 2], f"m_hid{hi % 2}"
                hi += 1
                for hc in range(8):
                    b, bk = bank()
                    for kc in range(8):
                        k.mm(b[:], w1[:, kc, hc * 128:(hc + 1) * 128], hT[:, kc, sl], kc == 0, kc == 7,
                             reads=[w1k, ("hT", tb)], writes=[bk])
                    rl, rk = rlu[hc % 2], f"m_rl{hc % 2}"
                    k.act(rl[:], b[:], AF.Relu, reads=[bk], writes=[rk])
                    k.tt(hd[:, hc, :], rl[:], rl[:], ALU.mult, reads=[rk], writes=[hk])
                    if hc % 2 == 1:
                        pump(nxt)
                for dc in range(8):
                    b, bk = bank()
                    for kc in range(8):
                        k.mm(b[:], w2[:, kc, dc * 128:(dc + 1) * 128], hd[:, kc, :], kc == 0, kc == 7, reads=[w2k, hk], writes=[bk])
                    k.stt(xT[:, dc, sl], b[:], modT[:, l, 40 + dc:41 + dc], xT[:, dc, sl], ALU.mult, ALU.add,
                          reads=[bk, "modT", ("xT", tb)], writes=[("xT", tb)])
                    if dc % 2 == 1:
                        pump(nxt)
            if nxt is not None:
                for _ in nxt:
                    pass

    for l in range(depth):
        with k.scope() as st:
            rmsnorm_mod(l, 0, st)
        if "h1" in dbg and l == 0:
            dbg_d["h1"] = k.dram("dbg_h1", [128, 8, T], BF16, kind="ExternalOutput")
            k.dma("sp", dbg_d["h1"], hT[:], reads=[("hT", i) for i in range(4)], writes=["dbg_h1"], chan="dbg")
            out_keys.append("dbg_h1")
        if "ret" in mixers:
            with k.scope() as st:
                ret_mixer(l, st)
        if "hy" in mixers:
            with k.scope() as st:
                hy_mixer(l, st)
        if "ssd" in mixers:
            with k.scope() as st:
                ssd_mixer(l, st)
        if "gdn" in mixers:
            with k.scope() as st:
                gdn_mixer(l, st)
        with k.scope() as st:
            rmsnorm_mod(l, 1, st)
        with k.scope() as st:
            mlp(l, st)

    with k.scope() as st:
        sq = k.sb("f_sq", [128, 512], BF16, stack=st)
        rstd = k.sb("f_rstd", [128, 512], stack=st)
        tmp = k.sb("f_tmp", [128, 8, 512], stack=st)
        yt = [k.sb(f"f_y{i}", [128, 1024], stack=st) for i in range(2)]
        yi = 0
        for tb in range(NB):
            sl = slice(tb * 512, (tb + 1) * 512)
            b, bk = bank()
            for c in range(8):
                k.act(sq[:], xT[:, c, sl], AF.Square, reads=[("xT", tb)], writes=["f_sq"])
                k.mm(b[:], ONESB, sq[:], c == 0, c == 7, reads=["f_sq", "cstb"], writes=[bk])
            k.act(rstd[:], b[:], AF.Ln, reads=[bk, "epsc"], writes=["f_rstd"], bias=epsc[:, 0:1], scale=1.0 / D)
            k.act(rstd[:], rstd[:], AF.Exp, reads=["f_rstd"], writes=["f_rstd"], scale=-0.5)
            for c in range(8):
                k.stt(tmp[:, c, :], xT[:, c, sl], fnw[:, c:c + 1], rstd[:], ALU.mult, ALU.mult,
                      reads=[("xT", tb), "fnw", "f_rstd"], writes=["f_tmp"])
            for t4 in range(4):
                t = tb * 4 + t4
                y, yk = yt[yi % 2], f"f_y{yi % 2}"
                yi += 1
                for half in range(2):
                    b2, bk2 = bank()
                    for c4 in range(4):
                        c = half * 4 + c4
                        k.tr(b2[:, c4 * 128:(c4 + 1) * 128], tmp[:, c, t4 * 128:(t4 + 1) * 128], C("ident"),
                             reads=["f_tmp", "cst"], writes=[bk2])
                    k.cp(y[:, half * 512:(half + 1) * 512], b2[:], reads=[bk2], writes=[yk])
                k.dma("sp", y_d[t * 128:(t + 1) * 128, :], y[:], reads=[yk], writes=["y_out"], chan=yk)
    out_keys += ["y_out", "nsret", "nsssd", "nsgdn"]
    k._deps("sp", out_keys + [f"f_y0", "f_y1"], [])
    k.barrier()
    return k, dbg_d


def _rope_tables(prompt):
    if prompt:
        cos = np.ones((T, 32), np.float32)
        sin = np.zeros((T, 32), np.float32)
    else:
        rows = T // 64
        r = np.repeat(np.arange(rows), 64).astype(np.float32)
        col = np.tile(np.arange(64), rows).astype(np.float32)
        inv = (10000.0 ** (-np.arange(16, dtype=np.float32) / 16)).astype(np.float32)
        ang = np.concatenate([r[:, None] * inv, col[:, None] * inv], axis=-1).astype(np.float32)
        cos, sin = np.cos(ang), np.sin(ang)
    tab = np.zeros((128, 2, T), np.float32)
    for g in range(4):
        tab[g * 32:(g + 1) * 32, 0, :] = cos.T
        tab[g * 32:(g + 1) * 32, 1, :] = sin.T * (-1.0 if g % 2 == 0 else 1.0)
    return tab.astype(ml_dtypes.bfloat16)


_HY = {}


def _hyena_tables(prompt):
    if prompt in _HY:
        return _HY[prompt]
    L = 256 if prompt else 2048
    N = 2 * L
    nseg = T // L
    pos = np.arange(T) % L
    seg = np.arange(T) // L
    tlin = np.linspace(0.0, 1.0, L, dtype=np.float32)[pos]
    w = (2.0 * np.pi * np.arange(L, dtype=np.float32) / L).astype(np.float32)[pos]
    bands = 16
    fb = np.linspace(1e-4, bands - 1, bands, dtype=np.float32)
    z = np.concatenate([tlin[:, None], np.cos(fb[None, :] * w[:, None]), -np.sin(fb[None, :] * w[:, None])], axis=1)
    deltas = np.abs(np.linspace(math.log(1e-2) / 1.5, math.log(1e-2) / 0.3, 256, dtype=np.float32))
    tok = lambda a: np.ascontiguousarray(a.reshape(NT, 128).T)
    hyt = np.stack([tok(-tlin), tok((pos != 0).astype(np.float32))], axis=1).astype(np.float32)
    sgn = np.where(seg == 0, (-1.0) ** pos, 0.0)
    frow = np.arange(T)
    fl = frow % L
    s = (fl != 0).astype(np.float32)
    hys = np.stack([tok(s), tok(1.0 - s)], axis=1).astype(np.float32)
    same = (seg[:, None] == (frow // L)[None, :])
    ang = 2.0 * np.pi * ((pos[:, None].astype(np.int64) * fl[None, :].astype(np.int64)) % N) / N
    RE = np.where(same, np.cos(ang), 0.0)
    IM = np.where(same, -np.sin(ang), 0.0)
    nyq = np.where(same, ((-1.0) ** pos)[:, None] * np.ones((1, T)), 0.0)
    IM = np.where((fl == 0)[None, :], nyq, IM)
    F = np.concatenate([RE, IM], axis=1).astype(np.float32)
    ffwd = F.reshape(NT, 128, 32, 128).transpose(2, 1, 0, 3)
    GRE = np.where(same.T, np.where((fl == 0)[:, None], 1.0 / N, (2.0 / N) * np.cos(ang.T)), 0.0)
    GIM = np.where(same.T, np.where((fl == 0)[:, None], (1.0 / N) * ((-1.0) ** pos)[None, :], -(2.0 / N) * np.sin(ang.T)), 0.0)
    G = np.concatenate([GRE, GIM], axis=0).astype(np.float32)
    finv = G.reshape(32, 128, 4, 512).transpose(2, 1, 0, 3)
    out = {"hyz": np.ascontiguousarray(z.T.astype(np.float32)), "hyd": deltas[None, :].astype(np.float32), "hyt": hyt,
           "hys": hys, "hyg": tok(sgn).astype(ml_dtypes.bfloat16),
           "ffwd": np.ascontiguousarray(ffwd).astype(ml_dtypes.bfloat16),
           "finv": np.ascontiguousarray(finv).astype(ml_dtypes.bfloat16)}
    _HY[prompt] = out
    return out


_CACHE = {}


def make_in_maps(inputs, depth=DEPTH):
    cst_np, colc_np = _consts()
    f = lambda a: np.ascontiguousarray(np.asarray(a, dtype=np.float32))
    shared = {n: f(inputs[n])[:depth] if n not in ("final_norm_w",) else f(inputs[n])
              for n in ["norm1_w", "norm2_w", "w_mod", "b_mod", "w_in", "w_out", "ret_log_decay", "ret_norm_w",
                        "hy_conv_w", "hy_conv_b", "hy_freq", "hy_w1", "hy_b1", "hy_w2", "hy_b2", "hy_w3", "hy_bias", "hy_norm_w",
                        "gdn_conv_w", "gdn_A_log", "gdn_dt_bias", "gdn_norm_w",
                        "ssd_conv_w", "ssd_conv_b", "ssd_A_log", "ssd_dt_bias", "ssd_D", "ssd_norm_w",
                        "mlp_w1", "mlp_w2", "final_norm_w"]}
    shared["cst"] = cst_np
    shared["cst2"] = _consts2()
    shared["colc"] = colc_np
    xp = f(inputs["x_prompt"]).reshape(4, 8 * 256, D)
    xs = f(inputs["x_sample"])
    c = f(inputs["c"])
    cctx = f(inputs["c_ctx"])
    maps = []
    for core in range(8):
        m = dict(shared)
        prompt = core >= 4
        if prompt:
            m["x"] = xp[core - 4]
            cond = cctx
            m["sret"] = np.zeros((depth, 2, 4, 64, 64), np.float32)
            m["sssd"] = np.zeros((depth, 2, 4, 128, 64), np.float32)
            m["sgdn"] = np.zeros((depth, 2, 4, 64, 64), np.float32)
        else:
            m["x"] = xs[core]
            cond = c[core]
            m["sret"] = f(inputs["state_ret"])[core][:depth]
            m["sssd"] = f(inputs["state_ssd"])[core][:depth]
            m["sgdn"] = f(inputs["state_gdn"])[core][:depth]
        m["cond"] = np.ascontiguousarray(cond.reshape(8, 128).T)
        m["mflag"] = np.array([[0.0, 1.0]] if prompt else [[1.0, 0.0]], np.float32)
        m["rope"] = _rope_tables(prompt)
        m.update(_hyena_tables(prompt))
        maps.append(m)
    return maps


def kernel(**inputs):
    if "nc" not in _CACHE:
        _CACHE["nc"] = build()
    k, _ = _CACHE["nc"]
    maps = make_in_maps(inputs)
    res = run_bass_kernel_spmd(k.nc, maps, core_ids=list(range(8)))
    r = res.results
    y_sample = np.stack([r[i]["y"] for i in range(4)], axis=0)
    y_prompt = np.concatenate([r[i]["y"].reshape(8, 256, D) for i in range(4, 8)], axis=0)
    ns_ret = np.concatenate([r[i]["nsret"] for i in range(4, 8)], axis=0)
    ns_gdn = np.concatenate([r[i]["nsgdn"] for i in range(4, 8)], axis=0)
    ns_ssd = np.concatenate([r[i]["nsssd"] for i in range(4, 8)], axis=0)
    return (y_prompt, y_sample, ns_ret, ns_gdn, ns_ssd)
```

```python
import math
import numpy as np
import ml_dtypes
from contextlib import ExitStack
import concourse.bass as bass
import concourse.mybir as mybir
from concourse.bass_utils import run_bass_kernel_spmd

F32 = mybir.dt.float32
BF16 = mybir.dt.bfloat16
AF = mybir.ActivationFunctionType
ALU = mybir.AluOpType
AX = mybir.AxisListType

D = 1024
T = 2048
NT = 16
NB = 4
DEPTH = 4
EPS = 1e-6
IN_COLS = 3864
NEG = -30000.0
GDN_FP32 = True


class MK:
    def __init__(self):
        self.nc = bass.Bass("TRN2", target_bir_lowering=False)
        self.es = ExitStack()
        nc = self.nc
        self.eng = {"pe": nc.tensor, "act": nc.scalar, "dve": nc.vector, "pool": nc.gpsimd, "sp": nc.sync}
        self.sem = {}
        self.cnt = {}
        for e in self.eng:
            self.sem[e] = self.es.enter_context(nc.semaphore("s_" + e))
            self.cnt[e] = 0
        self.waited = {e: {} for e in self.eng}
        self.last_w = {}
        self.readers = {}
        self.dsem = {}
        self.dcnt = {}
        self.n_inst = 0
        self.bank_i = 0

    def sb(self, name, shape, dt=F32, stack=None):
        self.uid = getattr(self, "uid", 0) + 1
        return (stack or self.es).enter_context(self.nc.sbuf_tensor(f"sb{self.uid}_{name}", list(shape), dt))

    def ps(self, name, shape, dt=F32):
        return self.es.enter_context(self.nc.psum_tensor("ps_" + name, list(shape), dt))

    def dram(self, name, shape, dt=F32, kind="ExternalInput"):
        return self.nc.dram_tensor(name, list(shape), dt, kind=kind).ap()

    def _deps(self, e, reads, writes, attach=False):
        evs = []
        for key in list(reads) + list(writes):
            ev = self.last_w.get(key)
            if ev is not None:
                evs.append(ev)
        for key in writes:
            evs.extend(self.readers.get(key, ()))
        need = {}
        for (s, v, src) in evs:
            if src == "pe" and e == "pe":
                continue
            if need.get(id(s), (0,))[0] < v:
                need[id(s)] = (v, s)
        w = self.waited[e]
        todo = [(sid, v, s) for sid, (v, s) in need.items() if w.get(sid, 0) < v]
        self._attach = None
        if attach and todo and e in ("act", "dve", "pool"):
            sid, v, s = todo.pop()
            self._attach = (s, v)
            w[sid] = v
        for sid, v, s in todo:
            self.eng[e].wait_ge(s, v)
            w[sid] = v
            self.n_inst += 1

    def _record(self, ev, reads, writes):
        for key in reads:
            lst = self.readers.setdefault(key, [])
            lst[:] = [x for x in lst if not (x[0] is ev[0])]
            lst.append(ev)
        for key in writes:
            self.last_w[key] = ev
            self.readers[key] = []

    def op(self, e, fn, reads=(), writes=(), signal=True):
        self._deps(e, reads, writes, attach=True)
        inst = fn(self.eng[e])
        if self._attach is not None:
            inst._wait_ge(self._attach[0], self._attach[1])
            self._attach = None
        self.n_inst += 1
        if signal:
            self.cnt[e] += 1
            inst.then_inc(self.sem[e], 1)
            ev = (self.sem[e], self.cnt[e], e)
        else:
            ev = (self.sem[e], self.cnt[e] + 1, e)
        self._record(ev, reads, writes)
        return ev

    def dma(self, q, out, in_, reads=(), writes=(), chan=None, **kw):
        assert chan is not None
        if chan == "c0":
            chan = "k_" + "".join(ch if ch.isalnum() else "_" for ch in str(writes[0]))
        if chan not in self.dsem:
            self.dsem[chan] = self.es.enter_context(self.nc.semaphore("d_" + chan))
            self.dcnt[chan] = 0
        self._deps(q, reads, writes)
        inst = self.eng[q].dma_start(out=out, in_=in_, **kw)
        self.dcnt[chan] += 16
        inst.then_inc(self.dsem[chan], 16)
        self.n_inst += 1
        ev = (self.dsem[chan], self.dcnt[chan], "dma")
        self._record(ev, reads, writes)
        return ev

    def barrier(self):
        for e in self.eng:
            w = self.waited[e]
            for e2 in self.eng:
                if e2 != e and self.cnt[e2] > w.get(id(self.sem[e2]), 0):
                    self.eng[e].wait_ge(self.sem[e2], self.cnt[e2])
                    w[id(self.sem[e2])] = self.cnt[e2]
                    self.n_inst += 1
            for c, s in self.dsem.items():
                if self.dcnt[c] > w.get(id(s), 0):
                    self.eng[e].wait_ge(s, self.dcnt[c])
                    w[id(s)] = self.dcnt[c]
                    self.n_inst += 1

    def scope(self):
        return _Scope(self)

    def mm(self, out, lhsT, rhs, start, stop, reads, writes, quiet=False, **kw):
        return self.op("pe", lambda e: e.matmul(out, lhsT=lhsT, rhs=rhs, start=start, stop=stop, **kw), reads, writes,
                       signal=not (quiet and not stop))

    def tr(self, out, in_, ident, reads, writes):
        return self.op("pe", lambda e: e.transpose(out, in_=in_, identity=ident), reads, writes)

    def act(self, out, in_, func, reads, writes, bias=None, scale=None, eng="act"):
        kw = {}
        if bias is not None:
            kw["bias"] = bias
        if scale is not None:
            kw["scale"] = scale
        return self.op(eng, lambda e: e.activation(out=out, in_=in_, func=func, **kw), reads, writes)

    def tt(self, out, in0, in1, op, reads, writes, eng="dve"):
        return self.op(eng, lambda e: e.tensor_tensor(out=out, in0=in0, in1=in1, op=op), reads, writes)

    def ts(self, out, in0, s1, op0, reads, writes, s2=None, op1=None, eng="dve"):
        if op1 is None:
            return self.op(eng, lambda e: e.tensor_scalar(out=out, in0=in0, scalar1=s1, scalar2=None, op0=op0), reads, writes)
        return self.op(eng, lambda e: e.tensor_scalar(out=out, in0=in0, scalar1=s1, scalar2=s2, op0=op0, op1=op1), reads, writes)

    def stt(self, out, in0, scalar, in1, op0, op1, reads, writes):
        return self.op("dve", lambda e: e.scalar_tensor_tensor(out=out, in0=in0, scalar=scalar, in1=in1, op0=op0, op1=op1), reads, writes)

    def cp(self, out, in_, reads, writes, eng="act"):
        if eng == "act":
            return self.op("act", lambda e: e.copy(out=out, in_=in_), reads, writes)
        return self.op(eng, lambda e: e.tensor_copy(out=out, in_=in_), reads, writes)


class _Scope:
    def __init__(self, k):
        self.k = k
        self.st = ExitStack()

    def __enter__(self):
        return self.st

    def __exit__(self, *a):
        if a[0] is None:
            self.k.barrier()
        self.st.close()
        return False


C32 = {}


def _consts():
    names = ["ident", "ones", "triU", "triL", "blk64", "idxm_f", "idxm_b", "idx1", "idx2",
             "neg_f", "neg_b"]
    i = np.arange(128)
    J, I = np.meshgrid(i, i, indexing="ij")
    tab = {}
    tab["ident"] = (J == I).astype(np.float32)
    tab["ones"] = np.ones((128, 128), np.float32)
    tab["triU"] = (J <= I).astype(np.float32)
    tab["triL"] = (J >= I).astype(np.float32)
    tab["blk64"] = ((J // 64) == (I // 64)).astype(np.float32)
    tab["idxm_f"] = np.where(I >= J, I - J, 1e6).astype(np.float32)
    tab["idxm_b"] = np.where(I <= J, J - I, 1e6).astype(np.float32)
    tab["idx1"] = (I + 1).astype(np.float32)
    tab["idx2"] = (128 - I).astype(np.float32)
    tab["neg_f"] = np.where(I >= J, 0.0, NEG).astype(np.float32)
    tab["neg_b"] = np.where(I <= J, 0.0, NEG).astype(np.float32)
    arr = np.stack([tab[n] for n in names], axis=1)
    for n_i, n in enumerate(names):
        C32[n] = n_i
    colc = np.stack([127.0 - i, i.astype(np.float64)], axis=1).astype(np.float32)
    return np.ascontiguousarray(arr), colc


def _consts2():
    i = np.arange(128)
    J, I = np.meshgrid(i, i, indexing="ij")
    same = (J // 64) == (I // 64)
    t = [
        (same & (J <= I)),
        (same & (J >= I)),
        (J < 64) & (I >= 0),
        (J >= 64) & (I >= 0),
    ]
    out = [x.astype(np.float32) for x in t]
    out.append(np.where(same & (I >= J), 0.0, NEG).astype(np.float32))
    out.append(np.where(same & (I <= J), 0.0, NEG).astype(np.float32))
    out.append(np.where(same & (I < J), 0.0, NEG).astype(np.float32))
    out.append(np.where(same & (I > J), 0.0, NEG).astype(np.float32))
    return np.ascontiguousarray(np.stack(out, axis=1))


def build(depth=DEPTH, mixers=("ret", "hy", "gdn", "ssd"), dbg=()):
    k = MK()
    nc = k.nc
    cst_np, colc_np = _consts()
    NCST = cst_np.shape[1]

    x_d = k.dram("x", [T, D])
    cond_d = k.dram("cond", [128, 8])
    mflag_d = k.dram("mflag", [1, 2])
    cst_d = k.dram("cst", [128, NCST, 128])
    colc_d = k.dram("colc", [128, 2])
    rope_d = k.dram("rope", [128, 2, T], BF16)
    sret_d = k.dram("sret", [depth, 2, 4, 64, 64])
    W = {}
    for name, shp in [("norm1_w", [depth, D]), ("norm2_w", [depth, D]), ("w_mod", [depth, D, 6 * D]),
                      ("b_mod", [depth, 6 * D]), ("w_in", [depth, D, IN_COLS]), ("w_out", [depth, D, D]),
                      ("ret_log_decay", [depth, 2, 4]), ("ret_norm_w", [depth, 256]),
                      ("hy_conv_w", [depth, 3, 768]), ("hy_conv_b", [depth, 768]), ("hy_freq", [depth, 64]),
                      ("hy_w1", [depth, 33, 64]), ("hy_b1", [depth, 64]), ("hy_w2", [depth, 64, 64]), ("hy_b2", [depth, 64]),
                      ("hy_w3", [depth, 64, 512]), ("hy_bias", [depth, 256]), ("hy_norm_w", [depth, 256]),
                      ("gdn_conv_w", [depth, 3, 768]), ("gdn_A_log", [depth, 2, 4]), ("gdn_dt_bias", [depth, 2, 4]),
                      ("gdn_norm_w", [depth, 256]),
                      ("ssd_conv_w", [depth, 3, 768]), ("ssd_conv_b", [depth, 768]), ("ssd_A_log", [depth, 2, 4]),
                      ("ssd_dt_bias", [depth, 2, 4]), ("ssd_D", [depth, 4]), ("ssd_norm_w", [depth, 256]),
                      ("mlp_w1", [depth, D, 4 * D]), ("mlp_w2", [depth, 4 * D, D]), ("final_norm_w", [D])]:
        W[name] = k.dram(name, shp)
    y_d = k.dram("y", [T, D], kind="ExternalOutput")
    nsret_d = k.dram("nsret", [8, depth, 2, 4, 64, 64], kind="ExternalOutput")
    sssd_d = k.dram("sssd", [depth, 2, 4, 128, 64])
    sgdn_d = k.dram("sgdn", [depth, 2, 4, 64, 64])
    nsgdn_d = k.dram("nsgdn", [8, depth, 2, 4, 64, 64], kind="ExternalOutput")
    cst2_d = k.dram("cst2", [128, 8, 128])
    hyz_d = k.dram("hyz", [33, T])
    hyd_d = k.dram("hyd", [1, 256])
    hyt_d = k.dram("hyt", [128, 2, NT])
    hys_d = k.dram("hys", [128, 2, 16])
    hyg_d = k.dram("hyg", [128, NT], BF16)
    ffwd_d = k.dram("ffwd", [32, 128, 16, 128], BF16)
    finv_d = k.dram("finv", [4, 128, 32, 512], BF16)
    nsssd_d = k.dram("nsssd", [8, depth, 2, 4, 128, 64], kind="ExternalOutput")
    dbg_d = {}
    out_keys = []

    cst = k.sb("cst", [128, NCST, 128])
    cstb = k.sb("cstb", [128, 3, 128], BF16)
    colc = k.sb("colc", [128, 2])
    epsc = k.sb("epsc", [128, 4])
    k.op("dve", lambda e: e.memset(epsc[:, 0:1], EPS), writes=["epsc"])
    k.op("dve", lambda e: e.memset(epsc[:, 1:2], 64.0 * EPS), writes=["epsc"])
    k.op("dve", lambda e: e.memset(epsc[:, 2:3], 128.0 * EPS), writes=["epsc"])
    k.op("dve", lambda e: e.memset(epsc[:, 3:4], 256.0 * EPS), writes=["epsc"])
    mfl = k.sb("mfl", [128, 2])
    modT = k.sb("modT", [128, depth, 48])
    nw = k.sb("nw", [128, depth, 2, 8])
    fnw = k.sb("fnw", [128, 8])
    AB = k.sb("AB", [128, 2, 8])
    banks = [k.ps(f"bank{i}", [128, 512]) for i in range(8)]

    def C(name):
        return cst[:, C32[name], :]

    IDB, ONESB, BLKB = cstb[:, 0, :], cstb[:, 1, :], cstb[:, 2, :]

    def run_pipelined(gen_fn, items, depth):
        items = list(items)
        active = []
        nxt = 0
        while nxt < len(items) or active:
            while nxt < len(items) and len(active) < depth:
                slot = [sl for sl in range(depth) if sl not in [a[0] for a in active]][0]
                active.append((slot, gen_fn(items[nxt], slot)))
                nxt += 1
            for a in list(active):
                try:
                    next(a[1])
                except StopIteration:
                    active.remove(a)

    def bank():
        i = k.bank_i
        k.bank_i = (i + 1) % 8
        return banks[i], f"bank{i}"

    stg_i = [0]
    cast_i = [0]

    def load_w(dst, dkey, src, nk, ncols, coff=0, cast_eng=None):
        if ncols > 1024:
            load_w(dst, dkey, src[:, 0:1024], nk, 1024, coff, cast_eng)
            load_w(dst, dkey, src[:, 1024:ncols], nk, ncols - 1024, coff + 1024, cast_eng)
            return
        per = max(1, 1024 // ncols)
        kc = 0
        while kc < nk:
            n = min(per, nk - kc)
            si = stg_i[0]
            stg_i[0] = (si + 1) % 2
            sview = stg[si][:, 0:n * ncols].rearrange("p (a c) -> p a c", c=ncols)
            k.dma("sp", sview, src[kc * 128:(kc + n) * 128, :].rearrange("(a p) c -> p a c", p=128),
                  writes=[f"stg{si}"], chan=f"stg{si}")
            cast_i[0] += 1
            k.cp(dst[:, kc:kc + n, coff:coff + ncols], sview, reads=[f"stg{si}"], writes=[dkey],
                 eng=(cast_eng or ("pool" if cast_i[0] % 3 == 0 else "act")))
            kc += n

    def load_w_gen(dst, dkey, src, nk, ncols, cast_eng="act"):
        assert ncols <= 1024
        per = max(1, 1024 // ncols)
        pend = None
        kc = 0
        while kc < nk or pend is not None:
            cur = None
            if kc < nk:
                n = min(per, nk - kc)
                si = stg_i[0]
                stg_i[0] = (si + 1) % 2
                sview = stg[si][:, 0:n * ncols].rearrange("p (a c) -> p a c", c=ncols)
                k.dma("sp", sview, src[kc * 128:(kc + n) * 128, :].rearrange("(a p) c -> p a c", p=128),
                      writes=[f"stg{si}"], chan=f"stg{si}")
                cur = (kc, n, si, sview)
                kc += n
            if pend is not None:
                pkc, pn, psi, pview = pend
                k.cp(dst[:, pkc:pkc + pn, 0:ncols], pview, reads=[f"stg{psi}"], writes=[dkey], eng=cast_eng)
            pend = cur
            yield

    k.dma("sp", cst[:], cst_d, writes=["cst"], chan="c0")
    k.dma("sp", colc[:], colc_d, writes=["colc"], chan="c0")
    k.dma("sp", mfl[:], mflag_d.partition_broadcast(128), writes=["mfl"], chan="c0")
    for j, nm in enumerate(["ident", "ones", "blk64"]):
        k.cp(cstb[:, j, :], C(nm), reads=["cst"], writes=["cstb"], eng="dve")
    with nc.allow_non_contiguous_dma(reason="tiny feature-major param loads"):
        for l in range(depth):
            k.dma("sp", nw[:, l, 0, :], W["norm1_w"][l].rearrange("(c p) -> p c", p=128), writes=["nw"], chan="c0")
            k.dma("sp", nw[:, l, 1, :], W["norm2_w"][l].rearrange("(c p) -> p c", p=128), writes=["nw"], chan="c0")
        k.dma("sp", fnw[:], W["final_norm_w"].rearrange("(c p) -> p c", p=128), writes=["fnw"], chan="c0")

    with k.scope() as st:
        scond = k.sb("scond", [128, 8], stack=st)
        condr = k.sb("condr", [128, 8], stack=st)
        mrow = k.sb("mrow", [1, 6 * D], stack=st)
        brow = k.sb("brow", [1, 6 * D], stack=st)
        one1 = k.sb("one1", [1, 1], stack=st)
        NWST = 8
        wst = [k.sb(f"wst{i}", [128, 2048], stack=st) for i in range(NWST)]
        k.dma("sp", condr[:], cond_d, writes=["condr"], chan="c0")
        k.act(scond[:], condr[:], AF.Silu, reads=["condr"], writes=["scond"])
        k.op("dve", lambda e: e.memset(one1[:], 1.0), writes=["one1"])
        for l in range(depth):
            k.dma("sp", brow[:], W["b_mod"][l:l + 1, :], writes=["brow"], chan="c0")
            wi = 0
            for cb in range(3):
                pbs = [bank() for _ in range(4)]
                for kc in range(8):
                    ws, wk = wst[wi % NWST], f"wst{wi % NWST}"
                    wi += 1
                    k.dma("sp", ws[:], W["w_mod"][l, kc * 128:(kc + 1) * 128, cb * 2048:(cb + 1) * 2048],
                          writes=[wk], chan=wk)
                    for q in range(4):
                        k.mm(pbs[q][0][0:1, :], scond[:, kc:kc + 1], ws[:, q * 512:(q + 1) * 512], kc == 0, kc == 7,
                             reads=[wk, "scond"], writes=[pbs[q][1]])
                for q in range(4):
                    c0 = cb * 2048 + q * 512
                    k.tt(mrow[:, c0:c0 + 512], pbs[q][0][0:1, :], brow[:, c0:c0 + 512], ALU.add,
                         reads=[pbs[q][1], "brow"], writes=["mrow"])
            b, bk = bank()
            for j in range(48):
                k.mm(b[:, j:j + 1], mrow[0:1, j * 128:(j + 1) * 128], one1[0:1, 0:1], True, True,
                     reads=["mrow", "one1"], writes=[bk])
            k.cp(modT[:, l, :], b[:, 0:48], reads=[bk], writes=["modT"], eng="dve")

    xT = k.sb("xT", [128, 8, T])
    hT = k.sb("hT", [128, 8, T], BF16)
    stg = []

    def alloc_stg(stack):
        stg[:] = [k.sb(f"stg{i}", [128, 1024], stack=stack) for i in range(2)]
    _xs = k.scope()
    alloc_stg(_xs.__enter__())
    for t in range(NT):
        si = stg_i[0]
        stg_i[0] = (si + 1) % 2
        k.dma("sp", stg[si][:], x_d[t * 128:(t + 1) * 128, :], writes=[f"stg{si}"], chan=f"stg{si}")
        for half in range(2):
            b, bk = bank()
            for c4 in range(4):
                c = half * 4 + c4
                k.tr(b[:, c4 * 128:(c4 + 1) * 128], stg[si][:, c * 128:(c + 1) * 128], C("ident"),
                     reads=[f"stg{si}", "cst"], writes=[bk])
            k.cp(xT[:, half * 4:half * 4 + 4, t * 128:(t + 1) * 128],
                 b[:].rearrange("p (a c) -> p a c", c=128), reads=[bk], writes=[("xT", t // 4)])
    _xs.__exit__(None, None, None)

    def rmsnorm_mod(l, which, st):
        sqs = [k.sb(f"nsq{i}", [128, 512], BF16, stack=st) for i in range(4)]
        rstds = [k.sb(f"nrstd{i}", [128, 512], stack=st) for i in range(2)]
        tmps = [k.sb(f"ntmp{i}", [128, 512], stack=st) for i in range(4)]
        o = 0 if which == 0 else 24
        k.stt(AB[:, 0, :], modT[:, l, o + 8:o + 16], 1.0, nw[:, l, which, :], ALU.add, ALU.mult,
              reads=["modT", "nw"], writes=["AB"])
        k.cp(AB[:, 1, :], modT[:, l, o:o + 8], reads=["modT"], writes=["AB"], eng="dve")
        for tb in range(NB):
            sl = slice(tb * 512, (tb + 1) * 512)
            b, bk = bank()
            rstd, rk_ = rstds[tb % 2], f"nrstd{tb % 2}"
            for c in range(8):
                sq, sk_ = sqs[c % 4], f"nsq{c % 4}"
                if c % 2 == 0:
                    k.act(sq[:], xT[:, c, sl], AF.Square, reads=[("xT", tb)], writes=[sk_])
                else:
                    k.tt(sq[:], xT[:, c, sl], xT[:, c, sl], ALU.mult, reads=[("xT", tb)], writes=[sk_])
                k.mm(b[:], ONESB, sq[:], c == 0, c == 7, reads=[sk_, "cstb"], writes=[bk])
            k.act(rstd[:], b[:], AF.Ln, reads=[bk, "epsc"], writes=[rk_], bias=epsc[:, 0:1], scale=1.0 / D)
            k.act(rstd[:], rstd[:], AF.Exp, reads=[rk_], writes=[rk_], scale=-0.5)
            for c in range(8):
                tmp, tk_ = tmps[c % 4], f"ntmp{c % 4}"
                k.tt(tmp[:], xT[:, c, sl], rstd[:], ALU.mult, reads=[("xT", tb), rk_], writes=[tk_])
                k.act(hT[:, c, sl], tmp[:], AF.Identity, reads=[tk_, "AB"], writes=[("hT", tb)],
                      bias=AB[:, 1, c:c + 1], scale=AB[:, 0, c:c + 1])

    def proj_fm(wt, wkey, col0, ncol, evac):
        for tb in range(NB):
            b, bk = bank()
            for kc in range(8):
                k.mm(b[0:ncol, :], wt[:, kc, col0:col0 + ncol], hT[:, kc, tb * 512:(tb + 1) * 512], kc == 0, kc == 7,
                     reads=[wkey, ("hT", tb)], writes=[bk], quiet=True)
            evac(tb, b, bk)

    def out_proj(l, src, skey, nkc, wrow0, gate_off):
        with k.scope() as st2:
            alloc_stg(st2)
            wt, wkey = k.sb("wo_w", [128, nkc, D], BF16, stack=st2), "wo_w"
            load_w(wt, wkey, W["w_out"][l, wrow0:wrow0 + nkc * 128, :], nkc, D)
            for dc in range(8):
                for tb in range(NB):
                    sl = slice(tb * 512, (tb + 1) * 512)
                    b, bk = bank()
                    for kc in range(nkc):
                        k.mm(b[:], wt[:, kc, dc * 128:(dc + 1) * 128], src[:, kc, sl], kc == 0, kc == nkc - 1,
                             reads=[wkey, skey], writes=[bk], quiet=True)
                    k.stt(xT[:, dc, sl], b[:], modT[:, l, gate_off + dc:gate_off + dc + 1], xT[:, dc, sl], ALU.mult, ALU.add,
                          reads=[bk, "modT", ("xT", tb)], writes=[("xT", tb)])

    def ret_proj(l, qT, kT, gT, vtok, t1):
        with k.scope() as st2:
            alloc_stg(st2)
            wa, wak = k.sb("r_wa", [128, 8, 1024], BF16, stack=st2), "r_wa"
            wsw, wswk = k.sb("r_wsw", [128, 8, 512], BF16, stack=st2), "r_wsw"
            ropet = k.sb("r_rope", [128, 2, T], BF16, stack=st2)
            t2 = k.sb("r_t2", [128, 512], stack=st2)
            k.dma("sp", ropet[:], rope_d, writes=["r_rope"], chan="c0")
            load_w(wa, wak, W["w_in"][l, :, 0:1024], 8, 1024)
            with nc.allow_non_contiguous_dma(reason="rope half-swapped q/k weight columns"):
                for kc in range(8):
                    si = stg_i[0]
                    stg_i[0] = (si + 1) % 2
                    sv = stg[si][:, 0:512].rearrange("p (h two f) -> p h two f", two=2, f=32)
                    src = W["w_in"][l, kc * 128:(kc + 1) * 128, 0:512].rearrange("p (h two f) -> p h two f", two=2, f=32)
                    k.dma("sp", sv[:, :, 0, :], src[:, :, 1, :], writes=[f"stg{si}"], chan=f"stg{si}")
                    k.dma("sp", sv[:, :, 1, :], src[:, :, 0, :], writes=[f"stg{si}"], chan=f"stg{si}")
                    k.cp(wsw[:, kc, 0:512], stg[si][:, 0:512], reads=[f"stg{si}"], writes=[wswk], eng="pool")

            for which, dst, dkey, scl in ((0, qT, "r_qT", 1.0), (1, kT, "r_kT", 0.125)):
                for pr in range(2):
                    col = which * 256 + pr * 128
                    for tb in range(NB):
                        sl = slice(tb * 512, (tb + 1) * 512)
                        b1, bk1 = bank()
                        b2, bk2 = bank()
                        for kc in range(8):
                            k.mm(b1[:], wa[:, kc, col:col + 128], hT[:, kc, sl], kc == 0, kc == 7, reads=[wak, ("hT", tb)], writes=[bk1], quiet=True)
                        for kc in range(8):
                            k.mm(b2[:], wsw[:, kc, col:col + 128], hT[:, kc, sl], kc == 0, kc == 7, reads=[wswk, ("hT", tb)], writes=[bk2], quiet=True)
                        k.stt(t1[:], b1[:], scl, ropet[:, 0, sl], ALU.mult, ALU.mult, reads=[bk1, "r_rope"], writes=["r_t1"])
                        k.stt(t2[:], b2[:], scl, ropet[:, 1, sl], ALU.mult, ALU.mult, reads=[bk2, "r_rope"], writes=["r_t2"])
                        k.tt(dst[:, pr, sl], t1[:], t2[:], ALU.add, reads=["r_t1", "r_t2"], writes=[dkey], eng="pool")
            for pr in range(2):
                proj_fm(wa, wak, 768 + pr * 128, 128,
                        lambda tb, b, bk, pr=pr: k.act(gT[:, pr, tb * 512:(tb + 1) * 512], b[:], AF.Silu, reads=[bk], writes=["r_gT"]))
            for t in range(NT):
                b, bk = bank()
                for kc in range(8):
                    k.mm(b[:, 0:256], hT[:, kc, t * 128:(t + 1) * 128], wa[:, kc, 512:768], kc == 0, kc == 7,
                         reads=[wak, ("hT", t // 4)], writes=[bk], quiet=True)
                k.cp(vtok[:, t, :], b[:, 0:256], reads=[bk], writes=["r_vtok"])

    def ret_mixer(l, st):
        omix = k.sb("r_omix", [128, 2, T], BF16, stack=st)
        qT = k.sb("r_qT", [128, 2, T], BF16, stack=st)
        kT = k.sb("r_kT", [128, 2, T], BF16, stack=st)
        gT = k.sb("r_gT", [128, 2, T], BF16, stack=st)
        vtok = k.sb("r_vtok", [128, NT, 256], BF16, stack=st)
        t1 = k.sb("r_t1", [128, 512], stack=st)
        w8 = k.sb("r_w8", [128, 2], stack=st)
        ld = W["ret_log_decay"]
        ret_proj(l, qT, kT, gT, vtok, t1)
        if "stop:proj" in dbg:
            return
        with k.scope() as st3:
            ret_scan(l, st3, omix, qT, kT, gT, vtok, t1, w8, ld)
        if "stop:scan" in dbg or "stop:tables" in dbg or "stop:state" in dbg:
            return
        out_proj(l, omix, "r_omix", 2, 0, 16)

    def ret_scan(l, st, omix, qT, kT, gT, vtok, t1, w8, ld):
        Sf = k.sb("r_Sf", [128, NT, 128], BF16, stack=st)
        Sb = k.sb("r_Sb", [128, NT, 128], BF16, stack=st)
        S32 = k.sb("r_S32", [128, 2, 128], stack=st)
        Sout = k.sb("r_Sout", [128, 2, 8, 128], stack=st)
        gcol = k.sb("r_gcol", [128, 2, 2], stack=st)
        acol = k.sb("r_acol", [128, 2, 2], stack=st)
        gall = k.sb("r_gall", [128, 8], stack=st)
        dk = k.sb("r_dk", [128, 8], stack=st)
        dec2 = k.sb("r_dec2", [128, 4, 128], BF16, stack=st)
        dtmp = k.sb("r_dtmp", [128, 4, 128], stack=st)
        EE = k.sb("r_EE", [128, 2, 2, 128], stack=st)
        kd = k.sb("r_kd", [128, 256], BF16, stack=st)
        att = k.sb("r_att", [128, 4, 128], BF16, stack=st)
        qE = k.sb("r_qE", [128, 2, 2, 128], BF16, stack=st)
        o32 = k.sb("r_o32", [128, 2, 512], stack=st)
        sq = k.sb("r_sq", [128, 512], BF16, stack=st)
        rs = k.sb("r_rs", [128, 512], stack=st)

        with nc.allow_non_contiguous_dma(reason="tiny"):
            for half in range(2):
                k.dma("sp", gcol[half * 64:(half + 1) * 64, :, :],
                      ld[l:l + 1].rearrange("o d (pr hf) -> o hf d pr", hf=2)[:, half].partition_broadcast(64),
                      writes=["r_gcol"], chan="c0")
            k.dma("sp", gall[:], ld[l:l + 1].rearrange("o d h -> o (d h)").partition_broadcast(128), writes=["r_gall"], chan="c0")
            k.dma("sp", w8[:], W["ret_norm_w"][l].rearrange("(c p) -> p c", p=128), writes=["r_w8"], chan="c0")
        k.ts(w8[:], w8[:], 8.0, ALU.mult, reads=["r_w8"], writes=["r_w8"])
        for h in range(4):
            k.act(dtmp[:, h, :], C("idxm_f"), AF.Exp, reads=["cst", "r_gall"], writes=["r_dtmp"], scale=gall[:, h:h + 1])
            k.act(att[:, h, :], C("idxm_b"), AF.Exp, reads=["cst", "r_gall"], writes=["r_att"], scale=gall[:, 4 + h:5 + h])
        k.tt(dec2[:], dtmp[:], att[:], ALU.add, reads=["r_dtmp", "r_att"], writes=["r_dec2"])
        for d in range(2):
            for pr in range(2):
                k.act(EE[:, d, pr, :], C("idx1" if d == 0 else "idx2"), AF.Exp, reads=["cst", "r_gcol"], writes=["r_EE"],
                      scale=gcol[:, d, pr:pr + 1])
            k.act(dk[:, d * 4:d * 4 + 4], gall[:, d * 4:d * 4 + 4], AF.Exp, reads=["r_gall", "colc"], writes=["r_dk"],
                  scale=colc[:, d:d + 1])
        k.act(acol[:], gcol[:], AF.Exp, reads=["r_gcol"], writes=["r_acol"], scale=128.0)

        if "stop:tables" in dbg:
            return
        kds = [kd, k.sb("r_kd1", [128, 256], BF16, stack=st)]

        def ret_pass(d):
            kd = kds[d]
            kdk = f"r_kd{d}"
            order = list(range(NT)) if d == 0 else list(range(NT - 1, -1, -1))
            snap = Sf if d == 0 else Sb
            snk = "r_Sf" if d == 0 else "r_Sb"
            skey = f"r_S32_{d}"
            with nc.allow_non_contiguous_dma(reason="state load"):
                k.dma("sp", S32[:, d, :].rearrange("p (pr v) -> p pr v", v=64),
                      sret_d[l, d].rearrange("(pr hf) kk v -> (hf kk) pr v", hf=2), writes=[skey], chan="c0")
            for idx, n in enumerate(order):
                if idx > 0 and idx % 2 == 0:
                    k.ts(S32[:, d, :], S32[:, d, :], mfl[:, 0:1], ALU.mult, reads=[skey, "mfl"], writes=[skey])
                k.cp(snap[:, n, :], S32[:, d, :], reads=[skey], writes=[snk])
                b, bk = bank()
                bt = b[:].bitcast(BF16)
                for pr in range(2):
                    k.tr(bt[:, pr * 128:(pr + 1) * 128], kT[:, pr, n * 128:(n + 1) * 128], IDB, reads=["r_kT", "cstb"], writes=[bk])
                k.tt(kd[:].rearrange("p (h f) -> p h f", f=64), bt[:, 0:256].rearrange("p (h f) -> p h f", f=64),
                     dk[:, d * 4:d * 4 + 4].unsqueeze(2).to_broadcast([128, 4, 64]), ALU.mult, reads=[bk, "r_dk"], writes=[kdk])
                b2, bk2 = bank()
                for h in range(4):
                    pb = (h % 2) * 64
                    k.mm(b2[pb:pb + 64, (h // 2) * 64:(h // 2) * 64 + 64], kd[:, h * 64:(h + 1) * 64], vtok[:, n, h * 64:(h + 1) * 64],
                         True, True, reads=[kdk, "r_vtok"], writes=[bk2])
                for pr in range(2):
                    k.stt(S32[:, d, pr * 64:(pr + 1) * 64], S32[:, d, pr * 64:(pr + 1) * 64], acol[:, d, pr:pr + 1],
                          b2[:, pr * 64:(pr + 1) * 64], ALU.mult, ALU.add, reads=[skey, "r_acol", bk2], writes=[skey])
                if idx % 2 == 1:
                    seg = n // 2
                    k.cp(Sout[:, d, seg, :], S32[:, d, :], reads=[skey], writes=["r_Sout"])
                yield
        gens = [ret_pass(0), ret_pass(1)]
        while gens:
            for gg in list(gens):
                try:
                    next(gg)
                except StopIteration:
                    gens.remove(gg)
        with nc.allow_non_contiguous_dma(reason="state store"):
            for d in range(2):
                for pr in range(2):
                    k.dma("sp", nsret_d[:, l, d, pr * 2:pr * 2 + 2].rearrange("s hf kk v -> (hf kk) s v"),
                          Sout[:, d, :, pr * 64:(pr + 1) * 64], reads=["r_Sout"], writes=["nsret"], chan="nsret")

        if "stop:state" in dbg:
            return
        PD = 3
        atts = [att] + [k.sb(f"r_att{i}", [128, 4, 128], BF16, stack=st) for i in range(1, PD)]
        qEs = [qE] + [k.sb(f"r_qE{i}", [128, 2, 2, 128], BF16, stack=st) for i in range(1, PD)]

        def ret_tile(item, slot):
            tb, t4 = item
            n = tb * 4 + t4
            tsl = slice(n * 128, (n + 1) * 128)
            att, qE = atts[slot], qEs[slot]
            sx = f"_{slot}"
            bb2 = [bank(), bank()]
            for h in (0, 2, 1, 3):
                pb = (h % 2) * 64
                b, bk = bb2[h % 2]
                k.mm(b[:, (h // 2) * 128:(h // 2 + 1) * 128], kT[pb:pb + 64, h // 2, tsl], qT[pb:pb + 64, h // 2, tsl], True, True,
                     reads=["r_kT", "r_qT"], writes=[bk])
            for d in range(2):
                k.tt(qE[:, d], qT[:, :, tsl], EE[:, d], ALU.mult, reads=["r_qT", "r_EE"], writes=["r_qE" + sx])
            yield
            for hf in range(2):
                b, bk = bb2[hf]
                k.tt(att[:, hf::2, :], b[:, 0:256].rearrange("p (h i) -> p h i", i=128), dec2[:, hf::2, :], ALU.mult,
                     reads=[bk, "r_dec2"], writes=["r_att" + sx])
            yield
            ob2 = [bank(), bank()]
            for h in (0, 2, 1, 3):
                pb = (h % 2) * 64
                pr = h // 2
                b2, bk2 = ob2[h % 2]
                o_ap = b2[pb:pb + 64, pr * 128:(pr + 1) * 128]
                k.mm(o_ap, vtok[:, n, h * 64:(h + 1) * 64], att[:, h, :], True, False, reads=["r_vtok", "r_att" + sx], writes=[bk2])
                k.mm(o_ap, Sf[pb:pb + 64, n, pr * 64:(pr + 1) * 64], qE[pb:pb + 64, 0, pr, :], False, False,
                     reads=["r_Sf", "r_qE" + sx], writes=[bk2])
                k.mm(o_ap, Sb[pb:pb + 64, n, pr * 64:(pr + 1) * 64], qE[pb:pb + 64, 1, pr, :], False, True,
                     reads=["r_Sb", "r_qE" + sx], writes=[bk2])
            yield
            for hf in range(2):
                b2, bk2 = ob2[hf]
                pb = hf * 64
                k.cp(o32[pb:pb + 64, :, t4 * 128:(t4 + 1) * 128], b2[pb:pb + 64, 0:256].rearrange("p (pr i) -> p pr i", i=128),
                     reads=[bk2], writes=[("r_o32", t4)])

        for tb in range(NB):
            run_pipelined(ret_tile, [(tb, t4) for t4 in range(4)], PD)
            okeys = [("r_o32", j) for j in range(4)]
            sl = slice(tb * 512, (tb + 1) * 512)
            if "ost:0" in dbg or "ost:1" in dbg or "ost:2" in dbg or "ost:3" in dbg:
                continue
            for pr in range(2):
                k.act(sq[:], o32[:, pr, :], AF.Square, reads=okeys, writes=["r_sq"])
                b, bk = bank()
                k.mm(b[:], BLKB, sq[:], True, True, reads=["r_sq", "cstb"], writes=[bk])
                k.act(rs[:], b[:], AF.Ln, reads=[bk, "epsc"], writes=["r_rs"], bias=epsc[:, 1:2], scale=1.0)
                k.act(rs[:], rs[:], AF.Exp, reads=["r_rs"], writes=["r_rs"], scale=-0.5)
                k.tt(t1[:], o32[:, pr, :], rs[:], ALU.mult, reads=okeys + ["r_rs"], writes=["r_t1"])
                k.stt(omix[:, pr, sl], t1[:], w8[:, pr:pr + 1], gT[:, pr, sl], ALU.mult, ALU.mult,
                      reads=["r_t1", "r_w8", "r_gT"], writes=["r_omix"])
        if "ret_o" in dbg:
            dbg_d["ret_o"] = k.dram("dbg_ret_o", [128, 2, T], BF16, kind="ExternalOutput")
            k.dma("sp", dbg_d["ret_o"], omix[:], reads=["r_omix"], writes=["dbg_ret_o"], chan="dbg")
            out_keys.append("dbg_ret_o")

    def load_conv_params(cw, ckey, wname, bname, l, nch, st):
        with nc.allow_non_contiguous_dma(reason="tiny conv params"):
            for kk in range(3):
                k.dma("sp", cw[:, :, kk], W[wname][l, kk].rearrange("(c p) -> p c", p=128), writes=[ckey], chan="c0")
            if bname is not None:
                k.dma("sp", cw[:, :, 3], W[bname][l].rearrange("(c p) -> p c", p=128), writes=[ckey], chan="c0")
            else:
                k.op("dve", lambda e: e.memset(cw[:, :, 3:4], 0.0), writes=[ckey])
        k.ts(cw[:, :, 4:5], cw[:, :, 0:1], mfl[:, 1:2], ALU.mult, reads=[ckey, "mfl"], writes=[ckey], s2=-1.0, op1=ALU.mult)
        k.ts(cw[:, :, 5:6], cw[:, :, 2:3], mfl[:, 1:2], ALU.mult, reads=[ckey, "mfl"], writes=[ckey], s2=-1.0, op1=ALU.mult)

    def conv_chunk(wt, wkey, col0, cw, ckey, ci, pre, acc, dst, dkey, func):
        for tb in range(NB):
            b, bk = bank()
            for kc in range(8):
                k.mm(b[:], wt[:, kc, col0:col0 + 128], hT[:, kc, tb * 512:(tb + 1) * 512], kc == 0, kc == 7,
                     reads=[wkey, ("hT", tb)], writes=[bk], quiet=True)
            k.cp(pre[:, 1 + tb * 512:1 + (tb + 1) * 512], b[:], reads=[bk], writes=["cv_pre"])
        k.act(acc[:], pre[:, 1:T + 1], AF.Identity, reads=["cv_pre", ckey], writes=["cv_acc"],
              bias=cw[:, ci, 3:4], scale=cw[:, ci, 1:2])
        k.stt(acc[:], pre[:, 0:T], cw[:, ci, 0:1], acc[:], ALU.mult, ALU.add, reads=["cv_pre", ckey, "cv_acc"], writes=["cv_acc"])
        k.stt(acc[:], pre[:, 2:T + 2], cw[:, ci, 2:3], acc[:], ALU.mult, ALU.add, reads=["cv_pre", ckey, "cv_acc"], writes=["cv_acc"])
        av = acc[:, 256:T].rearrange("p (s q) -> p s q", q=256)[:, :, 0]
        pv = pre[:, 256:T].rearrange("p (s q) -> p s q", q=256)[:, :, 0]
        k.stt(av, pv, cw[:, ci, 4:5], av, ALU.mult, ALU.add, reads=["cv_pre", ckey, "cv_acc"], writes=["cv_acc"])
        av2 = acc[:, 255:T - 1].rearrange("p (s q) -> p s q", q=256)[:, :, 0]
        pv2 = pre[:, 257:T + 1].rearrange("p (s q) -> p s q", q=256)[:, :, 0]
        k.stt(av2, pv2, cw[:, ci, 5:6], av2, ALU.mult, ALU.add, reads=["cv_pre", ckey, "cv_acc"], writes=["cv_acc"])
        if func is None:
            k.cp(dst, acc[:], reads=["cv_acc"], writes=[dkey], eng="dve")
        else:
            k.act(dst, acc[:], func, reads=["cv_acc"], writes=[dkey])

    def to_tok(src, skey, nchunk, dst, dkey):
        for t in range(NT):
            b, bk = bank()
            bt = b[:].bitcast(BF16)
            for c in range(nchunk):
                k.tr(bt[:, c * 128:(c + 1) * 128], src[:, c, t * 128:(t + 1) * 128], IDB, reads=[skey, "cstb"], writes=[bk])
            k.cp(dst[:, t, :], bt[:, 0:nchunk * 128], reads=[bk], writes=[dkey])

    def head_cols(dst, dkey, src_row):
        with nc.allow_non_contiguous_dma(reason="tiny"):
            for half in range(2):
                k.dma("sp", dst[half * 64:(half + 1) * 64, :],
                      src_row.rearrange("o (pr hf) -> o hf pr", hf=2)[:, half].partition_broadcast(64), writes=[dkey], chan="c0")

    SSD0 = 2832

    def ssd_mixer(l, st):
        omix = k.sb("s_omix", [128, 2, T], BF16, stack=st)
        bmT = k.sb("s_bmT", [128, 2, T], BF16, stack=st)
        cmT = k.sb("s_cmT", [128, 2, T], BF16, stack=st)
        zT = k.sb("s_zT", [128, 2, T], BF16, stack=st)
        xtok = k.sb("s_xtok", [128, NT, 256], BF16, stack=st)
        btok = k.sb("s_btok", [128, NT, 256], BF16, stack=st)
        dt = k.sb("s_dt", [128, NT, 8], stack=st)
        g = k.sb("s_g", [128, NT, 8], stack=st)
        lndt = k.sb("s_lndt", [128, NT, 8], stack=st)
        one_c = k.sb("s_one", [128, 1], stack=st)
        k.op("dve", lambda e: e.memset(one_c[:], 1.0), writes=["s_one"])
        with k.scope() as st2:
            alloc_stg(st2)
            wt, wkey = k.sb("s_w", [128, 8, 1032], BF16, stack=st2), "s_w"
            xsT = k.sb("s_xsT", [128, 2, T], BF16, stack=st2)
            pre = k.sb("cv_pre", [128, T + 2], stack=st2)
            acc = k.sb("cv_acc", [128, T], stack=st2)
            cw = k.sb("s_cw", [128, 6, 6], stack=st2)
            dtb = k.sb("s_dtb", [128, 8], stack=st2)
            nA = k.sb("s_nA", [128, 8], stack=st2)
            load_w(wt, wkey, W["w_in"][l, :, SSD0:SSD0 + 1032], 8, 1032)
            load_conv_params(cw, "s_cw", "ssd_conv_w", "ssd_conv_b", l, 6, st2)
            k.op("dve", lambda e: e.memset(pre[:, 0:1], 0.0), writes=["cv_pre"])
            k.op("dve", lambda e: e.memset(pre[:, T + 1:T + 2], 0.0), writes=["cv_pre"])
            k.dma("sp", dtb[:], W["ssd_dt_bias"][l:l + 1].rearrange("o d h -> o (d h)").partition_broadcast(128), writes=["s_dtb"], chan="c0")
            k.dma("sp", nA[:], W["ssd_A_log"][l:l + 1].rearrange("o d h -> o (d h)").partition_broadcast(128), writes=["s_nA"], chan="c0")
            k.act(nA[:], nA[:], AF.Exp, reads=["s_nA"], writes=["s_nA"])
            k.ts(nA[:], nA[:], -1.0, ALU.mult, reads=["s_nA"], writes=["s_nA"])
            for ci, (dst, dkey) in enumerate([(xsT[:, 0, :], "s_xsT"), (xsT[:, 1, :], "s_xsT"), (bmT[:, 0, :], "s_bmT"),
                                               (bmT[:, 1, :], "s_bmT"), (cmT[:, 0, :], "s_cmT"), (cmT[:, 1, :], "s_cmT")]):
                conv_chunk(wt, wkey, ci * 128, cw, "s_cw", ci, pre, acc, dst, dkey, AF.Silu)
            for pr in range(2):
                proj_fm(wt, wkey, 768 + pr * 128, 128,
                        lambda tb, b, bk, pr=pr: k.act(zT[:, pr, tb * 512:(tb + 1) * 512], b[:], AF.Silu, reads=[bk], writes=["s_zT"]))
            to_tok(xsT, "s_xsT", 2, xtok, "s_xtok")
            to_tok(bmT, "s_bmT", 2, btok, "s_btok")
            b, bk = bank()
            for t in range(NT):
                for kc in range(8):
                    k.mm(b[:, t * 8:(t + 1) * 8], hT[:, kc, t * 128:(t + 1) * 128], wt[:, kc, 1024:1032], kc == 0, kc == 7,
                         reads=[wkey, ("hT", t // 4)], writes=[bk])
            k.tt(dt[:], b[:, 0:NT * 8].rearrange("p (t e) -> p t e", e=8), dtb[:].unsqueeze(1).to_broadcast([128, NT, 8]), ALU.add,
                 reads=[bk, "s_dtb"], writes=["s_dt"])
            k.act(dt[:], dt[:], AF.Exp, reads=["s_dt"], writes=["s_dt"])
            k.act(dt[:], dt[:], AF.Ln, reads=["s_dt", "s_one"], writes=["s_dt"], bias=one_c[:, 0:1])
            k.ts(dt[:], dt[:], 1e-30, ALU.max, reads=["s_dt"], writes=["s_dt"])
            k.act(lndt[:], dt[:], AF.Ln, reads=["s_dt"], writes=["s_lndt"])
            k.tt(g[:], dt[:], nA[:].unsqueeze(1).to_broadcast([128, NT, 8]), ALU.mult, reads=["s_dt", "s_nA"], writes=["s_g"])
        if "stop:proj" in dbg:
            return
        with k.scope() as st3:
            ssd_scan(l, st3, omix, bmT, cmT, zT, xtok, btok, dt, g, lndt)
        if "stop:state" in dbg:
            return
        out_proj(l, omix, "s_omix", 2, 768, 16)

    def decay_tile(n, g, gkey, pfx, cumt, tot_sb):
        b, bk = bank()
        k.mm(b[:, 0:4], C("triU"), g[:, n, 0:4], True, True, reads=["cst", gkey], writes=[bk])
        k.mm(b[:, 4:8], C("triL"), g[:, n, 4:8], True, True, reads=["cst", gkey], writes=[bk])
        k.mm(b[:, 8:16], C("ones"), g[:, n, :], True, True, reads=["cst", gkey], writes=[bk])
        k.cp(cumt[:, n, :], b[:, 0:8], reads=[bk], writes=[pfx + "cumt"], eng="dve")
        k.cp(tot_sb[:, n, :], b[:, 8:16], reads=[bk], writes=[pfx + "tot"], eng="dve")

    def cumB_tile(n, g, gkey, pfx, GU):
        cb = []
        for d in range(2):
            k.tt(GU[:, 0], g[:, n, d * 4:(d + 1) * 4].unsqueeze(2).to_broadcast([128, 4, 128]),
                 C("triU" if d == 0 else "triL").unsqueeze(1).to_broadcast([128, 4, 128]), ALU.mult,
                 reads=[gkey, "cst"], writes=[pfx + "GU"])
            b, bk = bank()
            k.mm(b[:], C("ones"), GU[:, 0].rearrange("p h i -> p (h i)"), True, True, reads=["cst", pfx + "GU"], writes=[bk])
            cb.append((b, bk))
        return cb

    def ssd_scan(l, st, omix, bmT, cmT, zT, xtok, btok, dt, g, lndt):
        Sf = k.sb("s_Sf", [128, NT, 256], BF16, stack=st)
        Sb = k.sb("s_Sb", [128, NT, 256], BF16, stack=st)
        S32 = k.sb("s_S32", [128, 2, 256], stack=st)
        sst = [k.sb(f"s_sst{i}", [128, 256], stack=st) for i in range(2)]
        Dall = k.sb("s_Dall", [128, 4], stack=st)
        idD = k.sb("s_idD", [128, 4, 128], BF16, stack=st)
        cumt = k.sb("s_cumt", [128, NT, 8], stack=st)
        tot = k.sb("s_tot", [128, NT, 8], stack=st)
        wgt = k.sb("s_wgt", [128, NT, 8], stack=st)
        aex = k.sb("s_aex", [128, NT, 8], stack=st)
        bias1 = k.sb("s_bias1", [128, NT, 8], stack=st)
        GU = k.sb("s_GU", [128, 1, 4, 128], stack=st)
        tmpD = k.sb("s_tmpD", [128, 4, 128], stack=st)
        decm = k.sb("s_decm", [128, 2, 4, 128], BF16, stack=st)
        dec2 = k.sb("s_dec2", [128, 4, 128], BF16, stack=st)
        att = k.sb("s_att", [128, 4, 128], BF16, stack=st)
        EE = k.sb("s_EE", [128, 4, 128], BF16, stack=st)
        cE = k.sb("s_cE", [128, 2, 4, 128], BF16, stack=st)
        xsd = k.sb("s_xsd", [128, 4, 64], BF16, stack=st)
        o32 = k.sb("s_o32", [128, 2, 512], stack=st)
        y1 = k.sb("s_y1", [128, 512], stack=st)
        sq = k.sb("s_sq", [128, 512], BF16, stack=st)
        rs = k.sb("s_rs", [128, 512], stack=st)
        nwc = k.sb("s_nwc", [128, 2], stack=st)
        k.dma("sp", Dall[:], W["ssd_D"][l:l + 1].partition_broadcast(128), writes=["s_Dall"], chan="c0")
        for h in range(4):
            k.ts(idD[:, h, :], C("ident"), Dall[:, h:h + 1], ALU.mult, reads=["cst", "s_Dall"], writes=["s_idD"])
        with nc.allow_non_contiguous_dma(reason="tiny"):
            k.dma("sp", nwc[:], W["ssd_norm_w"][l].rearrange("(c p) -> p c", p=128), writes=["s_nwc"], chan="c0")
        for n in range(NT):
            decay_tile(n, g, "s_g", "s_", cumt, tot)
        k.tt(wgt[:], tot[:], cumt[:], ALU.subtract, reads=["s_tot", "s_cumt"], writes=["s_wgt"])
        k.act(wgt[:], wgt[:], AF.Exp, reads=["s_wgt"], writes=["s_wgt"])
        k.tt(wgt[:], wgt[:], dt[:], ALU.mult, reads=["s_wgt", "s_dt"], writes=["s_wgt"])
        k.act(aex[:], tot[:], AF.Exp, reads=["s_tot"], writes=["s_aex"])
        k.tt(bias1[:], lndt[:], cumt[:], ALU.subtract, reads=["s_lndt", "s_cumt"], writes=["s_bias1"])
        xsds = [xsd, k.sb("s_xsd1", [128, 4, 64], BF16, stack=st)]

        def ssd_pass(d):
            xsd = xsds[d]
            xk = f"s_xsd{d}"
            order = list(range(NT)) if d == 0 else list(range(NT - 1, -1, -1))
            snap = Sf if d == 0 else Sb
            snk = "s_Sf" if d == 0 else "s_Sb"
            skey = f"s_S32_{d}"
            k.dma("sp", S32[:, d, :].rearrange("p (h v) -> p h v", v=64), sssd_d[l, d].rearrange("h n v -> n h v"), writes=[skey], chan="c0")
            for idx, n in enumerate(order):
                if idx > 0 and idx % 2 == 0:
                    k.ts(S32[:, d, :], S32[:, d, :], mfl[:, 0:1], ALU.mult, reads=[skey, "mfl"], writes=[skey])
                k.cp(snap[:, n, :], S32[:, d, :], reads=[skey], writes=[snk])
                k.tt(xsd[:], xtok[:, n, :].rearrange("p (h v) -> p h v", v=64),
                     wgt[:, n, d * 4:(d + 1) * 4].unsqueeze(2).to_broadcast([128, 4, 64]), ALU.mult,
                     reads=["s_xtok", "s_wgt"], writes=[xk])
                b2, bk2 = bank()
                for h in range(4):
                    gi = h // 2
                    k.mm(b2[:, h * 64:(h + 1) * 64], btok[:, n, gi * 128:(gi + 1) * 128], xsd[:, h, :], True, True,
                         reads=["s_btok", xk], writes=[bk2])
                k.tt(S32[:, d, :].rearrange("p (h v) -> p h v", v=64), S32[:, d, :].rearrange("p (h v) -> p h v", v=64),
                     aex[:, n, d * 4:(d + 1) * 4].unsqueeze(2).to_broadcast([128, 4, 64]), ALU.mult, reads=[skey, "s_aex"], writes=[skey])
                k.tt(S32[:, d, :], S32[:, d, :], b2[:, 0:256], ALU.add, reads=[skey, bk2], writes=[skey])
                if idx % 2 == 1:
                    si = (n // 2) % 2
                    k.cp(sst[si][:], S32[:, d, :], reads=[skey], writes=[f"s_sst{si}"])
                    with nc.allow_non_contiguous_dma(reason="state store"):
                        k.dma("sp", nsssd_d[n // 2, l, d].rearrange("h n v -> n h v"), sst[si][:].rearrange("p (h v) -> p h v", v=64),
                              reads=[f"s_sst{si}"], writes=["nsssd"], chan=f"s_sst{si}")
                yield
        gens = [ssd_pass(0), ssd_pass(1)]
        while gens:
            for gg in list(gens):
                try:
                    next(gg)
                except StopIteration:
                    gens.remove(gg)
        if "stop:state" in dbg:
            return
        PD = 2
        decms = [decm] + [k.sb(f"s_decm{i}", [128, 2, 4, 128], BF16, stack=st) for i in range(1, PD)]
        dec2s = [dec2] + [k.sb(f"s_dec2{i}", [128, 4, 128], BF16, stack=st) for i in range(1, PD)]
        atts = [att] + [k.sb(f"s_att{i}", [128, 4, 128], BF16, stack=st) for i in range(1, PD)]
        EEs = [EE] + [k.sb(f"s_EE{i}", [128, 4, 128], BF16, stack=st) for i in range(1, PD)]
        cEs = [cE] + [k.sb(f"s_cE{i}", [128, 2, 4, 128], BF16, stack=st) for i in range(1, PD)]
        tmpDs = [tmpD] + [k.sb(f"s_tmpD{i}", [128, 4, 128], stack=st) for i in range(1, PD)]

        def ssd_tile(item, slot):
            tb, t4 = item
            n = tb * 4 + t4
            tsl = slice(n * 128, (n + 1) * 128)
            decm, dec2, att, EE, cE, tmpD = decms[slot], dec2s[slot], atts[slot], EEs[slot], cEs[slot], tmpDs[slot]
            sx = f"_{slot}"
            for d in range(2):
                k.tt(GU[:, 0], g[:, n, d * 4:(d + 1) * 4].unsqueeze(2).to_broadcast([128, 4, 128]),
                     C("triU" if d == 0 else "triL").unsqueeze(1).to_broadcast([128, 4, 128]), ALU.mult,
                     reads=["s_g", "cst"], writes=["s_GU"])
                b, bk = bank()
                k.mm(b[:], C("ones"), GU[:, 0].rearrange("p h i -> p (h i)"), True, True, reads=["cst", "s_GU"], writes=[bk])
                yield
                k.tt(tmpD[:], b[:].rearrange("p (h i) -> p h i", i=128),
                     C("neg_f" if d == 0 else "neg_b").unsqueeze(1).to_broadcast([128, 4, 128]), ALU.add,
                     reads=[bk, "cst"], writes=["s_tmpD" + sx])
                k.act(EE[:], b[:].rearrange("p (h i) -> p h i", i=128), AF.Exp, reads=[bk, "s_tmpD" + sx], writes=["s_EE" + sx])
                yield
                for h in range(4):
                    k.act(decm[:, d, h, :], tmpD[:, h, :], AF.Exp, reads=["s_tmpD" + sx, "s_bias1"], writes=["s_decm" + sx],
                          bias=bias1[:, n, d * 4 + h:d * 4 + h + 1])
                k.tt(cE[:, d].rearrange("p (g h) i -> p g h i", h=2), cmT[:, :, tsl].unsqueeze(2).to_broadcast([128, 2, 2, 128]),
                     EE[:].rearrange("p (g h) i -> p g h i", h=2), ALU.mult, reads=["s_cmT", "s_EE" + sx], writes=["s_cE" + sx])
                yield
            k.tt(dec2[:], decm[:, 0], decm[:, 1], ALU.add, reads=["s_decm" + sx], writes=["s_dec2" + sx], eng="pool")
            b, bk = bank()
            for gi in range(2):
                k.mm(b[:, gi * 128:(gi + 1) * 128], bmT[:, gi, tsl], cmT[:, gi, tsl], True, True, reads=["s_bmT", "s_cmT"], writes=[bk])
            yield
            k.tt(att[:].rearrange("p (g h) i -> p g h i", h=2),
                 b[:, 0:256].rearrange("p (g i) -> p g i", i=128).unsqueeze(2).to_broadcast([128, 2, 2, 128]),
                 dec2[:].rearrange("p (g h) i -> p g h i", h=2), ALU.mult, reads=[bk, "s_dec2" + sx], writes=["s_att" + sx])
            k.tt(att[:], att[:], idD[:], ALU.add, reads=["s_att" + sx, "s_idD"], writes=["s_att" + sx])
            yield
            b2, bk2 = bank()
            for h in range(4):
                pb = (h % 2) * 64
                pr = h // 2
                o_ap = b2[pb:pb + 64, pr * 128:(pr + 1) * 128]
                k.mm(o_ap, xtok[:, n, h * 64:(h + 1) * 64], att[:, h, :], True, False, reads=["s_xtok", "s_att" + sx], writes=[bk2])
                k.mm(o_ap, Sf[:, n, h * 64:(h + 1) * 64], cE[:, 0, h, :], False, False, reads=["s_Sf", "s_cE" + sx], writes=[bk2])
                k.mm(o_ap, Sb[:, n, h * 64:(h + 1) * 64], cE[:, 1, h, :], False, True, reads=["s_Sb", "s_cE" + sx], writes=[bk2])
            yield
            k.cp(o32[:, :, t4 * 128:(t4 + 1) * 128], b2[:, 0:256].rearrange("p (pr i) -> p pr i", i=128), reads=[bk2], writes=[("s_o32", t4)])

        for tb in range(NB):
            run_pipelined(ssd_tile, [(tb, t4) for t4 in range(4)], PD)
            okeys = [("s_o32", j) for j in range(4)]
            sl = slice(tb * 512, (tb + 1) * 512)
            for pr in range(2):
                k.tt(y1[:], o32[:, pr, :], zT[:, pr, sl], ALU.mult, reads=okeys + ["s_zT"], writes=["s_y1"])
                k.act(sq[:], y1[:], AF.Square, reads=["s_y1"], writes=["s_sq"])
                b, bk = bank()
                k.mm(b[:], ONESB, sq[:], True, True, reads=["s_sq", "cstb"], writes=[bk])
                k.act(rs[:], b[:], AF.Ln, reads=[bk, "epsc"], writes=["s_rs"], bias=epsc[:, 0:1], scale=1.0 / 128)
                k.act(rs[:], rs[:], AF.Exp, reads=["s_rs"], writes=["s_rs"], scale=-0.5)
                k.stt(omix[:, pr, sl], y1[:], nwc[:, pr:pr + 1], rs[:], ALU.mult, ALU.mult,
                      reads=["s_y1", "s_nwc", "s_rs"], writes=["s_omix"])
        if "ssd_o" in dbg:
            dbg_d["ssd_o"] = k.dram("dbg_ssd_o", [128, 2, T], BF16, kind="ExternalOutput")
            k.dma("sp", dbg_d["ssd_o"], omix[:], reads=["s_omix"], writes=["dbg_ssd_o"], chan="dbg")
            out_keys.append("dbg_ssd_o")

    GDN0 = 1792

    def gdn_mixer(l, st):
        omix = k.sb("g_omix", [128, 2, T], BF16, stack=st)
        qT = k.sb("g_qT", [128, 2, T], BF16, stack=st)
        kT = k.sb("g_kT", [128, 2, T], BF16, stack=st)
        zT = k.sb("g_zT", [128, 2, T], BF16, stack=st)
        ktok = k.sb("g_ktok", [128, NT, 256], BF16, stack=st)
        vtok = k.sb("g_vtok", [128, NT, 256], BF16, stack=st)
        g = k.sb("g_g", [128, NT, 8], stack=st)
        beta = k.sb("g_beta", [128, NT, 8], stack=st)
        one_c = k.sb("g_one", [128, 1], stack=st)
        k.op("dve", lambda e: e.memset(one_c[:], 1.0), writes=["g_one"])
        with k.scope() as st2:
            alloc_stg(st2)
            wt, wkey = k.sb("g_w", [128, 8, 1040], BF16, stack=st2), "g_w"
            vT = k.sb("g_vT", [128, 2, T], BF16, stack=st2)
            pre = k.sb("cv_pre", [128, T + 2], stack=st2)
            acc = k.sb("cv_acc", [128, T], stack=st2)
            xc = k.sb("g_xc", [128, T], BF16, stack=st2)
            cw = k.sb("g_cw", [128, 6, 6], stack=st2)
            sq = k.sb("g_sq", [128, 512], BF16, stack=st2)
            rs = acc[:, 0:512]
            dtb = k.sb("g_dtb", [128, 8], stack=st2)
            nA = k.sb("g_nA", [128, 8], stack=st2)
            load_w(wt, wkey, W["w_in"][l, :, GDN0:GDN0 + 1040], 8, 1040)
            load_conv_params(cw, "g_cw", "gdn_conv_w", None, l, 6, st2)
            k.op("dve", lambda e: e.memset(pre[:, 0:1], 0.0), writes=["cv_pre"])
            k.op("dve", lambda e: e.memset(pre[:, T + 1:T + 2], 0.0), writes=["cv_pre"])
            k.dma("sp", dtb[:], W["gdn_dt_bias"][l:l + 1].rearrange("o d h -> o (d h)").partition_broadcast(128), writes=["g_dtb"], chan="c0")
            k.dma("sp", nA[:], W["gdn_A_log"][l:l + 1].rearrange("o d h -> o (d h)").partition_broadcast(128), writes=["g_nA"], chan="c0")
            k.act(nA[:], nA[:], AF.Exp, reads=["g_nA"], writes=["g_nA"])
            k.ts(nA[:], nA[:], -1.0, ALU.mult, reads=["g_nA"], writes=["g_nA"])
            for ci in range(6):
                if ci < 4:
                    conv_chunk(wt, wkey, ci * 128, cw, "g_cw", ci, pre, acc, xc[:], "g_xc", AF.Silu)
                    dstT, dkey = (qT, "g_qT") if ci < 2 else (kT, "g_kT")
                    scl = 0.125 if ci < 2 else 1.0
                    for tb in range(NB):
                        sl = slice(tb * 512, (tb + 1) * 512)
                        k.act(sq[:], xc[:, sl], AF.Square, reads=["g_xc"], writes=["g_sq"])
                        b, bk = bank()
                        k.mm(b[:], BLKB, sq[:], True, True, reads=["g_sq", "cstb"], writes=[bk])
                        k.act(rs, b[:], AF.Ln, reads=[bk, "epsc"], writes=["cv_acc"], bias=epsc[:, 0:1])
                        k.act(rs, rs, AF.Exp, reads=["cv_acc"], writes=["cv_acc"], scale=-0.5)
                        k.stt(dstT[:, ci % 2, sl], xc[:, sl], scl, rs, ALU.mult, ALU.mult, reads=["g_xc", "cv_acc"], writes=[dkey])
                else:
                    conv_chunk(wt, wkey, ci * 128, cw, "g_cw", ci, pre, acc, vT[:, ci - 4, :], "g_vT", AF.Silu)
            for pr in range(2):
                proj_fm(wt, wkey, 768 + pr * 128, 128,
                        lambda tb, b, bk, pr=pr: k.act(zT[:, pr, tb * 512:(tb + 1) * 512], b[:], AF.Silu, reads=[bk], writes=["g_zT"]))
            to_tok(kT, "g_kT", 2, ktok, "g_ktok")
            to_tok(vT, "g_vT", 2, vtok, "g_vtok")
            b, bk = bank()
            for t in range(NT):
                for kc in range(8):
                    k.mm(b[:, t * 16:(t + 1) * 16], hT[:, kc, t * 128:(t + 1) * 128], wt[:, kc, 1024:1040], kc == 0, kc == 7,
                         reads=[wkey, ("hT", t // 4)], writes=[bk])
            bv = b[:, 0:NT * 16].rearrange("p (t e) -> p t e", e=16)
            k.tt(g[:], bv[:, :, 0:8], dtb[:].unsqueeze(1).to_broadcast([128, NT, 8]), ALU.add, reads=[bk, "g_dtb"], writes=["g_g"])
            k.act(g[:], g[:], AF.Exp, reads=["g_g"], writes=["g_g"])
            k.act(g[:], g[:], AF.Ln, reads=["g_g", "g_one"], writes=["g_g"], bias=one_c[:, 0:1])
            k.tt(g[:], g[:], nA[:].unsqueeze(1).to_broadcast([128, NT, 8]), ALU.mult, reads=["g_g", "g_nA"], writes=["g_g"])
            k.act(beta[:], bv[:, :, 8:16], AF.Sigmoid, reads=[bk], writes=["g_beta"])
        if "stop:proj" in dbg:
            return
        with k.scope() as st3:
            gdn_scan(l, st3, omix, qT, kT, zT, ktok, vtok, g, beta)
        if "gdn_o" in dbg:
            dbg_d["gdn_o"] = k.dram("dbg_gdn_o", [128, 2, T], BF16, kind="ExternalOutput")
            k.dma("sp", dbg_d["gdn_o"], omix[:], reads=["g_omix"], writes=["dbg_gdn_o"], chan="dbg2")
            out_keys.append("dbg_gdn_o")
        out_proj(l, omix, "g_omix", 2, 512, 16)

    def gdn_scan(l, st, omix, qT, kT, zT, ktok, vtok, g, beta):
        c2 = k.sb("g_c2", [128, 8, 128], stack=st)
        k.dma("sp", c2[:], cst2_d, writes=["g_c2"], chan="c0")
        C2N = {"triU64": 0, "triL64": 1, "sel0": 2, "sel1": 3, "negT_f": 4, "negT_b": 5, "negS_f": 6, "negS_b": 7}

        def C2(nm):
            return c2[:, C2N[nm], :]
        cumt = k.sb("g_cumt", [128, NT, 8], stack=st)
        ncumt = k.sb("g_ncumt", [128, NT, 8], stack=st)
        town = k.sb("g_town", [128, NT, 8], stack=st)
        aexp = k.sb("g_aexp", [128, NT, 2, 8], stack=st)
        Aall = k.sb("g_Aall", [128, NT, 2, 4], stack=st)
        bec = k.sb("g_bec", [128, NT, 8], stack=st)
        nbeta = k.sb("g_nbeta", [128, NT, 8], stack=st)
        dkw = k.sb("g_dkw", [128, NT, 8], stack=st)
        GU = k.sb("g_GU", [128, 4, 128], stack=st)
        tmpD = k.sb("g_tmpD", [128, 4, 128], stack=st)
        decT = k.sb("g_decT", [128, 4, 128], BF16, stack=st)
        decS = k.sb("g_decS", [128, 4, 128], BF16, stack=st)
        EE = k.sb("g_EE", [128, 4, 128], BF16, stack=st)
        GDT = F32 if GDN_FP32 else BF16
        XY = [[k.sb(f"g_X{i}", [128, 4, 128], GDT, stack=st), k.sb(f"g_Y{i}", [128, 4, 128], GDT, stack=st)] for i in range(2)]
        TT = k.sb("g_TT", [128, 4, 128], GDT, stack=st)
        att = k.sb("g_att", [128, 4, 128], BF16, stack=st)
        qE = k.sb("g_qE", [128, 2, 128], BF16, stack=st)
        rk = k.sb("g_rk", [128, 4, 64], GDT, stack=st)
        rv = k.sb("g_rv", [128, 4, 64], GDT, stack=st)
        kd = k.sb("g_kd", [128, 4, 64], BF16, stack=st)
        WT = k.sb("g_WT", [128, 2, 128], BF16, stack=st)
        U32 = k.sb("g_U32", [128, 4, 64], stack=st)
        vnew = k.sb("g_vnew", [128, 4, 64], BF16, stack=st)
        S32 = k.sb("g_S32", [128, 2, 128], stack=st)
        Sbf = k.sb("g_Sbf", [128, 2, 128], BF16, stack=st)
        hflat = hT[:].rearrange("p c t -> p (c t)")
        hoff = [0]

        def carve(shape, dt):
            nel = int(np.prod(shape[1:]))
            nb16 = nel * (2 if dt == F32 else 1)
            ap = hflat[:, hoff[0]:hoff[0] + nb16]
            hoff[0] += nb16
            if dt == F32:
                ap = ap.bitcast(F32)
            if len(shape) == 3:
                return ap.rearrange("p (a b) -> p a b", b=shape[2])
            return ap

        class _V:
            def __init__(self, ap):
                self.ap = ap

            def __getitem__(self, idx):
                return self.ap[idx]
        B1 = [_V(carve([128, 4, 128], F32)), _V(carve([128, 4, 128], F32)),
              _V(carve([128, 4, 128], BF16)), _V(carve([128, 4, 128], BF16)), _V(carve([128, 4, 128], BF16)), _V(carve([128, 4, 128], BF16)),
              _V(carve([128, 2, 128], BF16)),
              [[_V(carve([128, 4, 128], GDT)), _V(carve([128, 4, 128], GDT))] for _ in range(2)],
              _V(carve([128, 4, 128], GDT)), _V(carve([128, 4, 64], GDT)), _V(carve([128, 4, 64], GDT)), _V(carve([128, 4, 64], BF16)),
              _V(carve([128, 2, 128], BF16)), _V(carve([128, 4, 64], F32)), _V(carve([128, 4, 64], BF16)), _V(carve([128, 2, 128], BF16))]
        assert hoff[0] <= 8 * T
        BUFS = [[GU, tmpD, decT, decS, EE, att, qE, XY, TT, rk, rv, kd, WT, U32, vnew, Sbf], B1]
        gst = [k.sb(f"g_sst{i}", [128, 128], stack=st) for i in range(2)]
        oacc = k.sb("g_oacc", [128, 2, T], stack=st)
        sq = k.sb("g_sq2", [128, 512], BF16, stack=st)
        rs = GU[:].rearrange("p h i -> p (h i)")
        t1 = tmpD[:].rearrange("p h i -> p (h i)")
        w8 = k.sb("g_w8", [128, 2], stack=st)
        with nc.allow_non_contiguous_dma(reason="tiny"):
            k.dma("sp", w8[:], W["gdn_norm_w"][l].rearrange("(c p) -> p c", p=128), writes=["g_w8"], chan="c0")
        k.ts(w8[:], w8[:], 8.0, ALU.mult, reads=["g_w8"], writes=["g_w8"])
        for n in range(NT):
            b, bk = bank()
            k.mm(b[:, 0:4], C2("triU64"), g[:, n, 0:4], True, True, reads=["g_c2", "g_g"], writes=[bk])
            k.mm(b[:, 4:8], C2("triL64"), g[:, n, 4:8], True, True, reads=["g_c2", "g_g"], writes=[bk])
            k.mm(b[:, 8:16], C("blk64"), g[:, n, :], True, True, reads=["cst", "g_g"], writes=[bk])
            k.mm(b[:, 16:24], C2("sel0"), g[:, n, :], True, True, reads=["g_c2", "g_g"], writes=[bk])
            k.mm(b[:, 24:32], C2("sel1"), g[:, n, :], True, True, reads=["g_c2", "g_g"], writes=[bk])
            k.cp(cumt[:, n, :], b[:, 0:8], reads=[bk], writes=["g_cumt"], eng="dve")
            k.cp(town[:, n, :], b[:, 8:16], reads=[bk], writes=["g_town"], eng="dve")
            k.act(aexp[:, n, :, :], b[:, 16:32].rearrange("p (c e) -> p c e", e=8), AF.Exp, reads=[bk, "g_town"], writes=["g_aexp"])
        k.ts(ncumt[:], cumt[:], -1.0, ALU.mult, reads=["g_cumt"], writes=["g_ncumt"])
        k.act(bec[:], cumt[:], AF.Exp, reads=["g_cumt"], writes=["g_bec"])
        k.tt(bec[:], bec[:], beta[:], ALU.mult, reads=["g_bec", "g_beta"], writes=["g_bec"])
        k.ts(nbeta[:], beta[:], -1.0, ALU.mult, reads=["g_beta"], writes=["g_nbeta"])
        k.tt(dkw[:], town[:], cumt[:], ALU.subtract, reads=["g_town", "g_cumt"], writes=["g_dkw"])
        k.act(dkw[:], dkw[:], AF.Exp, reads=["g_dkw"], writes=["g_dkw"])
        for c in range(2):
            for d in range(2):
                for hf in range(2):
                    k.cp(Aall[hf * 64:(hf + 1) * 64, :, c, d * 2:d * 2 + 2],
                         aexp[hf * 64:(hf + 1) * 64, :, c, d * 4:d * 4 + 4].rearrange("p n (pr hf) -> p n pr hf", hf=2)[:, :, :, hf],
                         reads=["g_aexp"], writes=["g_Aall"], eng="dve")
        if "gst:0" in dbg:
            return
        def dir_pass(d, B):
            GU, tmpD, decT, decS, EE, att, qE, XY, TT, rk, rv, kd, WT, U32, vnew, Sbf = B
            sfx = f"_{d}"
            order = list(range(NT)) if d == 0 else list(range(NT - 1, -1, -1))
            skey = "g_S32" + sfx
            with nc.allow_non_contiguous_dma(reason="state load"):
                k.dma("sp", S32[:, d, :].rearrange("p (pr v) -> p pr v", v=64),
                      sgdn_d[l, d].rearrange("(pr hf) kk v -> (hf kk) pr v", hf=2), writes=[skey], chan="c0")
            tri = "triU64" if d == 0 else "triL64"
            for idx, n in enumerate(order):
                tsl = slice(n * 128, (n + 1) * 128)
                if idx > 0 and idx % 2 == 0:
                    k.ts(S32[:, d, :], S32[:, d, :], mfl[:, 0:1], ALU.mult, reads=[skey, "mfl"], writes=[skey])
                k.tt(GU[:], g[:, n, d * 4:(d + 1) * 4].unsqueeze(2).to_broadcast([128, 4, 128]),
                     C2(tri).unsqueeze(1).to_broadcast([128, 4, 128]), ALU.mult, reads=["g_g", "g_c2"], writes=["g_GU" + sfx])
                cb, cbk = bank()
                k.mm(cb[:], C("ones"), GU[:].rearrange("p h i -> p (h i)"), True, True, reads=["cst", "g_GU" + sfx], writes=[cbk])
                cbv = cb[:].rearrange("p (h i) -> p h i", i=128)
                k.tt(tmpD[:], cbv, C2("negT_f" if d == 0 else "negT_b").unsqueeze(1).to_broadcast([128, 4, 128]), ALU.add,
                     reads=[cbk, "g_c2"], writes=["g_tmpD" + sfx])
                for h in range(4):
                    k.act(decT[:, h, :], tmpD[:, h, :], AF.Exp, reads=["g_tmpD" + sfx, "g_ncumt"], writes=["g_decT" + sfx],
                          bias=ncumt[:, n, d * 4 + h:d * 4 + h + 1])
                k.tt(tmpD[:], cbv, C2("negS_f" if d == 0 else "negS_b").unsqueeze(1).to_broadcast([128, 4, 128]), ALU.subtract,
                     reads=[cbk, "g_c2", "g_decT" + sfx], writes=["g_tmpD" + sfx])
                for h in range(4):
                    k.act(decS[:, h, :], tmpD[:, h, :], AF.Exp, reads=["g_tmpD" + sfx, "g_cumt"], writes=["g_decS" + sfx],
                          bias=cumt[:, n, d * 4 + h:d * 4 + h + 1], scale=-1.0)
                k.act(EE[:], cbv, AF.Exp, reads=[cbk], writes=["g_EE" + sfx])
                for pr in range(2):
                    for hf in range(2):
                        ps_ = slice(hf * 64, (hf + 1) * 64)
                        k.tt(qE[ps_, pr, :], qT[ps_, pr, tsl], EE[ps_, pr * 2 + hf, :], ALU.mult, reads=["g_qT", "g_EE" + sfx], writes=["g_qE" + sfx])
                if "gst:a" in dbg:
                    continue
                yield
                kkb = [bank(), bank()]
                qkb = [bank(), bank()]
                for h in (0, 2, 1, 3):
                    pb = (h % 2) * 64
                    pr = h // 2
                    k.mm(kkb[h % 2][0][:, pr * 128:(pr + 1) * 128], kT[pb:pb + 64, pr, tsl], kT[pb:pb + 64, pr, tsl], True, True,
                         reads=["g_kT"], writes=[kkb[h % 2][1]])
                    k.mm(qkb[h % 2][0][:, pr * 128:(pr + 1) * 128], kT[pb:pb + 64, pr, tsl], qT[pb:pb + 64, pr, tsl], True, True,
                         reads=["g_kT", "g_qT"], writes=[qkb[h % 2][1]])
                X0, Y0 = XY[0]
                for h in range(4):
                    pr = h // 2
                    k.stt(X0[:, h, :], kkb[h % 2][0][:, pr * 128:(pr + 1) * 128], nbeta[:, n, d * 4 + h:d * 4 + h + 1], decS[:, h, :],
                          ALU.mult, ALU.mult, reads=[kkb[h % 2][1], "g_nbeta", "g_decS" + sfx], writes=["g_X0" + sfx])
                for hf in range(2):
                    k.tt(att[:, hf::2, :], qkb[hf][0][:, 0:256].rearrange("p (h i) -> p h i", i=128), decT[:, hf::2, :], ALU.mult,
                         reads=[qkb[hf][1], "g_decT" + sfx], writes=["g_att" + sfx])
                if "gst:b" in dbg:
                    continue
                yield
                yb, ybk = bank()
                if GDN_FP32:
                    for h in range(4):
                        k.tr(yb[:, h * 128:(h + 1) * 128], X0[:, h, :], C("ident"), reads=["g_X0" + sfx, "cst"], writes=[ybk])
                    ybv = yb[:].rearrange("p (h i) -> p h i", i=128)
                else:
                    ybt = yb[:].bitcast(BF16)
                    for h in range(4):
                        k.tr(ybt[:, h * 128:(h + 1) * 128], X0[:, h, :], IDB, reads=["g_X0" + sfx, "cstb"], writes=[ybk])
                    ybv = ybt[:, 0:512].rearrange("p (h i) -> p h i", i=128)
                if "gst:c1" in dbg:
                    continue
                k.cp(Y0[:], ybv, reads=[ybk], writes=["g_Y0" + sfx])
                if "gst:c2" in dbg:
                    continue
                k.tt(TT[:], Y0[:], C("ident").unsqueeze(1).to_broadcast([128, 4, 128]), ALU.add, reads=["g_Y0" + sfx, "cst"], writes=["g_TT" + sfx])
                if "gst:c" in dbg:
                    continue
                cur = 0
                for lev in range(5):
                    Xp, Yp = XY[cur]
                    Xn, Yn = XY[1 - cur]
                    xk, yk, xnk, ynk = f"g_X{cur}" + sfx, f"g_Y{cur}" + sfx, f"g_X{1 - cur}" + sfx, f"g_Y{1 - cur}" + sfx
                    bx, bxk = bank()
                    by, byk = bank()
                    for h in range(4):
                        k.mm(bx[:, h * 128:(h + 1) * 128], Yp[:, h, :], Xp[:, h, :], True, True, reads=[xk, yk], writes=[bxk])
                    for h in range(4):
                        k.mm(by[:, h * 128:(h + 1) * 128], Xp[:, h, :], Yp[:, h, :], True, True, reads=[xk, yk], writes=[byk])
                    k.cp(Xn[:], bx[:].rearrange("p (h i) -> p h i", i=128), reads=[bxk], writes=[xnk])
                    k.cp(Yn[:], by[:].rearrange("p (h i) -> p h i", i=128), reads=[byk], writes=[ynk], eng="dve")
                    bt_, btk = bank()
                    for h in range(4):
                        k.mm(bt_[:, h * 128:(h + 1) * 128], Xn[:, h, :], TT[:, h, :], True, True, reads=[xnk, "g_TT" + sfx], writes=[btk])
                    k.tt(TT[:], TT[:], bt_[:].rearrange("p (h i) -> p h i", i=128), ALU.add, reads=["g_TT" + sfx, btk], writes=["g_TT" + sfx])
                    cur = 1 - cur
                    yield
                if "gst:d" in dbg:
                    continue
                kv = ktok[:, n, :].rearrange("p (h f) -> p h f", f=64)
                vv = vtok[:, n, :].rearrange("p (h f) -> p h f", f=64)
                bsl = slice(d * 4, d * 4 + 4)
                k.tt(rk[:], kv, bec[:, n, bsl].unsqueeze(2).to_broadcast([128, 4, 64]), ALU.mult, reads=["g_ktok", "g_bec"], writes=["g_rk" + sfx])
                k.tt(rv[:], vv, beta[:, n, bsl].unsqueeze(2).to_broadcast([128, 4, 64]), ALU.mult, reads=["g_vtok", "g_beta"], writes=["g_rv" + sfx])
                k.tt(kd[:], kv, dkw[:, n, bsl].unsqueeze(2).to_broadcast([128, 4, 64]), ALU.mult, reads=["g_ktok", "g_dkw"], writes=["g_kd" + sfx])
                bw, bwk = bank()
                bu, buk = bank()
                for h in range(4):
                    pb = (h % 2) * 64
                    pr = h // 2
                    k.mm(bw[pb:pb + 64, pr * 128:(pr + 1) * 128], rk[:, h, :], TT[:, h, :], True, True, reads=["g_rk" + sfx, "g_TT" + sfx], writes=[bwk])
                    k.mm(bu[:, h * 64:(h + 1) * 64], TT[:, h, :], rv[:, h, :], True, True, reads=["g_rv" + sfx, "g_TT" + sfx], writes=[buk])
                k.cp(WT[:], bw[:, 0:256].rearrange("p (pr i) -> p pr i", i=128), reads=[bwk], writes=["g_WT" + sfx])
                k.cp(U32[:], bu[:, 0:256].rearrange("p (h v) -> p h v", v=64), reads=[buk], writes=["g_U32" + sfx], eng="dve")
                if "gst:e" in dbg:
                    continue
                yield
                for c in ((0, 1) if d == 0 else (1, 0)):
                    cs = slice(c * 64, (c + 1) * 64)
                    k.cp(Sbf[:, c, :], S32[:, d, :], reads=[skey], writes=[f"g_Sbf{c}" + sfx])
                    vb = [bank(), bank()]
                    for h in (0, 2, 1, 3):
                        pb = (h % 2) * 64
                        pr = h // 2
                        k.mm(vb[h % 2][0][cs, pr * 64:(pr + 1) * 64], WT[pb:pb + 64, pr, c * 64:(c + 1) * 64],
                             Sbf[pb:pb + 64, c, pr * 64:(pr + 1) * 64], True, True, reads=["g_WT" + sfx, f"g_Sbf{c}" + sfx], writes=[vb[h % 2][1]])
                    for hf in range(2):
                        k.tt(vnew[cs, hf::2, :], U32[cs, hf::2, :], vb[hf][0][cs, 0:128].rearrange("p (pr v) -> p pr v", v=64), ALU.subtract,
                             reads=["g_U32" + sfx, vb[hf][1]], writes=["g_vnew" + sfx])
                    bs_, bsk = bank()
                    for h in range(4):
                        pb = (h % 2) * 64
                        pr = h // 2
                        k.mm(bs_[pb:pb + 64, pr * 64:(pr + 1) * 64], kd[cs, h, :], vnew[cs, h, :], True, True,
                             reads=["g_kd" + sfx, "g_vnew" + sfx], writes=[bsk])
                    k.tt(S32[:, d, :].rearrange("p (pr v) -> p pr v", v=64), S32[:, d, :].rearrange("p (pr v) -> p pr v", v=64),
                         Aall[:, n, c, d * 2:d * 2 + 2].unsqueeze(2).to_broadcast([128, 2, 64]), ALU.mult, reads=[skey, "g_Aall"], writes=[skey])
                    k.tt(S32[:, d, :], S32[:, d, :], bs_[:, 0:128], ALU.add, reads=[skey, bsk], writes=[skey])
                    yield
                if "gst:f" in dbg:
                    continue
                yield
                ob = [bank(), bank()]
                for h in (0, 2, 1, 3):
                    pb = (h % 2) * 64
                    pr = h // 2
                    o_ap = ob[h % 2][0][pb:pb + 64, pr * 128:(pr + 1) * 128]
                    k.mm(o_ap, vnew[:, h, :], att[:, h, :], True, False, reads=["g_vnew" + sfx, "g_att" + sfx], writes=[ob[h % 2][1]])
                    for c in range(2):
                        k.mm(ob[h % 2][0][pb:pb + 64, pr * 128 + c * 64:pr * 128 + (c + 1) * 64],
                             Sbf[pb:pb + 64, c, pr * 64:(pr + 1) * 64], qE[pb:pb + 64, pr, c * 64:(c + 1) * 64], False, c == 1,
                             reads=[f"g_Sbf{c}" + sfx, "g_qE" + sfx], writes=[ob[h % 2][1]])
                for hf in range(2):
                    ps_ = slice(hf * 64, (hf + 1) * 64)
                    src = ob[hf][0][ps_, 0:256].rearrange("p (pr i) -> p pr i", i=128)
                    k.tt(oacc[ps_, :, tsl], oacc[ps_, :, tsl], src, ALU.add, reads=[ob[hf][1], ("g_oacc", n)], writes=[("g_oacc", n)])
                if idx % 2 == 1:
                    si = (n // 2) % 2
                    k.cp(gst[si][:], S32[:, d, :], reads=[skey], writes=[f"g_sst{si}"])
                    with nc.allow_non_contiguous_dma(reason="state store"):
                        for pr in range(2):
                            k.dma("sp", nsgdn_d[n // 2, l, d, pr * 2:pr * 2 + 2].rearrange("hf kk v -> (hf kk) v"),
                                  gst[si][:, pr * 64:(pr + 1) * 64], reads=[f"g_sst{si}"], writes=["nsgdn"], chan=f"g_sst{si}")

        for n in range(NT):
            k.op("pool", lambda e, n=n: e.memset(oacc[:, :, n * 128:(n + 1) * 128], 0.0), writes=[("g_oacc", n)])
        gens = [dir_pass(0, BUFS[0]), dir_pass(1, BUFS[1])]
        while gens:
            for gg in list(gens):
                try:
                    next(gg)
                except StopIteration:
                    gens.remove(gg)
        for tb in range(NB):
            sl = slice(tb * 512, (tb + 1) * 512)
            for pr in range(2):
                okeys = [("g_oacc", tb * 4 + j) for j in range(4)]
                k.act(sq[:], oacc[:, pr, sl], AF.Square, reads=okeys, writes=["g_sq2"])
                b, bk = bank()
                k.mm(b[:], BLKB, sq[:], True, True, reads=["g_sq2", "cstb"], writes=[bk])
                k.act(rs, b[:], AF.Ln, reads=[bk, "epsc"], writes=["g_GU_0"], bias=epsc[:, 1:2], scale=1.0)
                k.act(rs, rs, AF.Exp, reads=["g_GU_0"], writes=["g_GU_0"], scale=-0.5)
                k.tt(t1, oacc[:, pr, sl], rs, ALU.mult, reads=okeys + ["g_GU_0"], writes=["g_tmpD_0"])
                k.stt(omix[:, pr, sl], t1, w8[:, pr:pr + 1], zT[:, pr, sl], ALU.mult, ALU.mult,
                      reads=["g_tmpD_0", "g_w8", "g_zT"], writes=["g_omix"])

    HY0 = 1024
    I32 = mybir.dt.int32

    def hy_mixer(l, st):
        omix = k.sb("h_omix", [128, 2, T], BF16, stack=st)
        x0T = k.sb("h_x0T", [128, 2, T], BF16, stack=st)
        uT = k.sb("h_uT", [128, 2, T], BF16, stack=st)
        utok = k.sb("h_utok", [128, NT, 256], BF16, stack=st)
        Atok = k.sb("h_Atok", [128, NT, 256], BF16, stack=st)
        Btok = k.sb("h_Btok", [128, NT, 256], BF16, stack=st)
        with k.scope() as st2:
            alloc_stg(st2)
            wt, wkey = k.sb("h_w", [128, 8, 768], BF16, stack=st2), "h_w"
            pre = k.sb("cv_pre", [128, T + 2], stack=st2)
            acc = k.sb("cv_acc", [128, T], stack=st2)
            x1T = k.sb("h_x1T", [128, 2, T], BF16, stack=st2)
            cw = k.sb("h_cw", [128, 6, 6], stack=st2)
            load_w(wt, wkey, W["w_in"][l, :, HY0:HY0 + 768], 8, 768)
            load_conv_params(cw, "h_cw", "hy_conv_w", "hy_conv_b", l, 6, st2)
            k.op("dve", lambda e: e.memset(pre[:, 0:1], 0.0), writes=["cv_pre"])
            k.op("dve", lambda e: e.memset(pre[:, T + 1:T + 2], 0.0), writes=["cv_pre"])
            for ci in range(6):
                dst, dkey = [(x0T, "h_x0T"), (x1T, "h_x1T"), (uT, "h_uT")][ci // 2]
                conv_chunk(wt, wkey, ci * 128, cw, "h_cw", ci, pre, acc, dst[:, ci % 2, :], dkey, None)
            k.tt(uT[:], uT[:], x1T[:], ALU.mult, reads=["h_uT", "h_x1T"], writes=["h_uT"])
            to_tok(uT, "h_uT", 2, utok, "h_utok")
        with k.scope() as st2:
            w1 = k.sb("h_w1", [33, 64], stack=st2)
            w2 = k.sb("h_w2", [64, 64], stack=st2)
            w3 = k.sb("h_w3", [64, 512], stack=st2)
            fcol = k.sb("h_fcol", [64, 5], stack=st2)
            zb = k.sb("h_zb", [33, 512], stack=st2)
            arg = k.sb("h_arg", [64, 512], stack=st2)
            ki = k.sb("h_ki", [64, 512], I32, stack=st2)
            h1 = k.sb("h_h1", [64, 512], stack=st2)
            h2 = k.sb("h_h2", [64, 512], stack=st2)
            dl = k.sb("h_dl", [128, 256], stack=st2)
            dec = k.sb("h_dec", [128, 256], stack=st2)
            tf = k.sb("h_tf", [128, 256], stack=st2)
            tb_ = k.sb("h_tb", [128, 256], stack=st2)
            tcol = k.sb("h_tcol", [128, 2, NT], stack=st2)
            k.dma("sp", w1[:], W["hy_w1"][l], writes=["h_w1"], chan="c0")
            k.dma("sp", w2[:], W["hy_w2"][l], writes=["h_w2"], chan="c0")
            k.dma("sp", w3[:], W["hy_w3"][l], writes=["h_w3"], chan="c0")
            with nc.allow_non_contiguous_dma(reason="tiny"):
                for j, nm in enumerate(["hy_freq", "hy_b1", "hy_b2"]):
                    k.dma("sp", fcol[:, j:j + 1], W[nm][l:l + 1].rearrange("o f -> f o"), writes=["h_fcol"], chan="c0")
            k.tt(fcol[:, 3:4], fcol[:, 1:2], fcol[:, 0:1], ALU.mult, reads=["h_fcol"], writes=["h_fcol"])
            k.tt(fcol[:, 4:5], fcol[:, 2:3], fcol[:, 0:1], ALU.mult, reads=["h_fcol"], writes=["h_fcol"])
            k.dma("sp", dl[:], hyd_d.partition_broadcast(128), writes=["h_dl"], chan="c0")
            k.dma("sp", tcol[:], hyt_d, writes=["h_tcol"], chan="c0")

            def sin_layer(dst, dkey, wmat, wk, src, skey, nk, bcol):
                b, bk = bank()
                k.mm(b[0:64, :], wmat[0:nk, :], src[0:nk, :], True, True, reads=[wk, skey], writes=[bk])
                k.act(arg[:], b[0:64, :], AF.Identity, reads=[bk, "h_fcol"], writes=["h_arg"], bias=fcol[:, bcol:bcol + 1], scale=fcol[:, 0:1])
                k.ts(ki[:], arg[:], 1.0 / (2 * math.pi), ALU.mult, reads=["h_arg"], writes=["h_ki"])
                k.stt(arg[:], ki[:], -2 * math.pi, arg[:], ALU.mult, ALU.add, reads=["h_ki", "h_arg"], writes=["h_arg"])
                k.act(dst[:], arg[:], AF.Sin, reads=["h_arg"], writes=[dkey])

            for tb in range(NB):
                k.dma("sp", zb[:], hyz_d[:, tb * 512:(tb + 1) * 512], writes=["h_zb"], chan="c0")
                sin_layer(h1, "h_h1", w1, "h_w1", zb, "h_zb", 33, 3)
                sin_layer(h2, "h_h2", w2, "h_w2", h1, "h_h1", 64, 4)
                for t4 in range(4):
                    n = tb * 4 + t4
                    b, bk = bank()
                    k.mm(b[:], h2[:, t4 * 128:(t4 + 1) * 128], w3[:], True, True, reads=["h_h2", "h_w3"], writes=[bk])
                    k.act(dec[:], dl[:], AF.Exp, reads=["h_dl", "h_tcol"], writes=["h_dec"], scale=tcol[:, 0, n:n + 1])
                    k.tt(tf[:], b[:, 0:256], dec[:], ALU.mult, reads=[bk, "h_dec"], writes=["h_tf"])
                    k.stt(tb_[:], b[:, 256:512], tcol[:, 1, n:n + 1], dec[:], ALU.mult, ALU.mult, reads=[bk, "h_dec", "h_tcol"], writes=["h_tb"])
                    k.tt(Atok[:, n, :], tf[:], tb_[:], ALU.add, reads=["h_tf", "h_tb"], writes=["h_Atok"])
                    k.tt(Btok[:, n, :], tf[:], tb_[:], ALU.subtract, reads=["h_tf", "h_tb"], writes=["h_Btok"], eng="pool")
        if "stop:proj" in dbg:
            return
        with k.scope() as st3:
            Ysb = k.sb("h_Ysb", [128, 32, 256], BF16, stack=st3)
            with k.scope() as st2:
                fw = [k.sb(f"h_fw{i}", [128, 2, 16, 128], BF16, stack=st2) for i in range(2)]
                sm = k.sb("h_sm", [128, 2, 16], stack=st2)
                sgn = k.sb("h_sgn", [128, NT], BF16, stack=st2)
                tn = k.sb("h_tn", [1, 256], stack=st2)
                Tre = k.sb("h_Tre", [128, 256], stack=st2)
                Tim = k.sb("h_Tim", [128, 256], stack=st2)
                Dd = k.sb("h_Dd", [128, 256], stack=st2)
                t1 = k.sb("h_t1", [128, 256], stack=st2)
                t2 = k.sb("h_t2", [128, 256], stack=st2)
                t3 = k.sb("h_t3", [128, 256], stack=st2)
                t4_ = k.sb("h_t4", [128, 256], stack=st2)
                k.dma("sp", sm[:], hys_d, writes=["h_sm"], chan="c0")
                k.dma("sp", sgn[:], hyg_d, writes=["h_sgn"], chan="c0")
                b, bk = bank()
                for tc in range(NT):
                    k.mm(b[0:1, 0:256], sgn[:, tc:tc + 1], Atok[:, tc, :], tc == 0, tc == NT - 1, reads=["h_sgn", "h_Atok"], writes=[bk])
                k.cp(tn[:], b[0:1, 0:256], reads=[bk], writes=["h_tn"], eng="dve")
                for fc in range(16):
                    f_, fk = fw[fc % 2], f"h_fw{fc % 2}"
                    k.dma("sp", f_[:, 0], ffwd_d[fc], writes=[fk], chan=fk)
                    k.dma("sp", f_[:, 1], ffwd_d[16 + fc], reads=[], writes=[fk + "b"], chan=fk + "b")
                    bU, bUk = bank()
                    bT, bTk = bank()
                    for part, (src, skey) in enumerate(((utok, "h_utok"), (utok, "h_utok"))):
                        for tc in range(NT):
                            k.mm(bU[:, part * 256:(part + 1) * 256], f_[:, part, tc, :], src[:, tc, :], tc == 0, tc == NT - 1,
                                 reads=[fk, fk + "b", skey], writes=[bUk], quiet=True)
                    for part, (src, skey) in enumerate(((Atok, "h_Atok"), (Btok, "h_Btok"))):
                        for tc in range(NT):
                            k.mm(bT[:, part * 256:(part + 1) * 256], f_[:, part, tc, :], src[:, tc, :], tc == 0, tc == NT - 1,
                                 reads=[fk, fk + "b", skey], writes=[bTk], quiet=True)
                    k.cp(Tre[:], bT[:, 0:256], reads=[bTk], writes=["h_Tre"])
                    k.act(Tim[:], bT[:, 256:512], AF.Copy, reads=[bTk, "h_sm"], writes=["h_Tim"], scale=sm[:, 0, fc:fc + 1])
                    k.cp(Dd[:], Tre[:], reads=["h_Tre"], writes=["h_Dd"], eng="pool")
                    k.ts(Dd[0:1, :], Dd[0:1, :], sm[0:1, 0, fc:fc + 1], ALU.mult, reads=["h_Dd", "h_sm"], writes=["h_Dd"])
                    k.stt(Dd[0:1, :], tn[0:1, :], sm[0:1, 1, fc:fc + 1], Dd[0:1, :], ALU.mult, ALU.add, reads=["h_tn", "h_sm", "h_Dd"], writes=["h_Dd"])
                    k.tt(t1[:], bU[:, 0:256], Tre[:], ALU.mult, reads=[bUk, "h_Tre"], writes=["h_t1"])
                    k.tt(t2[:], bU[:, 256:512], Tim[:], ALU.mult, reads=[bUk, "h_Tim"], writes=["h_t2"])
                    k.tt(Ysb[:, fc, :], t1[:], t2[:], ALU.subtract, reads=["h_t1", "h_t2"], writes=["h_Ysb"], eng="pool")
                    k.tt(t3[:], bU[:, 0:256], Tim[:], ALU.mult, reads=[bUk, "h_Tim"], writes=["h_t3"])
                    k.tt(t4_[:], bU[:, 256:512], Dd[:], ALU.mult, reads=[bUk, "h_Dd"], writes=["h_t4"])
                    k.tt(Ysb[:, 16 + fc, :], t3[:], t4_[:], ALU.add, reads=["h_t3", "h_t4"], writes=["h_Ysb"], eng="pool")
            with k.scope() as st2:
                fi = [k.sb(f"h_fi{i}", [128, 16, 512], BF16, stack=st2) for i in range(2)]
                y32 = k.sb("h_y32", [128, 2, 512], stack=st2)
                sq = k.sb("h_sq", [128, 512], BF16, stack=st2)
                rs = k.sb("h_rs", [128, 512], stack=st2)
                hb = k.sb("h_hb", [128, 2], stack=st2)
                nwc = k.sb("h_nwc", [128, 2], stack=st2)
                with nc.allow_non_contiguous_dma(reason="tiny"):
                    k.dma("sp", hb[:], W["hy_bias"][l].rearrange("(c p) -> p c", p=128), writes=["h_hb"], chan="c0")
                    k.dma("sp", nwc[:], W["hy_norm_w"][l].rearrange("(c p) -> p c", p=128), writes=["h_nwc"], chan="c0")
                for tb in range(NB):
                    sl = slice(tb * 512, (tb + 1) * 512)
                    for hf in range(2):
                        k.dma("sp", fi[hf][:], finv_d[tb, :, hf * 16:(hf + 1) * 16, :], writes=[f"h_fi{hf}"], chan=f"h_fi{hf}")
                    for ch in range(2):
                        b, bk = bank()
                        for fc in range(32):
                            k.mm(b[:], Ysb[:, fc, ch * 128:(ch + 1) * 128], fi[fc // 16][:, fc % 16, :], fc == 0, fc == 31,
                                 reads=["h_Ysb", f"h_fi{fc // 16}"], writes=[bk], quiet=True)
                        k.stt(y32[:, ch, :], uT[:, ch, sl], hb[:, ch:ch + 1], b[:], ALU.mult, ALU.add, reads=["h_uT", "h_hb", bk], writes=["h_y32"])
                        k.tt(y32[:, ch, :], y32[:, ch, :], x0T[:, ch, sl], ALU.mult, reads=["h_y32", "h_x0T"], writes=["h_y32"])
                    b, bk = bank()
                    for ch in range(2):
                        k.act(sq[:], y32[:, ch, :], AF.Square, reads=["h_y32"], writes=["h_sq"])
                        k.mm(b[:], ONESB, sq[:], ch == 0, ch == 1, reads=["h_sq", "cstb"], writes=[bk])
                    k.act(rs[:], b[:], AF.Ln, reads=[bk, "epsc"], writes=["h_rs"], bias=epsc[:, 0:1], scale=1.0 / 256)
                    k.act(rs[:], rs[:], AF.Exp, reads=["h_rs"], writes=["h_rs"], scale=-0.5)
                    for ch in range(2):
                        k.stt(omix[:, ch, sl], y32[:, ch, :], nwc[:, ch:ch + 1], rs[:], ALU.mult, ALU.mult,
                              reads=["h_y32", "h_nwc", "h_rs"], writes=["h_omix"])
        if "hy_o" in dbg:
            dbg_d["hy_o"] = k.dram("dbg_hy_o", [128, 2, T], BF16, kind="ExternalOutput")
            k.dma("sp", dbg_d["hy_o"], omix[:], reads=["h_omix"], writes=["dbg_hy_o"], chan="dbg3")
            out_keys.append("dbg_hy_o")
        out_proj(l, omix, "h_omix", 2, 256, 16)

    def mlp(l, st):
        hid = [k.sb(f"m_hid{i}", [128, 8, 512], BF16, stack=st) for i in range(2)]
        alloc_stg(st)
        wb = [k.sb(f"wb{i}", [128, 8, 1024], BF16, stack=st) for i in range(4)]
        rlu = [k.sb(f"m_rl{i}", [128, 512], BF16, stack=st) for i in range(2)]
        hi = 0
        def ld_gen(q):
            yield from load_w_gen(wb[(2 * q) % 4], f"wb{(2 * q) % 4}", W["mlp_w1"][l, :, q * 1024:(q + 1) * 1024], 8, 1024)
            yield from load_w_gen(wb[(2 * q + 1) % 4], f"wb{(2 * q + 1) % 4}", W["mlp_w2"][l, q * 1024:(q + 1) * 1024, :], 8, 1024)

        def pump(gen):
            if gen is not None:
                try:
                    next(gen)
                except StopIteration:
                    pass

        for _ in ld_gen(0):
            pass
        for q in range(4):
            w1, w1k = wb[(2 * q) % 4], f"wb{(2 * q) % 4}"
            w2, w2k = wb[(2 * q + 1) % 4], f"wb{(2 * q + 1) % 4}"
            nxt = ld_gen(q + 1) if q + 1 < 4 else None
            for tb in range(NB):
                sl = slice(tb * 512, (tb + 1) * 512)
                hd, hk = hid[hi % 2], f"m_hid{hi % 2}"
                hi += 1
                for hc in range(8):
                    b, bk = bank()
                    for kc in range(8):
                        k.mm(b[:], w1[:, kc, hc * 128:(hc + 1) * 128], hT[:, kc, sl], kc == 0, kc == 7,
                             reads=[w1k, ("hT", tb)], writes=[bk], quiet=True)
                    rl, rk = rlu[hc % 2], f"m_rl{hc % 2}"
                    k.act(rl[:], b[:], AF.Relu, reads=[bk], writes=[rk])
                    k.tt(hd[:, hc, :], rl[:], rl[:], ALU.mult, reads=[rk], writes=[hk])
                    if hc % 2 == 1:
                        pump(nxt)
                for dc in range(8):
                    b, bk = bank()
                    for kc in range(8):
                        k.mm(b[:], w2[:, kc, dc * 128:(dc + 1) * 128], hd[:, kc, :], kc == 0, kc == 7, reads=[w2k, hk], writes=[bk], quiet=True)
                    k.stt(xT[:, dc, sl], b[:], modT[:, l, 40 + dc:41 + dc], xT[:, dc, sl], ALU.mult, ALU.add,
                          reads=[bk, "modT", ("xT", tb)], writes=[("xT", tb)])
                    if dc % 2 == 1:
                        pump(nxt)
            if nxt is not None:
                for _ in nxt:
                    pass

    for l in range(depth):
        with k.scope() as st:
            rmsnorm_mod(l, 0, st)
        if "h1" in dbg and l == 0:
            dbg_d["h1"] = k.dram("dbg_h1", [128, 8, T], BF16, kind="ExternalOutput")
            k.dma("sp", dbg_d["h1"], hT[:], reads=[("hT", i) for i in range(4)], writes=["dbg_h1"], chan="dbg")
            out_keys.append("dbg_h1")
        if "ret" in mixers:
            with k.scope() as st:
                ret_mixer(l, st)
        if "hy" in mixers:
            with k.scope() as st:
                hy_mixer(l, st)
        if "ssd" in mixers:
            with k.scope() as st:
                ssd_mixer(l, st)
        if "gdn" in mixers:
            with k.scope() as st:
                gdn_mixer(l, st)
        with k.scope() as st:
            rmsnorm_mod(l, 1, st)
        with k.scope() as st:
            mlp(l, st)

    with k.scope() as st:
        sq = k.sb("f_sq", [128, 512], BF16, stack=st)
        rstd = k.sb("f_rstd", [128, 512], stack=st)
        tmp = k.sb("f_tmp", [128, 8, 512], stack=st)
        yt = [k.sb(f"f_y{i}", [128, 1024], stack=st) for i in range(2)]
        yi = 0
        for tb in range(NB):
            sl = slice(tb * 512, (tb + 1) * 512)
            b, bk = bank()
            for c in range(8):
                k.act(sq[:], xT[:, c, sl], AF.Square, reads=[("xT", tb)], writes=["f_sq"])
                k.mm(b[:], ONESB, sq[:], c == 0, c == 7, reads=["f_sq", "cstb"], writes=[bk])
            k.act(rstd[:], b[:], AF.Ln, reads=[bk, "epsc"], writes=["f_rstd"], bias=epsc[:, 0:1], scale=1.0 / D)
            k.act(rstd[:], rstd[:], AF.Exp, reads=["f_rstd"], writes=["f_rstd"], scale=-0.5)
            for c in range(8):
                k.stt(tmp[:, c, :], xT[:, c, sl], fnw[:, c:c + 1], rstd[:], ALU.mult, ALU.mult,
                      reads=[("xT", tb), "fnw", "f_rstd"], writes=["f_tmp"])
            for t4 in range(4):
                t = tb * 4 + t4
                y, yk = yt[yi % 2], f"f_y{yi % 2}"
                yi += 1
                for half in range(2):
                    b2, bk2 = bank()
                    for c4 in range(4):
                        c = half * 4 + c4
                        k.tr(b2[:, c4 * 128:(c4 + 1) * 128], tmp[:, c, t4 * 128:(t4 + 1) * 128], C("ident"),
                             reads=["f_tmp", "cst"], writes=[bk2])
                    k.cp(y[:, half * 512:(half + 1) * 512], b2[:], reads=[bk2], writes=[yk])
                k.dma("sp", y_d[t * 128:(t + 1) * 128, :], y[:], reads=[yk], writes=["y_out"], chan=yk)
    out_keys += ["y_out", "nsret", "nsssd", "nsgdn"]
    k._deps("sp", out_keys + [f"f_y0", "f_y1"], [])
    k.barrier()
    return k, dbg_d


def _rope_tables(prompt):
    if prompt:
        cos = np.ones((T, 32), np.float32)
        sin = np.zeros((T, 32), np.float32)
    else:
        rows = T // 64
        r = np.repeat(np.arange(rows), 64).astype(np.float32)
        col = np.tile(np.arange(64), rows).astype(np.float32)
        inv = (10000.0 ** (-np.arange(16, dtype=np.float32) / 16)).astype(np.float32)
        ang = np.concatenate([r[:, None] * inv, col[:, None] * inv], axis=-1).astype(np.float32)
        cos, sin = np.cos(ang), np.sin(ang)
    tab = np.zeros((128, 2, T), np.float32)
    for g in range(4):
        tab[g * 32:(g + 1) * 32, 0, :] = cos.T
        tab[g * 32:(g + 1) * 32, 1, :] = sin.T * (-1.0 if g % 2 == 0 else 1.0)
    return tab.astype(ml_dtypes.bfloat16)


_HY = {}


def _hyena_tables(prompt):
    if prompt in _HY:
        return _HY[prompt]
    L = 256 if prompt else 2048
    N = 2 * L
    nseg = T // L
    pos = np.arange(T) % L
    seg = np.arange(T) // L
    tlin = np.linspace(0.0, 1.0, L, dtype=np.float32)[pos]
    w = (2.0 * np.pi * np.arange(L, dtype=np.float32) / L).astype(np.float32)[pos]
    bands = 16
    fb = np.linspace(1e-4, bands - 1, bands, dtype=np.float32)
    z = np.concatenate([tlin[:, None], np.cos(fb[None, :] * w[:, None]), -np.sin(fb[None, :] * w[:, None])], axis=1)
    deltas = np.abs(np.linspace(math.log(1e-2) / 1.5, math.log(1e-2) / 0.3, 256, dtype=np.float32))
    tok = lambda a: np.ascontiguousarray(a.reshape(NT, 128).T)
    hyt = np.stack([tok(-tlin), tok((pos != 0).astype(np.float32))], axis=1).astype(np.float32)
    sgn = np.where(seg == 0, (-1.0) ** pos, 0.0)
    frow = np.arange(T)
    fl = frow % L
    s = (fl != 0).astype(np.float32)
    hys = np.stack([tok(s), tok(1.0 - s)], axis=1).astype(np.float32)
    same = (seg[:, None] == (frow // L)[None, :])
    ang = 2.0 * np.pi * ((pos[:, None].astype(np.int64) * fl[None, :].astype(np.int64)) % N) / N
    RE = np.where(same, np.cos(ang), 0.0)
    IM = np.where(same, -np.sin(ang), 0.0)
    nyq = np.where(same, ((-1.0) ** pos)[:, None] * np.ones((1, T)), 0.0)
    IM = np.where((fl == 0)[None, :], nyq, IM)
    F = np.concatenate([RE, IM], axis=1).astype(np.float32)
    ffwd = F.reshape(NT, 128, 32, 128).transpose(2, 1, 0, 3)
    GRE = np.where(same.T, np.where((fl == 0)[:, None], 1.0 / N, (2.0 / N) * np.cos(ang.T)), 0.0)
    GIM = np.where(same.T, np.where((fl == 0)[:, None], (1.0 / N) * ((-1.0) ** pos)[None, :], -(2.0 / N) * np.sin(ang.T)), 0.0)
    G = np.concatenate([GRE, GIM], axis=0).astype(np.float32)
    finv = G.reshape(32, 128, 4, 512).transpose(2, 1, 0, 3)
    out = {"hyz": np.ascontiguousarray(z.T.astype(np.float32)), "hyd": deltas[None, :].astype(np.float32), "hyt": hyt,
           "hys": hys, "hyg": tok(sgn).astype(ml_dtypes.bfloat16),
           "ffwd": np.ascontiguousarray(ffwd).astype(ml_dtypes.bfloat16),
           "finv": np.ascontiguousarray(finv).astype(ml_dtypes.bfloat16)}
    _HY[prompt] = out
    return out


_CACHE = {}


def make_in_maps(inputs, depth=DEPTH):
    cst_np, colc_np = _consts()
    f = lambda a: np.ascontiguousarray(np.asarray(a, dtype=np.float32))
    shared = {n: f(inputs[n])[:depth] if n not in ("final_norm_w",) else f(inputs[n])
              for n in ["norm1_w", "norm2_w", "w_mod", "b_mod", "w_in", "w_out", "ret_log_decay", "ret_norm_w",
                        "hy_conv_w", "hy_conv_b", "hy_freq", "hy_w1", "hy_b1", "hy_w2", "hy_b2", "hy_w3", "hy_bias", "hy_norm_w",
                        "gdn_conv_w", "gdn_A_log", "gdn_dt_bias", "gdn_norm_w",
                        "ssd_conv_w", "ssd_conv_b", "ssd_A_log", "ssd_dt_bias", "ssd_D", "ssd_norm_w",
                        "mlp_w1", "mlp_w2", "final_norm_w"]}
    shared["cst"] = cst_np
    shared["cst2"] = _consts2()
    shared["colc"] = colc_np
    xp = f(inputs["x_prompt"]).reshape(4, 8 * 256, D)
    xs = f(inputs["x_sample"])
    c = f(inputs["c"])
    cctx = f(inputs["c_ctx"])
    maps = []
    for core in range(8):
        m = dict(shared)
        prompt = core >= 4
        if prompt:
            m["x"] = xp[core - 4]
            cond = cctx
            m["sret"] = np.zeros((depth, 2, 4, 64, 64), np.float32)
            m["sssd"] = np.zeros((depth, 2, 4, 128, 64), np.float32)
            m["sgdn"] = np.zeros((depth, 2, 4, 64, 64), np.float32)
        else:
            m["x"] = xs[core]
            cond = c[core]
            m["sret"] = f(inputs["state_ret"])[core][:depth]
            m["sssd"] = f(inputs["state_ssd"])[core][:depth]
            m["sgdn"] = f(inputs["state_gdn"])[core][:depth]
        m["cond"] = np.ascontiguousarray(cond.reshape(8, 128).T)
        m["mflag"] = np.array([[0.0, 1.0]] if prompt else [[1.0, 0.0]], np.float32)
        m["rope"] = _rope_tables(prompt)
        m.update(_hyena_tables(prompt))
        maps.append(m)
    return maps


def kernel(**inputs):
    if "nc" not in _CACHE:
        _CACHE["nc"] = build()
    k, _ = _CACHE["nc"]
    maps = make_in_maps(inputs)
    res = run_bass_kernel_spmd(k.nc, maps, core_ids=list(range(8)))
    r = res.results
    y_sample = np.stack([r[i]["y"] for i in range(4)], axis=0)
    y_prompt = np.concatenate([r[i]["y"].reshape(8, 256, D) for i in range(4, 8)], axis=0)
    ns_ret = np.concatenate([r[i]["nsret"] for i in range(4, 8)], axis=0)
    ns_gdn = np.concatenate([r[i]["nsgdn"] for i in range(4, 8)], axis=0)
    ns_ssd = np.concatenate([r[i]["nsssd"] for i in range(4, 8)], axis=0)
    return (y_prompt, y_sample, ns_ret, ns_gdn, ns_ssd)
```
